# Optimizing a Trainium2 kernel written in Bass

```python
import math
import jax, jax.numpy as jnp
from jax import lax
import numpy as np

D_MODEL = 1024
BATCH = 8
SEQ = 2048
DEPTH = 4

N_MIXERS = 3
BLOCK = 128
EPS = 1e-6
NEG = -1e30

SB_HEADS = 16
SB_HEAD_DIM = 64

DIL_GROUPS = ((128, 1), (512, 4), (2048, 16))
DIL_HEADS = 8
DIL_HEAD_DIM = 64
N_BUCKETS = 32
BUCKET_MAX_DIST = 2048

MLA_HEADS = 16
MLA_Q_RANK = 384
MLA_KV_RANK = 256
MLA_NOPE = 64
MLA_ROPE = 32
MLA_V = 64
ROPE_THETA = 10000.0

D_FF = 2816
CONV_W = 3

kernel_name = 'hybrid_sb_dilated_mla_convffn'


def rms_norm(x, g):
    xf = x.astype(jnp.float32)
    y = xf * lax.rsqrt(jnp.mean(xf * xf, axis=-1, keepdims=True) + EPS)
    return (y * g.astype(jnp.float32)).astype(x.dtype)


def t5_bucket(dist):
    dist = jnp.maximum(dist, 0)
    max_exact = N_BUCKETS // 2
    d = jnp.maximum(dist.astype(jnp.float32), 1.0)
    large = max_exact + (jnp.log(d / max_exact) / math.log(BUCKET_MAX_DIST / max_exact)
                         * (N_BUCKETS - max_exact)).astype(jnp.int32)
    large = jnp.minimum(large, N_BUCKETS - 1)
    return jnp.where(dist < max_exact, dist, large)


def stick_breaking_attention(xn, w_qkv, w_o):
    B, S, _ = xn.shape
    H, dh = SB_HEADS, SB_HEAD_DIM
    qkv = (xn @ w_qkv).reshape(B, S, 3, H, dh)
    q, k, v = qkv[:, :, 0], qkv[:, :, 1], qkv[:, :, 2]
    nb = S // BLOCK
    qb = q.reshape(B, nb, BLOCK, H, dh).transpose(1, 0, 3, 2, 4)
    kt = k.transpose(0, 2, 1, 3)
    vt = v.transpose(0, 2, 1, 3)
    key_pos = jnp.arange(S)
    scale = dh ** -0.5

    def one_block(args):
        q_blk, n = args
        z = jnp.einsum('bhqd,bhkd->bhqk', q_blk, kt).astype(jnp.float32) * scale
        q_pos = n * BLOCK + jnp.arange(BLOCK)
        mask = key_pos[None, :] < q_pos[:, None]
        log_not = jnp.where(mask, jax.nn.log_sigmoid(-z), 0.0)
        between = lax.cumsum(log_not, axis=3, reverse=True) - log_not
        a = jnp.where(mask, jnp.exp(jax.nn.log_sigmoid(z) + between), 0.0)
        return jnp.einsum('bhqk,bhkd->bhqd', a.astype(vt.dtype), vt)

    o = lax.map(one_block, (qb, jnp.arange(nb)))
    o = o.transpose(1, 0, 3, 2, 4).reshape(B, S, H * dh)
    return o @ w_o


def strided_window_attention(q, k, v, dilation, n_back, bias_g):
    B, S, Hg, dh = q.shape
    r, W = dilation, n_back
    L = S // r

    def to_sub(a):
        return a.reshape(B, L, r, Hg, dh).transpose(0, 2, 3, 1, 4)

    qs, ks, vs = to_sub(q), to_sub(k), to_sub(v)
    Lp = -(-L // W) * W
    nb = Lp // W
    pad = Lp - L
    qs = jnp.pad(qs, ((0, 0), (0, 0), (0, 0), (0, pad), (0, 0)))
    kp = jnp.pad(ks, ((0, 0), (0, 0), (0, 0), (W, pad), (0, 0)))
    vp = jnp.pad(vs, ((0, 0), (0, 0), (0, 0), (W, pad), (0, 0)))
    q_blk = qs.reshape(B, r, Hg, nb, W, dh)
    k_blk = jnp.concatenate([kp[:, :, :, :Lp].reshape(B, r, Hg, nb, W, dh),
                             kp[:, :, :, W:].reshape(B, r, Hg, nb, W, dh)], axis=4)
    v_blk = jnp.concatenate([vp[:, :, :, :Lp].reshape(B, r, Hg, nb, W, dh),
                             vp[:, :, :, W:].reshape(B, r, Hg, nb, W, dh)], axis=4)
    i = jnp.arange(W)[:, None]
    j = jnp.arange(2 * W)[None, :]
    m = i + W - j
    key_abs = jnp.arange(nb)[:, None, None] * W - W + j[None]
    mask = (m >= 0)[None] & (m <= n_back)[None] & (key_abs >= 0)
    bias = bias_g[t5_bucket(m * r)].astype(jnp.float32).transpose(2, 0, 1)
    s = jnp.einsum('brhnqd,brhnkd->brhnqk', q_blk, k_blk).astype(jnp.float32) * (dh ** -0.5)
    s = jnp.where(mask[None, None, None], s + bias[None, None, :, None], NEG)
    lse = jax.nn.logsumexp(s, axis=-1)
    p = jnp.exp(s - lse[..., None])
    o = jnp.einsum('brhnqk,brhnkd->brhnqd', p.astype(v_blk.dtype), v_blk)
    o = o.reshape(B, r, Hg, Lp, dh)[:, :, :, :L].transpose(0, 3, 1, 2, 4).reshape(B, S, Hg, dh)
    lse = lse.reshape(B, r, Hg, Lp)[:, :, :, :L].transpose(0, 3, 1, 2).reshape(B, S, Hg)
    return o, lse


def dilated_attention(xn, w_qkv, w_o, rel_bias):
    B, S, _ = xn.shape
    G, Hg, dh = len(DIL_GROUPS), DIL_HEADS, DIL_HEAD_DIM
    qkv = (xn @ w_qkv).reshape(B, S, 3, G, Hg, dh)
    outs, lses = [], []
    for g, (window, dilation) in enumerate(DIL_GROUPS):
        o, lse = strided_window_attention(qkv[:, :, 0, g], qkv[:, :, 1, g], qkv[:, :, 2, g],
                                          dilation, window // dilation,
                                          rel_bias[:, g * Hg:(g + 1) * Hg])
        outs.append(o)
        lses.append(lse)
    outs = jnp.stack(outs)
    wts = jax.nn.softmax(jnp.stack(lses), axis=0)
    o = jnp.einsum('gbshd,gbsh->bshd', outs, wts.astype(outs.dtype)).reshape(B, S, Hg * dh)
    return o @ w_o


def apply_rope(x, positions):
    half = x.shape[-1] // 2
    freqs = ROPE_THETA ** (-jnp.arange(half, dtype=jnp.float32) / half)
    ang = positions.astype(jnp.float32)[:, :, None, None] * freqs
    cos, sin = jnp.cos(ang).astype(x.dtype), jnp.sin(ang).astype(x.dtype)
    x1, x2 = x[..., :half], x[..., half:]
    return jnp.concatenate([x1 * cos - x2 * sin, x2 * cos + x1 * sin], axis=-1)


def causal_softmax_blocks(q, k, v, scale):
    B, S, H, dq = q.shape
    nb = S // BLOCK
    qb = q.reshape(B, nb, BLOCK, H, dq).transpose(1, 0, 3, 2, 4)
    kt = k.transpose(0, 2, 1, 3)
    vt = v.transpose(0, 2, 1, 3)
    key_pos = jnp.arange(S)

    def one_block(args):
        q_blk, n = args
        s = jnp.einsum('bhqd,bhkd->bhqk', q_blk, kt).astype(jnp.float32) * scale
        q_pos = n * BLOCK + jnp.arange(BLOCK)
        s = jnp.where(key_pos[None, :] <= q_pos[:, None], s, NEG)
        p = jax.nn.softmax(s, axis=-1)
        return jnp.einsum('bhqk,bhkd->bhqd', p.astype(vt.dtype), vt)

    o = lax.map(one_block, (qb, jnp.arange(nb)))
    return o.transpose(1, 0, 3, 2, 4).reshape(B, S, H, v.shape[-1])


def latent_attention(xn, positions, w_in, q_norm, w_qb, kv_norm, w_kvb, w_o):
    B, S, _ = xn.shape
    H = MLA_HEADS
    h = xn @ w_in
    c_q = h[..., :MLA_Q_RANK]
    c_kv = h[..., MLA_Q_RANK:MLA_Q_RANK + MLA_KV_RANK]
    k_rope = h[..., MLA_Q_RANK + MLA_KV_RANK:][:, :, None, :]
    q = (rms_norm(c_q, q_norm) @ w_qb).reshape(B, S, H, MLA_NOPE + MLA_ROPE)
    kv = (rms_norm(c_kv, kv_norm) @ w_kvb).reshape(B, S, H, MLA_NOPE + MLA_V)
    q_rope = apply_rope(q[..., MLA_NOPE:], positions)
    k_rope = apply_rope(k_rope, positions)
    q_full = jnp.concatenate([q[..., :MLA_NOPE], q_rope], axis=-1)
    k_full = jnp.concatenate([kv[..., :MLA_NOPE],
                              jnp.broadcast_to(k_rope, (B, S, H, MLA_ROPE))], axis=-1)
    o = causal_softmax_blocks(q_full, k_full, kv[..., MLA_NOPE:], (MLA_NOPE + MLA_ROPE) ** -0.5)
    return o.reshape(B, S, H * MLA_V) @ w_o


def conv_ffn(xn, w_up, conv_w, conv_b, w_down):
    h = xn @ w_up
    C = h.shape[-1]
    h = lax.conv_general_dilated(h, conv_w[:, None, :], window_strides=(1,),
                                 padding=[(CONV_W - 1, 0)],
                                 dimension_numbers=('NWC', 'WIO', 'NWC'),
                                 feature_group_count=C) + conv_b
    g, u = h[..., :D_FF], h[..., D_FF:]
    return (jax.nn.silu(g) * u) @ w_down


def setup_inputs(seed: int = 0) -> dict:
    key = jax.random.key(seed)
    ks = iter(jax.random.split(key, 32))
    n_a = len(range(0, DEPTH, N_MIXERS))
    n_b = len(range(1, DEPTH, N_MIXERS))
    n_c = len(range(2, DEPTH, N_MIXERS))
    G = len(DIL_GROUPS)

    def w(shape, fan_in):
        return jax.random.normal(next(ks), shape, jnp.float32) * fan_in ** -0.5

    def gain(shape):
        return 1.0 + 0.05 * jax.random.normal(next(ks), shape, jnp.float32)

    return {
        'x': jax.random.normal(next(ks), (BATCH, SEQ, D_MODEL), jnp.float32),
        'positions': jnp.broadcast_to(jnp.arange(SEQ, dtype=jnp.int32), (BATCH, SEQ)),
        'rel_bias': 0.5 * jax.random.normal(next(ks), (N_BUCKETS, G * DIL_HEADS), jnp.float32),
        'norm_gains': gain((DEPTH, 4, D_MODEL)),
        'sb_w_qkv': w((n_a, D_MODEL, 3 * SB_HEADS * SB_HEAD_DIM), D_MODEL),
        'sb_w_o': w((n_a, SB_HEADS * SB_HEAD_DIM, D_MODEL), SB_HEADS * SB_HEAD_DIM),
        'dil_w_qkv': w((n_b, D_MODEL, 3 * G * DIL_HEADS * DIL_HEAD_DIM), D_MODEL),
        'dil_w_o': w((n_b, DIL_HEADS * DIL_HEAD_DIM, D_MODEL), DIL_HEADS * DIL_HEAD_DIM),
        'mla_w_in': w((n_c, D_MODEL, MLA_Q_RANK + MLA_KV_RANK + MLA_ROPE), D_MODEL),
        'mla_q_norm': gain((n_c, MLA_Q_RANK)),
        'mla_w_qb': w((n_c, MLA_Q_RANK, MLA_HEADS * (MLA_NOPE + MLA_ROPE)), MLA_Q_RANK),
        'mla_kv_norm': gain((n_c, MLA_KV_RANK)),
        'mla_w_kvb': w((n_c, MLA_KV_RANK, MLA_HEADS * (MLA_NOPE + MLA_V)), MLA_KV_RANK),
        'mla_w_o': w((n_c, MLA_HEADS * MLA_V, D_MODEL), MLA_HEADS * MLA_V),
        'ffn_w_up': w((DEPTH, D_MODEL, 2 * D_FF), D_MODEL),
        'ffn_conv_w': w((DEPTH, CONV_W, 2 * D_FF), CONV_W),
        'ffn_conv_b': 0.02 * jax.random.normal(next(ks), (DEPTH, 2 * D_FF), jnp.float32),
        'ffn_w_down': w((DEPTH, D_FF, D_MODEL), D_FF),
    }


def reference(x, positions, rel_bias, norm_gains, sb_w_qkv, sb_w_o, dil_w_qkv, dil_w_o,
              mla_w_in, mla_q_norm, mla_w_qb, mla_kv_norm, mla_w_kvb, mla_w_o,
              ffn_w_up, ffn_conv_w, ffn_conv_b, ffn_w_down):
    for i in range(DEPTH):
        kind, j = i % N_MIXERS, i // N_MIXERS
        hn = rms_norm(x, norm_gains[i, 0])
        if kind == 0:
            m = stick_breaking_attention(hn, sb_w_qkv[j], sb_w_o[j])
        elif kind == 1:
            m = dilated_attention(hn, dil_w_qkv[j], dil_w_o[j], rel_bias)
        else:
            m = latent_attention(hn, positions, mla_w_in[j], mla_q_norm[j], mla_w_qb[j],
                                 mla_kv_norm[j], mla_w_kvb[j], mla_w_o[j])
        x = x + rms_norm(m, norm_gains[i, 1])
        f = conv_ffn(rms_norm(x, norm_gains[i, 2]), ffn_w_up[i], ffn_conv_w[i],
                     ffn_conv_b[i], ffn_w_down[i])
        x = x + rms_norm(f, norm_gains[i, 3])
    return x
```

```python
import math
from collections import deque
from contextlib import ExitStack

import numpy as np
import concourse.bass as bass
import concourse.mybir as mybir
from concourse.bass_utils import run_bass_kernel_spmd

F32 = mybir.dt.float32
BF16 = mybir.dt.bfloat16
I32 = mybir.dt.int32
AF = mybir.ActivationFunctionType
ALU = mybir.AluOpType

S = 2048
D = 1024
DFF = 2816
NPAIR = 22
EPS = 1e-6
ENGS = ("pe", "act", "dve", "pool", "sp")
import os as _os
SAME_ENGINE_SYNC = _os.environ.get("NOSES", "") == ""


class Res:
    __slots__ = ("name", "w", "rs", "excl")

    def __init__(self, name=""):
        self.name = name
        self.w = None
        self.rs = {}
        self.excl = False


class Buf:
    def __init__(self, t, nres=1, name=""):
        self.t = t
        self.r = [Res("%s.%d" % (name, i)) for i in range(nres)]

    def __getitem__(self, idx):
        return self.t[idx]


class Prog:
    def __init__(self, nc, es, same_engine_sync=SAME_ENGINE_SYNC):
        self.nc = nc
        self.es = es
        self.ops = {e: [] for e in ENGS}
        self.ecnt = {e: 0 for e in ENGS}
        self.esem = {}
        for e in ("pe", "act", "dve", "pool"):
            self.esem[e] = es.enter_context(nc.semaphore("es_" + e))
        self.known = {e: {} for e in ENGS}
        self.dcnt = {}
        self.same_engine_sync = same_engine_sync
        self.n_wait = 0

    def sbuf(self, name, shape, dtype, nres=1):
        t = self.es.enter_context(self.nc.sbuf_tensor(name, list(shape), dtype))
        return Buf(t, nres, name)

    def psum(self, name, shape, dtype, nres=1):
        t = self.es.enter_context(self.nc.psum_tensor(name, list(shape), dtype))
        b = Buf(t, nres, name)
        for r in b.r:
            r.excl = True
        return b

    def dsem(self, name):
        s = self.es.enter_context(self.nc.semaphore(name))
        self.dcnt[s] = 0
        return s

    def op(self, eng, fn, reads=(), writes=(), inc=True, dsem=None):
        waits = {}
        kn = self.known[eng]
        own = self.esem.get(eng)
        if any(r.excl for r in reads):
            writes = list(writes) + [r for r in reads if r.excl and r not in writes]
            reads = [r for r in reads if not r.excl]

        def need(m):
            if m is None:
                return
            sem, val = m
            if sem in self.dcnt:
                val = self.dcnt[sem]
            elif sem is own:
                if eng == "pe" or not self.same_engine_sync:
                    return
            if kn.get(sem, 0) >= val:
                return
            if waits.get(sem, 0) < val:
                waits[sem] = val

        for r in reads:
            need(r.w)
        for w in writes:
            need(w.w)
            for m in w.rs.items():
                need(m)
        for sem, val in waits.items():
            kn[sem] = val
        if dsem is not None:
            self.dcnt[dsem] += 16
            marker = (dsem, self.dcnt[dsem])
            incspec = (dsem, 16)
        elif eng == "sp":
            marker = None
            incspec = None
        elif inc:
            self.ecnt[eng] += 1
            marker = (own, self.ecnt[eng])
            incspec = (own, 1)
        else:
            marker = (own, self.ecnt[eng] + 1)
            incspec = None
        if marker is not None:
            for r in reads:
                if r.rs.get(marker[0], 0) < marker[1]:
                    r.rs[marker[0]] = marker[1]
            for w in writes:
                w.w = marker
                w.rs = {}
        self.n_wait += len(waits)
        self.ops[eng].append((list(waits.items()), fn, incspec))

    def barrier(self):
        targets = [(self.esem[e], self.ecnt[e]) for e in self.esem if self.ecnt[e] > 0]
        targets += [(s, c) for s, c in self.dcnt.items() if c > 0]
        for eng in ENGS:
            waits = []
            for sem, val in targets:
                if sem is self.esem.get(eng):
                    continue
                if self.known[eng].get(sem, 0) >= val:
                    continue
                self.known[eng][sem] = val
                waits.append((sem, val))
            if waits:
                self.ops[eng].append((waits, None, None))

    def emit(self):
        nc = self.nc
        with nc.Block() as block:
            def run(engname):
                def body(e):
                    for waits, fn, incspec in self.ops[engname]:
                        for sem, val in waits:
                            e.wait_ge(sem, val)
                        if fn is None:
                            continue
                        ins = fn(e)
                        if incspec is not None:
                            ins.then_inc(incspec[0], incspec[1])
                return body

            block.tensor(run("pe"))
            block.scalar(run("act"))
            block.vector(run("dve"))
            block.gpsimd(run("pool"))
            block.sync(run("sp"))

    def dma(self, eng, out, in_, dsem, reads=(), writes=()):
        if eng == "pool":
            self.op(eng, lambda e: e.dma_start(out=out, in_=in_, max_dma_last_dim=4096), reads, writes, dsem=dsem)
        else:
            self.op(eng, lambda e: e.dma_start(out=out, in_=in_), reads, writes, dsem=dsem)

    def mm(self, out, lhsT, rhs, start, stop, reads, writes, inc=True, **kw):
        self.op("pe", lambda e: e.matmul(out, lhsT, rhs, start=start, stop=stop, **kw),
                reads, writes, inc=inc)

    def mm_group(self, out, pairs, reads, writes, **kw):
        n = len(pairs)
        for i, (l, r) in enumerate(pairs):
            self.mm(out, l, r, start=(i == 0), stop=(i == n - 1),
                    reads=reads, writes=writes, inc=(i == n - 1), **kw)

    def act(self, out, in_, func, reads, writes, **kw):
        self.op("act", lambda e: e.activation(out=out, in_=in_, func=func, **kw), reads, writes)

    def tt(self, eng, out, in0, in1, op, reads, writes):
        self.op(eng, lambda e: e.tensor_tensor(out=out, in0=in0, in1=in1, op=op), reads, writes)

    def ts(self, eng, out, in0, s1, s2, op0, op1, reads, writes):
        if s2 is None:
            self.op(eng, lambda e: e.tensor_scalar(out=out, in0=in0, scalar1=s1, scalar2=None, op0=op0),
                    reads, writes)
        else:
            self.op(eng, lambda e: e.tensor_scalar(out=out, in0=in0, scalar1=s1, scalar2=s2,
                                                   op0=op0, op1=op1), reads, writes)

    def stt(self, out, in0, scalar, in1, op0, op1, reads, writes):
        self.op("dve", lambda e: e.scalar_tensor_tensor(out=out, in0=in0, scalar=scalar, in1=in1,
                                                        op0=op0, op1=op1), reads, writes)

    def copy(self, eng, out, in_, reads, writes):
        if eng == "act":
            self.act(out, in_, AF.Copy, reads, writes)
        else:
            self.op(eng, lambda e: e.tensor_copy(out=out, in_=in_), reads, writes)

    def recip(self, out, in_, reads, writes):
        self.op("dve", lambda e: e.reciprocal(out=out, in_=in_), reads, writes)

    def memset(self, eng, out, val, writes):
        self.op(eng, lambda e: e.memset(out, val), (), writes)


def split_mm_group(P, out, pairs, reads, writes, chunk=2):
    units = []
    n = len(pairs)
    for c0 in range(0, n, chunk):
        def f(c0=c0):
            for i in range(c0, min(n, c0 + chunk)):
                P.mm(out, pairs[i][0], pairs[i][1], start=(i == 0), stop=(i == n - 1),
                     reads=reads, writes=writes, inc=(i == n - 1))
        units.append(f)
    return units


class Arena:
    def __init__(self, buf, nf32):
        self.buf = buf
        self.n = nf32
        self.off = 0

    def reset(self, to=0):
        self.off = to

    def alloc(self, name, shape, dtype, nres=1):
        n = 1
        for s_ in shape:
            n *= s_
        esz = 4 if dtype in (F32, I32) else 2
        nf = (n * esz + 3) // 4
        nf = (nf + 3) // 4 * 4
        assert self.off + nf <= self.n, "arena overflow %s: %d + %d > %d" % (name, self.off, nf, self.n)
        ap = self.buf.t[:, self.off:self.off + nf]
        self.off += nf
        if dtype != F32:
            ap = ap.bitcast(dtype)
        ap = ap[:, 0:n]
        if len(shape) == 2:
            ap = ap.rearrange("p (a b) -> p a b", b=shape[1])
        elif len(shape) == 3:
            ap = ap.rearrange("p (a b c) -> p a b c", b=shape[1], c=shape[2])
        return Buf(ap, nres, name)


CB_IDENT, CB_ONES, CB_TRI_INCL, CB_TRI_REST, CB_M_STRICT, CB_M_INCL, CB_NEG_STRICT, CB_NEG_INCL = range(8)
NEG_BIG = -30000.0
ARENA_F32 = 25216


class Ctx:
    pass


def cbs(C, i):
    return C.cb[:, 128 * i:128 * (i + 1)]


def emit_norm(C, gidx):
    P = C.P
    for tg in range(4):
        cols = slice(512 * tg, 512 * tg + 512)
        ss = C.PS[6 + tg % 2]
        for k in range(8):
            sq = C.sq[k % 2]
            P.act(sq[:, :], C.xT[:, k, cols], AF.Square, [C.xT.r[4 * k + tg]], sq.r)
            P.mm(ss[:, :], cbs(C, CB_ONES), sq[:, :], start=(k == 0), stop=(k == 7),
                 reads=sq.r + C.cb.r, writes=ss.r)
        rsb = C.rsb
        P.act(rsb[:, :], ss[:, :], AF.Sqrt, ss.r, rsb.r, bias=C.epsb[:, 0:1], scale=1.0 / D)
        rstd = C.rstd[tg % 2]
        P.recip(rstd[:, :], rsb[:, :], rsb.r, rstd.r)
        for k in range(8):
            P.stt(C.xnT[:, k, cols], C.xT[:, k, cols], C.gn[:, gidx * 8 + k:gidx * 8 + k + 1], rstd[:, :],
                  ALU.mult, ALU.mult, [C.xT.r[4 * k + tg]] + rstd.r + C.gn.r, [C.xnT.r[4 * k + tg]])


class OutProj:
    def __init__(self, C, bufs, wsrc, nk, rhs_fn, tgs, gidx):
        self.C, self.bufs, self.wsrc, self.nk, self.rhs_fn, self.tgs, self.gidx = C, bufs, wsrc, nk, rhs_fn, tgs, gidx
        self.loaded = set()

    def load(self, dc):
        if dc in self.loaded or dc >= 8:
            return
        self.loaded.add(dc)
        C = self.C
        w = self.bufs[0][dc % 2]
        C.P.dma("pool", w[:, :], self.wsrc(dc), C.wsem[dc % 2], writes=w.r)

    def run_mm(self, fofs=0, ssb=2, bg=None):
        C, nk, rhs_fn, tgs, gidx = self.C, self.nk, self.rhs_fn, self.tgs, self.gidx
        P = C.P
        wbuf, fT, tmpb = self.bufs
        self.fofs = fofs
        self.ssb = ssb
        self.load(0)
        for dc in range(8):
            w = wbuf[dc % 2]
            for ti, tg in enumerate(tgs):
                ps = C.PS[ti]
                pairs = []
                reads = list(w.r)
                for kc in range(nk):
                    ap, rr = rhs_fn(kc, ti)
                    pairs.append((w[:, kc * 128:(kc + 1) * 128], ap))
                    reads += rr
                P.mm_group(ps[:, :], pairs, reads, ps.r)
                if ti == 0:
                    self.load(dc + 1)
                fo = fofs + ti * 512
                P.copy("dve", fT[:, dc, fo:fo + 512], ps[:, :], ps.r, [fT.r[(fofs // 512 + ti) * 8 + dc]])
                sq = C.sq[ti]
                P.act(sq[:, :], ps[:, :], AF.Square, ps.r, sq.r)
                ss = C.PS[ssb + ti]
                P.mm(ss[:, :], cbs(C, CB_ONES), sq[:, :], start=(dc == 0), stop=(dc == 7),
                     reads=sq.r + C.cb.r, writes=ss.r)
                if bg:
                    bg.popleft()()
                    if dc >= 1 and bg:
                        bg.popleft()()

    def final_units(self, rstd_bufs=None):
        C, tgs, gidx = self.C, self.tgs, self.gidx
        P = C.P
        wbuf, fT, tmpb = self.bufs
        fofs, ssb = self.fofs, self.ssb
        rstds = rstd_bufs or C.rstd
        units = []

        def rs(ti):
            def f():
                ss = C.PS[ssb + ti]
                rsb = C.rsb2[ti]
                P.act(rsb[:, :], ss[:, :], AF.Sqrt, ss.r, rsb.r, bias=C.epsb[:, 0:1], scale=1.0 / D)
                P.recip(rstds[ti][:, :], rsb[:, :], rsb.r, rstds[ti].r)
            return f

        def fin(dc, ti, tg):
            def f():
                cols = slice(512 * tg, 512 * tg + 512)
                rstd = rstds[ti]
                tmp = tmpb[ti]
                fo = fofs + ti * 512
                fr = [fT.r[(fofs // 512 + ti) * 8 + dc]]
                P.tt("pool", tmp[:, :], fT[:, dc, fo:fo + 512], rstd[:, :], ALU.mult, fr + rstd.r, tmp.r)
                P.stt(C.xT[:, dc, cols], tmp[:, :], C.gn[:, gidx * 8 + dc:gidx * 8 + dc + 1], C.xT[:, dc, cols],
                      ALU.mult, ALU.add, [C.xT.r[4 * dc + tg]] + tmp.r + C.gn.r, [C.xT.r[4 * dc + tg]])
            return f

        for ti, tg in enumerate(tgs):
            units.append(rs(ti))
        for dc in range(8):
            for ti, tg in enumerate(tgs):
                units.append(fin(dc, ti, tg))
        return units

    def run(self):
        self.run_mm()
        for u in self.final_units():
            u()


def emit_attn_out_proj(C, A, wsrc, nk, oT, gidx):
    wbuf = [A.alloc("wo%d" % i, [nk * 128], BF16) for i in range(2)]
    fT = A.alloc("fT", [8, 2048], BF16, nres=32)
    tmpb = [A.alloc("tmp%d" % i, [512], F32) for i in range(2)]
    rstd2 = [A.alloc("rstdb%d" % i, [512], F32) for i in range(2)]
    ops = []
    for half in range(2):
        def rhs_fn(kc, ti, half=half):
            tg = 2 * half + ti
            return oT[:, kc, 512 * tg:512 * tg + 512], [oT.r[4 * kc + tg]]
        ops.append(OutProj(C, (wbuf, fT, tmpb), wsrc, nk, rhs_fn, [2 * half, 2 * half + 1], gidx))
    ops[0].run_mm(fofs=0, ssb=2)
    f0 = deque(ops[0].final_units())
    ops[1].run_mm(fofs=1024, ssb=6, bg=f0)
    while f0:
        f0.popleft()()
    for u in ops[1].final_units(rstd_bufs=rstd2):
        u()


def emit_out_proj(C, bufs, wsrc, nk, rhs_fn, tgs, gidx):
    OutProj(C, bufs, wsrc, nk, rhs_fn, tgs, gidx).run()


def alloc_outproj_bufs(C, nk, tmpb=None):
    A = C.A
    wbuf = [A.alloc("wo%d" % i, [nk * 128], BF16) for i in range(2)]
    fT = A.alloc("fT", [8, 1024], BF16, nres=16)
    if tmpb is None:
        tmpb = [A.alloc("tmp%d" % i, [512], F32) for i in range(2)]
    return wbuf, fT, tmpb


def emit_ffn(C, l):
    P = C.P
    A = C.A
    A.reset()
    mT = A.alloc("mT", [NPAIR, 1024], BF16, nres=NPAIR * 2)
    wup = [A.alloc("wup%d" % i, [8, 256], BF16) for i in range(2)]
    hs = [[A.alloc("hs%d%d" % (i, j), [514], F32) for j in range(2)] for i in range(2)]
    acc = [[A.alloc("acc%d%d" % (i, j), [512], F32) for j in range(2)] for i in range(2)]
    tails = A.alloc("tails", [44, 2], F32, nres=44)
    tmpx = A.alloc("tmpx", [512], F32)
    opb = alloc_outproj_bufs(C, NPAIR, tmpb=[tmpx, tmpx])
    nload = [0]

    def load_w(seq):
        if seq >= 2 * NPAIR or seq < nload[0]:
            return
        nload[0] = seq + 1
        i = seq % NPAIR
        w = wup[seq % 2]
        P.dma("pool", w[:, :, :], C.d_wup[l, i].rearrange("p (a b) -> p a b", b=256), C.usem[seq % 2], writes=w.r)

    def finish(u):
        half, i, t2, par = u
        ag, au = acc[par]
        P.act(ag[:, :], ag[:, :], AF.Silu, ag.r, ag.r)
        P.tt("pool", mT[:, i, 512 * t2:512 * t2 + 512], ag[:, :], au[:, :], ALU.mult,
             ag.r + au.r, [mT.r[2 * i + t2]])

    load_w(0)
    emit_norm(C, 4 * l + 2)
    cnt = 0
    pending = deque()
    for half in range(2):
        def rhs_fn(kc, ti):
            return mT[:, kc, 512 * ti:512 * ti + 512], [mT.r[2 * kc + ti]]
        op = OutProj(C, opb, lambda dc: C.d_wdn[l, dc], NPAIR, rhs_fn, [2 * half, 2 * half + 1], 4 * l + 3)
        prev = None
        for i in range(NPAIR):
            seq = half * NPAIR + i
            w = wup[seq % 2]
            for t2 in range(2):
                tg = 2 * half + t2
                cols = slice(512 * tg, 512 * tg + 512)
                par = cnt % 2
                pss = []
                for gu in range(2):
                    ps = C.PS[(2 * cnt + gu) % 8]
                    pairs = [(w[:, k, 128 * gu:128 * gu + 128], C.xnT[:, k, cols]) for k in range(8)]
                    P.mm_group(ps[:, :], pairs, w.r + [C.xnT.r[4 * k + tg] for k in range(8)], ps.r)
                    pss.append(ps)
                if t2 == 0:
                    load_w(seq + 1)
                    if i == NPAIR - 2:
                        op.load(0)
                cnt += 1
                for gu in range(2):
                    ci = i + NPAIR * gu
                    pbase = (l * 44 + ci) * 4
                    h = hs[par][gu]
                    a = acc[par][gu]
                    ps = pss[gu]
                    P.copy("act", h[:, 2:514], ps[:, :], ps.r, h.r)
                    P.act(a[:, :], ps[:, :], AF.Identity, ps.r + C.cvp.r, a.r,
                          bias=C.cvp[:, pbase + 3:pbase + 4], scale=C.cvp[:, pbase + 2:pbase + 3])
                for gu in range(2):
                    ci = i + NPAIR * gu
                    h = hs[par][gu]
                    if tg == 0:
                        P.memset("pool", h[:, 0:2], 0.0, h.r)
                    else:
                        P.copy("pool", h[:, 0:2], tails[:, ci, :], [tails.r[ci]], h.r)
                if tg < 3:
                    for gu in range(2):
                        ci = i + NPAIR * gu
                        h = hs[par][gu]
                        P.copy("pool", tails[:, ci, :], h[:, 512:514], h.r, [tails.r[ci]])
                if prev is not None:
                    finish(prev)
                if pending:
                    pending.popleft()()
                for tap in (1, 0):
                    for gu in range(2):
                        ci = i + NPAIR * gu
                        pbase = (l * 44 + ci) * 4
                        h = hs[par][gu]
                        a = acc[par][gu]
                        P.stt(a[:, :], h[:, tap:tap + 512], C.cvp[:, pbase + tap:pbase + tap + 1], a[:, :],
                              ALU.mult, ALU.add, h.r + a.r + C.cvp.r, a.r)
                prev = (half, i, t2, par)
        finish(prev)
        op.run_mm()
        pending = deque(op.final_units())
        pending.popleft()()
        pending.popleft()()
        if half == 1:
            while pending:
                pending.popleft()()
    P.barrier()


class Chain:
    pass


def emit_transposes(C, o_pair, oT, p):
    P = C.P
    for tb in range(4):
        ps = C.PS[6 + tb % 2]
        psb = ps.t[:, :].bitcast(BF16)
        for j in range(4):
            tile = 4 * tb + j
            P.op("pe", (lambda o, i: (lambda e: e.transpose(o, i, cbs(C, CB_IDENT))))(
                psb[:, j * 128:(j + 1) * 128], o_pair[:, tile, :]),
                 [o_pair.r[tile // 4]] + C.cb.r, ps.r)
        P.copy("dve", oT[:, p, 512 * tb:512 * tb + 512], psb[:, 0:512], ps.r, [oT.r[4 * p + tb]])


def sb_steps():
    steps = []
    for QG in range(4):
        nk = 4 * QG + 4
        for jb in range(nk - 1, -1, -1):
            r = jb - 4 * QG
            ta = 4 * QG + max(0, r)
            steps.append(dict(QG=QG, jb=jb, ta=ta, tb=4 * QG + 3, diag=(jb if r >= 0 else None),
                              first=(jb == nk - 1), last=(jb == 0)))
    return steps


def emit_sb_attention(C, chains, bg):
    P = C.P
    steps = sb_steps()
    n = len(steps)

    def front_pe(k):
        st = steps[k]
        q0 = (st["ta"] - 4 * st["QG"]) * 128
        for ch in chains:
            kap, kr = ch.kT(st["jb"])
            qap, qr = ch.qT(st["ta"] * 128, 512 - q0)
            if st["diag"] is None:
                P.mm(ch.Zb[:, q0:512], kap, qap, True, True, kr + qr, ch.Zb.r)
            else:
                P.mm(ch.Zb[:, q0:512], kap, qap, True, False, kr + qr, ch.Zb.r, inc=False)
                P.mm(ch.Zb[:, q0:q0 + 128], cbs(C, CB_IDENT), cbs(C, CB_NEG_STRICT), False, True,
                     C.cb.r, ch.Zb.r, skip_group_check=True)

    def front_rest(k):
        st = steps[k]
        q0 = (st["ta"] - 4 * st["QG"]) * 128
        b = k % 2
        for ch in chains:
            E = ch.E[b]
            P.act(E[:, q0:512], ch.Zb[:, q0:512], AF.Exp, ch.Zb.r, E.r)
        for ch in chains:
            E = ch.E[b]
            Lp = ch.Lp[b]
            P.act(Lp[:, q0:512], E[:, q0:512], AF.Ln, E.r, Lp.r, bias=1.0, scale=1.0)

    front_pe(0)
    front_rest(0)
    for k in range(n):
        st = steps[k]
        q0 = (st["ta"] - 4 * st["QG"]) * 128
        b = k % 2
        for ch in chains:
            Lp = ch.Lp[b]
            P.mm(ch.Ab[:, q0:512], cbs(C, CB_TRI_INCL), Lp[:, q0:512], st["first"], False,
                 Lp.r + C.cb.r, ch.Ab.r, skip_group_check=True)
        if k + 1 < n:
            front_pe(k + 1)
        for ch in chains:
            X = ch.X[b]
            P.act(X[:, q0:512], ch.Ab[:, q0:512], AF.Exp, ch.Ab.r, X.r)
        if k + 1 < n:
            front_rest(k + 1)
        if not st["last"]:
            for ch in chains:
                Lp = ch.Lp[b]
                P.mm(ch.Ab[:, q0:512], cbs(C, CB_TRI_REST), Lp[:, q0:512], False, False,
                     Lp.r + C.cb.r, ch.Ab.r, skip_group_check=True)
        for ch in chains:
            aT = ch.aT[b]
            P.tt("dve", aT[:, q0:512], ch.E[b][:, q0:512], ch.X[b][:, q0:512], ALU.mult,
                 ch.E[b].r + ch.X[b].r, aT.r)
        for ch in chains:
            aT = ch.aT[b]
            vap, vr = ch.V(st["jb"])
            tiles = list(range(st["ta"], st["tb"] + 1))
            for ti_, t in enumerate(tiles):
                i = t - 4 * st["QG"]
                P.mm(ch.Ob[:, i * 64:(i + 1) * 64], aT[:, i * 128:(i + 1) * 128], vap,
                     start=(st["first"] and ti_ == 0), stop=False, reads=aT.r + vr, writes=ch.Ob.r,
                     inc=(ti_ == len(tiles) - 1), skip_group_check=True)
        if st["last"]:
            for ci_, ch in enumerate(chains):
                for i in range(4):
                    t = 4 * st["QG"] + i
                    ap, rr = ch.o_dst(t)
                    P.copy("dve", ap, ch.Ob[:, i * 64:(i + 1) * 64], ch.Ob.r, rr)
        nb_ = -(-len(bg) // max(1, n - 1 - k)) if bg else 0
        for _ in range(nb_):
            if bg:
                bg.popleft()()
    while bg:
        bg.popleft()()


def emit_sb_layer(C, l, j):
    P = C.P
    A = C.A
    A.reset()
    oT = A.alloc("oT", [8, 2048], BF16, nres=32)
    mark = A.off
    wq = [A.alloc("wq%d" % i, [8, 384], BF16) for i in range(2)]
    qT = [A.alloc("qT%d" % i, [2048], BF16, nres=4) for i in range(2)]
    kT = [A.alloc("kT%d" % i, [2048], BF16, nres=4) for i in range(2)]
    Vv = [A.alloc("V%d" % i, [2048], BF16, nres=4) for i in range(2)]
    o_pair = [A.alloc("op%d" % i, [16, 128], BF16, nres=4) for i in range(2)]
    chains = []
    for c in range(2):
        ch = Chain()
        ch.E = [A.alloc("E%d%d" % (c, i), [512], F32) for i in range(2)]
        ch.X = [A.alloc("X%d" % c, [512], F32)] * 2
        ch.Lp = [A.alloc("Lp%d%d" % (c, i), [512], BF16) for i in range(2)]
        ch.aT = [A.alloc("aT%d%d" % (c, i), [512], BF16) for i in range(2)]
        ch.Zb = C.PS[c]
        ch.Ab = C.PS[2 + c]
        ch.Ob = C.PS[4 + c]
        chains.append(ch)

    def proj_units(p):
        s_ = p % 2
        w = wq[s_]
        units = []
        units.append(lambda: P.dma("pool", w[:, :, :], C.d_sbqkv[j, p].rearrange("p (a b) -> p a b", b=384),
                                   C.usem[s_], writes=w.r))
        ucnt = [0]

        def qk_units(which, tg):
            ps = C.PS[6 + ucnt[0] % 2]
            ucnt[0] += 1
            cols = slice(512 * tg, 512 * tg + 512)
            pairs = [(w[:, k, 128 * which:128 * which + 128], C.xnT[:, k, cols]) for k in range(8)]
            us = split_mm_group(P, ps[:, :], pairs, w.r + [C.xnT.r[4 * k + tg] for k in range(8)], ps.r, 2)
            if which == 0:
                us.append(lambda: P.ts("dve", qT[s_][:, cols], ps[:, :], 0.125, None, ALU.mult, None, ps.r,
                                       [qT[s_].r[tg]]))
            else:
                us.append(lambda: P.copy("dve", kT[s_][:, cols], ps[:, :], ps.r, [kT[s_].r[tg]]))
            return us

        def v_units(vb):
            ps = C.PS[6 + ucnt[0] % 2]
            ucnt[0] += 1
            us = []
            for jj in range(4):
                tile = 4 * vb + jj
                tcols = slice(128 * tile, 128 * tile + 128)
                pairs = [(C.xnT[:, k, tcols], w[:, k, 256:384]) for k in range(8)]
                us += split_mm_group(P, ps[:, jj * 128:(jj + 1) * 128], pairs,
                                     w.r + [C.xnT.r[4 * k + vb] for k in range(8)], ps.r, 4)
            us.append(lambda: P.copy("dve", Vv[s_][:, 512 * vb:512 * vb + 512], ps[:, :], ps.r, [Vv[s_].r[vb]]))
            return us

        for tg in range(4):
            units += qk_units(1, tg)
            units += v_units(tg)
            units += qk_units(0, tg)
        return units

    u0 = proj_units(0)
    u0[0]()
    emit_norm(C, 4 * l + 0)
    for u in u0[1:]:
        u()
    for p in range(8):
        s_ = p % 2
        for c, ch in enumerate(chains):
            rows = slice(64 * c, 64 * c + 64)
            ch.kT = (lambda rows, s_: (lambda jb: (kT[s_][rows, 128 * jb:128 * jb + 128], [kT[s_].r[jb // 4]])))(rows, s_)
            ch.qT = (lambda rows, s_: (lambda t0, n: (qT[s_][rows, t0:t0 + n], [qT[s_].r[t0 // 512]])))(rows, s_)
            ch.V = (lambda c, s_: (lambda jb: (Vv[s_][:, 128 * jb + 64 * c:128 * jb + 64 * c + 64],
                                               [Vv[s_].r[jb // 4]])))(c, s_)
            ch.o_dst = (lambda c, s_: (lambda t: (o_pair[s_][:, t, 64 * c:64 * c + 64], [o_pair[s_].r[t // 4]])))(c, s_)
        bg = deque(proj_units(p + 1)) if p < 7 else deque()
        emit_sb_attention(C, chains, bg)
        emit_transposes(C, o_pair[s_], oT, p)
    P.barrier()
    A.reset(mark)
    emit_attn_out_proj(C, A, lambda dc: C.d_sbwo[j, dc], 8, oT, 4 * l + 1)
    P.barrier()


def emit_softmax_attention(C, chains, bg, bg_every=3, look=1):
    P = C.P
    n = len(chains[0].steps)

    nbuf = look + 1

    def front(k):
        b = k % nbuf
        for ch in chains:
            st = ch.steps[k]
            q0 = (st["ta"] - 4 * st["QG"]) * 128
            N = (st["tb"] - st["ta"] + 1) * 128
            kap, kr = ch.kT(st["g"], st["jb"])
            qap, qr = ch.qT(st["g"], st["ta"] * 128, N)
            if st.get("negmask") is None:
                P.mm(ch.Zb[b][:, q0:q0 + N], kap, qap, True, True, kr + qr, ch.Zb[b].r)
            else:
                P.mm(ch.Zb[b][:, q0:q0 + N], kap, qap, True, False, kr + qr, ch.Zb[b].r, inc=False)
                P.mm(ch.Zb[b][:, q0:q0 + 128], cbs(C, CB_IDENT), st["negmask"], False, True,
                     C.cb.r, ch.Zb[b].r, skip_group_check=True)
        for ch in chains:
            st = ch.steps[k]
            q0 = (st["ta"] - 4 * st["QG"]) * 128
            N = (st["tb"] - st["ta"] + 1) * 128
            PT = ch.PT[b]
            P.act(PT[:, q0:q0 + N], ch.Zb[b][:, q0:q0 + N], AF.Exp, ch.Zb[b].r, PT.r)
        for ci_, ch in enumerate(chains):
            st = ch.steps[k]
            if st["mask"] is None:
                continue
            q0 = (st["ta"] - 4 * st["QG"]) * 128
            PT = ch.PT[b]
            map_, mr, mn = st["mask"]
            eng = "pool" if (ci_ + k) % 2 == 0 else "dve"
            P.tt(eng, PT[:, q0:q0 + mn], PT[:, q0:q0 + mn], map_, ALU.mult, PT.r + mr, PT.r)

    def back(k):
        b = k % nbuf
        for ch in chains:
            st = ch.steps[k]
            PT = ch.PT[b]
            vap, vr = ch.V(st["g"], st["jb"])
            tiles = list(range(st["ta"], st["tb"] + 1))
            for ti_, t in enumerate(tiles):
                i = t - 4 * st["QG"]
                P.mm(ch.Ob[:, i * 65:(i + 1) * 65], PT[:, i * 128:(i + 1) * 128], vap,
                     start=(st["first"] and ti_ == 0), stop=False, reads=PT.r + vr, writes=ch.Ob.r,
                     inc=(ti_ == len(tiles) - 1), skip_group_check=True)
        for ch in chains:
            st = ch.steps[k]
            if st["last"]:
                for i in range(4):
                    t = 4 * st["QG"] + i
                    rd = ch.rden
                    P.recip(rd[:, 0:1], ch.Ob[:, i * 65 + 64:i * 65 + 65], ch.Ob.r, rd.r)
                    ap, rr = ch.o_dst(t)
                    P.ts("dve", ap, ch.Ob[:, i * 65:i * 65 + 64], rd[:, 0:1], None, ALU.mult, None,
                         ch.Ob.r + rd.r, rr)

    for j_ in range(min(look, n)):
        front(j_)
    for k in range(n):
        if k + look < n:
            front(k + look)
        back(k)
        if bg and k % bg_every == bg_every - 1:
            bg.popleft()()
    while bg:
        bg.popleft()()


def emit_sin_table(C, out_ap, out_res, ang, tmp, tmpi, rows, turns_off):
    P = C.P
    TWO_PI = 2.0 * math.pi
    a = ang[rows, :]
    t = tmp[rows, :]
    ti = tmpi[rows, :]
    P.ts("dve", t, a, 1.0 / TWO_PI, turns_off + 0.5, ALU.mult, ALU.add, ang.r, tmp.r)
    P.copy("dve", ti, t, tmp.r, tmpi.r)
    P.copy("dve", t, ti, tmpi.r, tmp.r)
    P.stt(t, t, -TWO_PI, a, ALU.mult, ALU.add, tmp.r + ang.r, tmp.r)
    if turns_off != 0.0:
        P.ts("dve", t, t, TWO_PI * turns_off, None, ALU.add, None, tmp.r, tmp.r)
    for _ in range(2):
        P.ts("dve", ti.bitcast(F32), t, -math.pi, TWO_PI, ALU.is_lt, ALU.mult, tmp.r, tmpi.r)
        P.tt("dve", t, t, ti.bitcast(F32), ALU.add, tmp.r + tmpi.r, tmp.r)
        P.ts("dve", ti.bitcast(F32), t, math.pi, -TWO_PI, ALU.is_gt, ALU.mult, tmp.r, tmpi.r)
        P.tt("dve", t, t, ti.bitcast(F32), ALU.add, tmp.r + tmpi.r, tmp.r)
    P.ts("dve", t, t, math.pi, -math.pi, ALU.min, ALU.max, tmp.r, tmp.r)
    P.act(out_ap, t, AF.Sin, tmp.r, out_res)


def emit_mla_layer(C, l):
    P = C.P
    A = C.A
    SC = 96.0 ** -0.5
    A.reset()
    R64 = slice(64, 96)
    cqn = A.alloc("cqn", [3, 2048], BF16, nres=12)
    ckvn = A.alloc("ckvn", [2, 2048], BF16, nres=8)
    Ct = A.alloc("Ct", [2048], F32)
    St = A.alloc("St", [2048], F32)
    kr = A.alloc("kr", [2048], BF16, nres=4)
    mark0 = A.off
    win = A.alloc("win", [8, 768], BF16)
    cq_t = [A.alloc("cqt%d" % i, [512], F32) for i in range(3)]
    ang = A.alloc("ang", [2048], F32)
    tmp = A.alloc("rtmp", [2048], F32)
    tmpi = A.alloc("rtmpi", [2048], I32)
    t1 = A.alloc("t1", [512], F32)
    t2 = A.alloc("t2", [512], F32)
    qng = A.alloc("qng", [8], F32)
    P.dma("pool", win[:, :, :], C.d_mlawin.rearrange("p (a b) -> p a b", b=768), C.usem[0], writes=win.r)
    P.dma("sp", qng[:, :], C.d_mlang, C.usem[1], writes=qng.r)
    P.dma("sp", tmpi[0:96, :], C.d_pos.broadcast_to([96, 2048]), C.usem[1], writes=tmpi.r)
    emit_norm(C, 4 * l + 0)
    P.copy("dve", ang[R64, :], tmpi[R64, :], tmpi.r, ang.r)
    P.ts("dve", ang[R64, :], ang[R64, :], C.cf[R64, 0:1], None, ALU.mult, None, ang.r + C.cf.r, ang.r)
    emit_sin_table(C, Ct[R64, :], Ct.r, ang, tmp, tmpi, R64, 0.25)
    emit_sin_table(C, St[R64, :], St.r, ang, tmp, tmpi, R64, 0.0)
    P.ts("dve", St[R64, :], St[R64, :], C.cf[R64, 1:2], None, ALU.mult, None, St.r + C.cf.r, St.r)
    for tg in range(4):
        cols = slice(512 * tg, 512 * tg + 512)
        xr = [C.xnT.r[4 * k + tg] for k in range(8)]
        for (nch, c0, dst, gcol, ssb) in ((3, 0, cqn, 0, 6), (2, 384, ckvn, 3, 7)):
            ss = C.PS[ssb]
            for ch_ in range(nch):
                ps = C.PS[ch_ % 2]
                pairs = [(win[:, k, c0 + 128 * ch_:c0 + 128 * ch_ + 128], C.xnT[:, k, cols]) for k in range(8)]
                P.mm_group(ps[:, :], pairs, win.r + xr, ps.r)
                P.copy("dve", cq_t[ch_][:, :], ps[:, :], ps.r, cq_t[ch_].r)
                sq = C.sq[ch_ % 2]
                P.act(sq[:, :], ps[:, :], AF.Square, ps.r, sq.r)
                P.mm(ss[:, :], cbs(C, CB_ONES), sq[:, :], start=(ch_ == 0), stop=(ch_ == nch - 1),
                     reads=sq.r + C.cb.r, writes=ss.r)
            P.act(C.rsb[:, :], ss[:, :], AF.Sqrt, ss.r, C.rsb.r, bias=C.epsb[:, 0:1], scale=1.0 / (128 * nch))
            rstd = C.rstd[0]
            P.recip(rstd[:, :], C.rsb[:, :], C.rsb.r, rstd.r)
            for ch_ in range(nch):
                P.stt(dst[:, ch_, cols], cq_t[ch_][:, :], qng[:, gcol + ch_:gcol + ch_ + 1], rstd[:, :],
                      ALU.mult, ALU.mult, cq_t[ch_].r + rstd.r + qng.r, [dst.r[4 * ch_ + tg]])
        psA = C.PS[2]
        psB = C.PS[3]
        P.mm_group(psA[0:96, :], [(win[:, k, 576:672], C.xnT[:, k, cols]) for k in range(8)], win.r + xr, psA.r)
        P.mm_group(psB[0:96, :], [(win[:, k, 672:768], C.xnT[:, k, cols]) for k in range(8)], win.r + xr, psB.r)
        P.tt("dve", t1[R64, :], psA[R64, :], Ct[R64, cols], ALU.mult, psA.r + Ct.r, t1.r)
        P.tt("dve", t2[R64, :], psB[R64, :], St[R64, cols], ALU.mult, psB.r + St.r, t2.r)
        P.tt("pool", kr[R64, cols], t1[R64, :], t2[R64, :], ALU.add, t1.r + t2.r, [kr.r[tg]])
    P.barrier()
    A.reset(mark0)
    oT = C.xnT
    wh = [A.alloc("wh%d" % i, [3 * 192 + 2 * 128], BF16) for i in range(2)]
    qf = [A.alloc("qf%d" % i, [2048], BF16, nres=4) for i in range(2)]
    kf = [A.alloc("kf%d" % i, [2048], BF16, nres=4) for i in range(2)]
    Vh = [A.alloc("Vh%d" % i, [16, 65], BF16, nres=4) for i in range(2)]
    o_pair = [A.alloc("op%d" % i, [16, 128], BF16, nres=4) for i in range(2)]
    t1 = A.alloc("t1b", [512], F32)
    t2 = A.alloc("t2b", [512], F32)
    chains = []
    for c in range(2):
        ch = Chain()
        ch.PT = [A.alloc("PT%d%d" % (c, i), [512], BF16) for i in range(2)]
        ch.rden = A.alloc("rden%d" % c, [4], F32)
        ch.Zb = [C.PS[c], C.PS[2 + c]]
        ch.Ob = C.PS[4 + c]
        steps = []
        for QG in ([0, 3] if c == 0 else [1, 2]):
            nk = 4 * QG + 4
            for jb in range(nk):
                r = jb - 4 * QG
                ta = 4 * QG + max(0, r)
                steps.append(dict(QG=QG, jb=jb, ta=ta, tb=4 * QG + 3, first=(jb == 0), last=(jb == nk - 1),
                                  g=0, mask=None, negmask=(cbs(C, CB_NEG_INCL) if r >= 0 else None)))
        ch.steps = steps
        chains.append(ch)
    for s_ in range(2):
        P.memset("pool", Vh[s_][:, :, 64:65], 1.0, Vh[s_].r)

    def head_units(h):
        s_ = h % 2
        w = wh[s_]
        wq_ = w[:, 0:576].rearrange("p (a b) -> p a b", b=192)
        wkv = w[:, 576:832].rearrange("p (a b) -> p a b", b=128)
        units = []
        units.append(lambda: P.dma("pool", w[:, :], C.d_mlawh[h], C.usem[s_], writes=w.r))
        ucnt = [0]

        def q_unit(tg):
            def f():
                cols = slice(512 * tg, 512 * tg + 512)
                psA = C.PS[6]
                psB = C.PS[7]
                cr = [cqn.r[4 * kc + tg] for kc in range(3)]
                P.mm_group(psA[0:96, :], [(wq_[:, kc, 0:96], cqn[:, kc, cols]) for kc in range(3)], w.r + cr, psA.r)
                P.mm_group(psB[0:96, :], [(wq_[:, kc, 96:192], cqn[:, kc, cols]) for kc in range(3)], w.r + cr, psB.r)
                P.ts("dve", qf[s_][0:64, cols], psA[0:64, :], SC, None, ALU.mult, None, psA.r, [qf[s_].r[tg]])
                P.stt(t1[R64, :], psA[R64, :], SC, Ct[R64, cols], ALU.mult, ALU.mult, psA.r + Ct.r, t1.r)
                P.stt(t2[R64, :], psB[R64, :], SC, St[R64, cols], ALU.mult, ALU.mult, psB.r + St.r, t2.r)
                P.tt("pool", qf[s_][R64, cols], t1[R64, :], t2[R64, :], ALU.add, t1.r + t2.r, [qf[s_].r[tg]])
            return f

        def k_unit(tg):
            def f():
                cols = slice(512 * tg, 512 * tg + 512)
                ps = C.PS[6 + tg % 2]
                cr = [ckvn.r[4 * kc + tg] for kc in range(2)]
                P.mm_group(ps[0:64, :], [(wkv[:, kc, 0:64], ckvn[:, kc, cols]) for kc in range(2)], w.r + cr, ps.r)
                P.copy("dve", kf[s_][0:64, cols], ps[0:64, :], ps.r, [kf[s_].r[tg]])
                P.copy("pool", kf[s_][R64, cols], kr[R64, cols], [kr.r[tg]], [kf[s_].r[tg]])
            return f

        def v_unit(vb):
            def f():
                ps = C.PS[6 + vb % 2]
                for jj in range(8):
                    tile = 8 * vb + jj
                    tcols = slice(128 * tile, 128 * tile + 128)
                    pairs = [(ckvn[:, kc, tcols], wkv[:, kc, 64:128]) for kc in range(2)]
                    P.mm_group(ps[:, jj * 64:(jj + 1) * 64], pairs,
                               w.r + [ckvn.r[4 * kc + tile // 4] for kc in range(2)], ps.r)
                P.copy("dve", Vh[s_][:, 8 * vb:8 * vb + 8, 0:64],
                       ps[:, :].rearrange("p (a b) -> p a b", b=64), ps.r, [Vh[s_].r[2 * vb], Vh[s_].r[2 * vb + 1]])
            return f

        for tg in range(4):
            units.append(k_unit(tg))
            units.append(q_unit(tg))
        units.append(v_unit(0))
        units.append(v_unit(1))
        return units

    for u in head_units(0):
        u()
    for h in range(16):
        s_ = h % 2
        ps_ = (h // 2) % 2
        for ch in chains:
            ch.kT = (lambda s_: (lambda g, jb: (kf[s_][0:96, 128 * jb:128 * jb + 128], [kf[s_].r[jb // 4]])))(s_)
            ch.qT = (lambda s_: (lambda g, t0, n: (qf[s_][0:96, t0:t0 + n], [qf[s_].r[t0 // 512]])))(s_)
            ch.V = (lambda s_: (lambda g, jb: (Vh[s_][:, jb, :], [Vh[s_].r[jb // 4]])))(s_)
            ch.o_dst = (lambda hh, ps_: (lambda t: (o_pair[ps_][:, t, 64 * hh:64 * hh + 64],
                                                    [o_pair[ps_].r[t // 4]])))(h % 2, ps_)
        bg = deque(head_units(h + 1)) if h < 15 else deque()
        emit_softmax_attention(C, chains, bg, bg_every=1)
        if h % 2 == 1:
            emit_transposes(C, o_pair[ps_], oT, h // 2)
    P.barrier()
    A.reset(mark0)
    emit_attn_out_proj(C, A, lambda dc: C.d_mlawo[dc], 8, oT, 4 * l + 1)
    P.barrier()


DIL_W = (256, 640, 2048)
DIL_OFF = (0, 256, 896)
DIL_BACK = (1, 4, 15)


def emit_dil_layer(C, l):
    P = C.P
    A = C.A
    A.reset()
    oT = A.alloc("oT", [4, 2048], BF16, nres=16)
    mark = A.off
    valid = A.alloc("valid", [2944], BF16)
    wq = [A.alloc("wq%d" % i, [8, 384], BF16) for i in range(2)]
    qT = [A.alloc("qT%d" % g, [2048], BF16, nres=4) for g in range(3)]
    kT = [A.alloc("kT%d" % g, [2048], BF16, nres=4) for g in range(3)]
    Vv = [A.alloc("V%d" % g, [16, 130], BF16, nres=4) for g in range(3)]
    M = [A.alloc("M%d" % hh, [2944], BF16, nres=3) for hh in range(2)]
    stage = [A.alloc("stage0", [1024], F32)] * 2
    scnt = [0]
    o_pair = A.alloc("op", [16, 128], BF16, nres=4)
    for g in range(3):
        for hh in range(2):
            P.memset("pool", Vv[g][:, :, 65 * hh + 64:65 * hh + 65], 1.0, Vv[g].r)
    chains = []
    for c in range(2):
        ch = Chain()
        ch.PT = [A.alloc("PT%d%d" % (c, i), [512], BF16) for i in range(3)]
        ch.rden = A.alloc("rden%d" % c, [4], F32)
        ch.Zb = [C.PS[c], C.PS[2 + c], C.PS[6 + c]]
        ch.Ob = C.PS[4 + c]
        steps = []
        for QG in range(4):
            lst = []
            for g in range(3):
                for jb in range(max(0, 4 * QG - DIL_BACK[g]), 4 * QG + 4):
                    ta = max(jb, 4 * QG)
                    tb = min(jb + DIL_BACK[g], 4 * QG + 3)
                    if tb < ta:
                        continue
                    x0 = DIL_OFF[g] + (ta - jb) * 128
                    N = (tb - ta + 1) * 128
                    lst.append(dict(QG=QG, jb=jb, ta=ta, tb=tb, first=False, last=False, g=g,
                                    mask=(M[c][:, x0:x0 + N], [M[c].r[g]], N)))
            lst[0]["first"] = True
            lst[-1]["last"] = True
            steps += lst
        ch.steps = steps
        chains.append(ch)
    wissued = [0]

    def issue_w(idx):
        if idx >= 12 or idx < wissued[0]:
            return
        wissued[0] = idx + 1
        w_ = wq[idx % 2]
        P.dma("pool", w_[:, :, :], C.d_dilqkv[idx].rearrange("p (a b) -> p a b", b=384),
              C.usem[idx % 2], writes=w_.r)

    for p in range(4):
        issue_w(3 * p)
        issue_w(3 * p + 1)
        if p == 0:
            P.dma("pool", valid[:, :], C.d_dilvalid, C.wsem[1], writes=valid.r)
            emit_norm(C, 4 * l + 0)
        for hh in range(2):
            for g in range(3):
                for c0 in range(0, DIL_W[g], 1024):
                    W = min(1024, DIL_W[g] - c0)
                    o = DIL_OFF[g] + c0
                    stg = stage[scnt[0] % 2]
                    scnt[0] += 1
                    P.dma("sp", stg[:, 0:W], C.d_dilslab[g * 8 + 2 * p + hh][:, c0:c0 + W], C.wsem[scnt[0] % 2],
                          writes=stg.r)
                    P.act(stg[:, 0:W], stg[:, 0:W], AF.Exp, stg.r, stg.r)
                    P.tt("pool", M[hh][:, o:o + W], stg[:, 0:W], valid[:, o:o + W], ALU.mult,
                         stg.r + valid.r, [M[hh].r[g]])
        for g in range(3):
            issue_w(3 * p + g)
            w = wq[(3 * p + g) % 2]
            u = 0
            for tg in range(4):
                cols = slice(512 * tg, 512 * tg + 512)
                xr = [C.xnT.r[4 * k + tg] for k in range(8)]
                for which in range(2):
                    ps = C.PS[6 + u % 2]
                    u += 1
                    pairs = [(w[:, k, 128 * which:128 * which + 128], C.xnT[:, k, cols]) for k in range(8)]
                    P.mm_group(ps[:, :], pairs, w.r + xr, ps.r)
                    if which == 0:
                        P.ts("dve", qT[g][:, cols], ps[:, :], 0.125, None, ALU.mult, None, ps.r, [qT[g].r[tg]])
                    else:
                        P.copy("dve", kT[g][:, cols], ps[:, :], ps.r, [kT[g].r[tg]])
                ps = C.PS[6 + u % 2]
                u += 1
                for jj in range(4):
                    tile = 4 * tg + jj
                    tcols = slice(128 * tile, 128 * tile + 128)
                    pairs = [(C.xnT[:, k, tcols], w[:, k, 256:384]) for k in range(8)]
                    P.mm_group(ps[:, jj * 128:(jj + 1) * 128], pairs, w.r + xr, ps.r)
                for hh in range(2):
                    src = ps[:, :].rearrange("p (a b) -> p a b", b=128)[:, :, 64 * hh:64 * hh + 64]
                    P.copy("dve", Vv[g][:, 4 * tg:4 * tg + 4, 65 * hh:65 * hh + 64], src,
                           ps.r, [Vv[g].r[tg]])
            issue_w(3 * p + g + 2)
            if g == 0 and p > 0:
                emit_transposes(C, o_pair, oT, p - 1)
        for c, ch in enumerate(chains):
            rows = slice(64 * c, 64 * c + 64)
            ch.kT = (lambda rows: (lambda g, jb: (kT[g][rows, 128 * jb:128 * jb + 128], [kT[g].r[jb // 4]])))(rows)
            ch.qT = (lambda rows: (lambda g, t0, n: (qT[g][rows, t0:t0 + n],
                                                     [qT[g].r[i] for i in range(t0 // 512, (t0 + n - 1) // 512 + 1)])))(rows)
            ch.V = (lambda c: (lambda g, jb: (Vv[g][:, jb, 65 * c:65 * c + 65], [Vv[g].r[jb // 4]])))(c)
            ch.o_dst = (lambda c: (lambda t: (o_pair[:, t, 64 * c:64 * c + 64], [o_pair.r[t // 4]])))(c)
        emit_softmax_attention(C, chains, deque(), look=2)
    emit_transposes(C, o_pair, oT, 3)
    P.barrier()
    A.reset(mark)
    emit_attn_out_proj(C, A, lambda dc: C.d_dilwo[dc], 4, oT, 4 * l + 1)
    P.barrier()


def build_nc(stages):
    nc = bass.Bass("TRN2", target_bir_lowering=False)
    C = Ctx()
    C.nc = nc

    def din(name, shape, dt=F32):
        return nc.dram_tensor(name, list(shape), dt, kind="ExternalInput").ap()

    d_xT = din("xT", [D, S])
    d_cst = din("cst", [128, 1024])
    d_gn = din("gains", [128, 128])
    d_cvp = din("convp", [128, 704])
    C.d_wup = din("wup", [4, NPAIR, 128, 2048])
    C.d_wdn = din("wdn", [4, 8, 128, DFF])
    C.d_sbqkv = din("sbqkv", [2, 8, 128, 3072])
    C.d_sbwo = din("sbwo", [2, 8, 128, 1024])
    C.d_pos = din("pos", [1, S], I32)
    d_cf = din("cf", [128, 8])
    C.d_mlawin = din("mlawin", [128, 8 * 768])
    C.d_mlang = din("mlang", [128, 8])
    C.d_mlawh = din("mlawh", [16, 128, 832])
    C.d_mlawo = din("mlawo", [8, 128, 1024])
    C.d_dilqkv = din("dilqkv", [12, 128, 3072])
    C.d_dilwo = din("dilwo", [8, 128, 512])
    C.d_dilslab = din("dilslab", [24, 128, 2048])
    C.d_dilvalid = din("dilvalid", [128, 2944])
    d_out = nc.dram_tensor("outT", [D, S], F32, kind="ExternalOutput").ap()

    with ExitStack() as es:
        P = Prog(nc, es)
        C.P = P
        C.xT = P.sbuf("xT_sb", [128, 8, S], F32, nres=32)
        C.xnT = P.sbuf("xnT_sb", [128, 8, S], BF16, nres=32)
        C.cb = P.sbuf("cb", [128, 1024], BF16)
        C.gn = P.sbuf("gn", [128, 128], F32)
        C.cvp = P.sbuf("cvp", [128, 704], F32)
        C.sq = [P.sbuf("sq%d" % i, [128, 512], BF16) for i in range(2)]
        C.rsb = P.sbuf("rsb", [128, 512], F32)
        C.rsb2 = [C.rsb, C.rsb]
        C.rstd = [P.sbuf("rstd%d" % i, [128, 512], F32) for i in range(2)]
        C.cf = P.sbuf("cf_sb", [128, 8], F32)
        C.epsb = P.sbuf("epsb", [128, 1], F32)
        C.oneb = P.sbuf("oneb", [128, 1], F32)
        arena = P.sbuf("arena", [128, ARENA_F32], F32)
        C.A = Arena(arena, ARENA_F32)
        C.PS = [P.psum("ps%d" % i, [128, 512], F32) for i in range(8)]
        C.wsem = [P.dsem("wsem%d" % i) for i in range(2)]
        C.usem = [P.dsem("usem%d" % i) for i in range(2)]
        s_in = P.dsem("s_in")
        s_x = [P.dsem("s_x%d" % i) for i in range(2)]
        s_out = P.dsem("s_out")

        P.dma("pool", C.cb[:, :], d_cst, s_in, writes=C.cb.r)
        P.dma("sp", C.gn[:, :], d_gn, s_in, writes=C.gn.r)
        P.dma("sp", C.cvp[:, :], d_cvp, s_in, writes=C.cvp.r)
        P.dma("sp", C.cf[:, :], d_cf, s_in, writes=C.cf.r)
        P.memset("pool", C.epsb[:, :], EPS, C.epsb.r)
        P.memset("pool", C.oneb[:, :], 1.0, C.oneb.r)
        for k in range(8):
            P.dma("sp", C.xT[:, k, :], d_xT[128 * k:128 * k + 128, :], s_x[k % 2],
                  writes=[C.xT.r[4 * k + tg] for tg in range(4)])

        for st in stages:
            kind, l = st
            if kind == "norm":
                emit_norm(C, l)
            elif kind == "ffn":
                emit_ffn(C, l)
            elif kind == "mix":
                if l % 3 == 0:
                    emit_sb_layer(C, l, l // 3)
                elif l % 3 == 1:
                    emit_dil_layer(C, l)
                else:
                    emit_mla_layer(C, l)
        P.barrier()
        fin = Res("fin")
        for k in range(8):
            P.dma("sp", d_out[128 * k:128 * k + 128, :], C.xT[:, k, :], s_out,
                  reads=[C.xT.r[4 * k + tg] for tg in range(4)], writes=[fin])
        P.op("sp", None, reads=[fin])
        P.emit()
        C.stats = dict(ecnt=dict(P.ecnt), n_wait=P.n_wait)
    return nc, C


def lhsT_stream_layout(W):
    K = W.shape[0]
    nk = K // 128
    return np.ascontiguousarray(W.reshape(nk, 128, 8, 128).transpose(2, 1, 0, 3).reshape(8, 128, K))


def prep_shared(inp):
    f = lambda a: np.asarray(a, dtype=np.float32)
    out = {}
    i = np.arange(128)
    cst = np.zeros((128, 1024), np.float32)
    cst[:, 0:128] = np.eye(128)
    cst[:, 128:256] = 1.0
    cst[:, 256:384] = -1.0 * (i[:, None] >= i[None, :])
    cst[:, 384:512] = -1.0 * (i[:, None] < i[None, :])
    cst[:, 512:640] = (i[:, None] < i[None, :])
    cst[:, 640:768] = (i[:, None] <= i[None, :])
    cst[:, 768:896] = NEG_BIG * (i[:, None] >= i[None, :])
    cst[:, 896:1024] = NEG_BIG * (i[:, None] > i[None, :])
    out["cst"] = cst
    ng = f(inp["norm_gains"])
    out["gains"] = np.ascontiguousarray(ng.reshape(16, 8, 128).transpose(2, 0, 1).reshape(128, 128))
    cw = f(inp["ffn_conv_w"])
    cbias = f(inp["ffn_conv_b"])
    cv = np.concatenate([cw, cbias[:, None, :]], axis=1)
    cv = cv.reshape(4, 4, 44, 128).transpose(3, 0, 2, 1)
    out["convp"] = np.ascontiguousarray(cv.reshape(128, 704))
    wu = f(inp["ffn_w_up"])
    g = wu[:, :, :DFF].reshape(4, 8, 128, NPAIR, 128)
    u = wu[:, :, DFF:].reshape(4, 8, 128, NPAIR, 128)
    gu = np.stack([g, u], axis=4)
    out["wup"] = np.ascontiguousarray(gu.transpose(0, 3, 2, 1, 4, 5).reshape(4, NPAIR, 128, 2048))
    wd = f(inp["ffn_w_down"])
    out["wdn"] = np.stack([lhsT_stream_layout(wd[l]) for l in range(4)])
    sq = f(inp["sb_w_qkv"])
    t = sq.reshape(2, 8, 128, 3, 8, 128)
    out["sbqkv"] = np.ascontiguousarray(t.transpose(0, 4, 2, 1, 3, 5).reshape(2, 8, 128, 3072))
    so = f(inp["sb_w_o"])
    out["sbwo"] = np.stack([lhsT_stream_layout(so[jj]) for jj in range(2)])
    cfm = np.zeros((128, 8), np.float32)
    freqs = (np.float32(10000.0) ** (-np.arange(16, dtype=np.float32) / np.float32(16))).astype(np.float32)
    cfm[64:80, 0] = freqs
    cfm[80:96, 0] = freqs
    cfm[64:80, 1] = -1.0
    cfm[80:96, 1] = 1.0
    out["cf"] = cfm
    wi = f(inp["mla_w_in"])[0]
    wi2 = np.concatenate([wi, wi[:, 576:640], wi[:, 656:672], wi[:, 640:656]], axis=1)
    out["mlawin"] = np.ascontiguousarray(wi2.reshape(8, 128, 768).transpose(1, 0, 2).reshape(128, 8 * 768))
    ng_ = np.zeros((128, 8), np.float32)
    ng_[:, 0:3] = f(inp["mla_q_norm"])[0].reshape(3, 128).T
    ng_[:, 3:5] = f(inp["mla_kv_norm"])[0].reshape(2, 128).T
    out["mlang"] = ng_
    wqb = f(inp["mla_w_qb"])[0]
    wkvb = f(inp["mla_w_kvb"])[0]
    whs = []
    for h in range(16):
        a = wqb[:, 96 * h:96 * h + 96]
        b = np.concatenate([wqb[:, 96 * h:96 * h + 64], wqb[:, 96 * h + 80:96 * h + 96],
                            wqb[:, 96 * h + 64:96 * h + 80]], axis=1)
        q2 = np.concatenate([a, b], axis=1).reshape(3, 128, 192).transpose(1, 0, 2).reshape(128, 576)
        kv = wkvb[:, 128 * h:128 * h + 128].reshape(2, 128, 128).transpose(1, 0, 2).reshape(128, 256)
        whs.append(np.concatenate([q2, kv], axis=1))
    out["mlawh"] = np.ascontiguousarray(np.stack(whs))
    out["mlawo"] = lhsT_stream_layout(f(inp["mla_w_o"])[0])
    dq = f(inp["dil_w_qkv"])[0]
    t = dq.reshape(8, 128, 3, 3, 4, 128)
    out["dilqkv"] = np.ascontiguousarray(t.transpose(4, 3, 1, 0, 2, 5).reshape(12, 128, 3072))
    out["dilwo"] = lhsT_stream_layout(f(inp["dil_w_o"])[0])
    rb = f(inp["rel_bias"])
    jk = np.arange(128)[:, None]
    xx = np.arange(2048)[None, :]
    delta = xx - jk
    dpos = np.maximum(delta, 0)
    dflt = np.maximum(dpos.astype(np.float32), np.float32(1.0))
    large = 16 + (np.log(dflt / np.float32(16)) / np.float32(math.log(2048 / 16)) * np.float32(16)).astype(np.int32)
    large = np.minimum(large, 31)
    bucket = np.where(dpos < 16, dpos, large)
    slab = np.zeros((24, 128, 2048), np.float32)
    for gh in range(24):
        slab[gh] = rb[bucket, gh]
    out["dilslab"] = slab
    valid = np.zeros((128, 2944), np.float32)
    for g, (W, r) in enumerate(((256, 1), (640, 4), (2048, 16))):
        dl = delta[:, :W]
        valid[:, DIL_OFF[g]:DIL_OFF[g] + W] = (dl >= 0) & (dl % r == 0) & (dl <= 128 * r)
    out["dilvalid"] = valid
    return out


ALL_STAGES = [("mix", 0), ("ffn", 0), ("mix", 1), ("ffn", 1), ("mix", 2), ("ffn", 2), ("mix", 3), ("ffn", 3)]
_CACHE = {}


def run(inputs, stages, n_cores=8, trace=False):
    key = tuple(stages)
    if key not in _CACHE:
        _CACHE[key] = build_nc(stages)
    nc, C = _CACHE[key]
    shared = prep_shared(inputs)
    x = np.asarray(inputs["x"], dtype=np.float32)
    in_maps = []
    for b in range(n_cores):
        m = dict(shared)
        m["xT"] = np.ascontiguousarray(x[b].T)
        m["pos"] = np.ascontiguousarray(np.asarray(inputs["positions"])[b][None, :].astype(np.int32))
        in_maps.append(m)
    res = run_bass_kernel_spmd(nc, in_maps, core_ids=list(range(n_cores)), trace=trace)
    out = np.stack([np.ascontiguousarray(r["outT"].T) for r in res.results])
    return out, res


def kernel(**inputs):
    out, _ = run(inputs, ALL_STAGES, 8)
    return out.astype(np.float32)
```

```python
import math
from collections import deque
from contextlib import ExitStack

import numpy as np
import concourse.bass as bass
import concourse.mybir as mybir
from concourse.bass_utils import run_bass_kernel_spmd

F32 = mybir.dt.float32
BF16 = mybir.dt.bfloat16
I32 = mybir.dt.int32
AF = mybir.ActivationFunctionType
ALU = mybir.AluOpType

S = 2048
D = 1024
DFF = 2816
NPAIR = 22
EPS = 1e-6
ENGS = ("pe", "act", "dve", "pool", "sp")
import os as _os
SAME_ENGINE_SYNC = _os.environ.get("NOSES", "") == ""


class Res:
    __slots__ = ("name", "w", "rs", "excl")

    def __init__(self, name=""):
        self.name = name
        self.w = None
        self.rs = {}
        self.excl = False


class Buf:
    def __init__(self, t, nres=1, name=""):
        self.t = t
        self.r = [Res("%s.%d" % (name, i)) for i in range(nres)]

    def __getitem__(self, idx):
        return self.t[idx]


class Prog:
    def __init__(self, nc, es, same_engine_sync=SAME_ENGINE_SYNC):
        self.nc = nc
        self.es = es
        self.ops = {e: [] for e in ENGS}
        self.ecnt = {e: 0 for e in ENGS}
        self.esem = {}
        for e in ("pe", "act", "dve", "pool"):
            self.esem[e] = es.enter_context(nc.semaphore("es_" + e))
        self.known = {e: {} for e in ENGS}
        self.dcnt = {}
        self.same_engine_sync = same_engine_sync
        self.n_wait = 0

    def sbuf(self, name, shape, dtype, nres=1):
        t = self.es.enter_context(self.nc.sbuf_tensor(name, list(shape), dtype))
        return Buf(t, nres, name)

    def psum(self, name, shape, dtype, nres=1):
        t = self.es.enter_context(self.nc.psum_tensor(name, list(shape), dtype))
        b = Buf(t, nres, name)
        for r in b.r:
            r.excl = True
        return b

    def dsem(self, name):
        s = self.es.enter_context(self.nc.semaphore(name))
        self.dcnt[s] = 0
        return s

    def op(self, eng, fn, reads=(), writes=(), inc=True, dsem=None):
        waits = {}
        kn = self.known[eng]
        own = self.esem.get(eng)
        if any(r.excl for r in reads):
            writes = list(writes) + [r for r in reads if r.excl and r not in writes]
            reads = [r for r in reads if not r.excl]

        def need(m):
            if m is None:
                return
            sem, val = m
            if sem in self.dcnt:
                val = self.dcnt[sem]
            elif sem is own:
                if eng == "pe" or not self.same_engine_sync:
                    return
            if kn.get(sem, 0) >= val:
                return
            if waits.get(sem, 0) < val:
                waits[sem] = val

        for r in reads:
            need(r.w)
        for w in writes:
            need(w.w)
            for m in w.rs.items():
                need(m)
        for sem, val in waits.items():
            kn[sem] = val
        if dsem is not None:
            self.dcnt[dsem] += 16
            marker = (dsem, self.dcnt[dsem])
            incspec = (dsem, 16)
        elif eng == "sp":
            marker = None
            incspec = None
        elif inc:
            self.ecnt[eng] += 1
            marker = (own, self.ecnt[eng])
            incspec = (own, 1)
        else:
            marker = (own, self.ecnt[eng] + 1)
            incspec = None
        if marker is not None:
            for r in reads:
                if r.rs.get(marker[0], 0) < marker[1]:
                    r.rs[marker[0]] = marker[1]
            for w in writes:
                w.w = marker
                w.rs = {}
        self.n_wait += len(waits)
        self.ops[eng].append((list(waits.items()), fn, incspec))

    def barrier(self):
        targets = [(self.esem[e], self.ecnt[e]) for e in self.esem if self.ecnt[e] > 0]
        targets += [(s, c) for s, c in self.dcnt.items() if c > 0]
        for eng in ENGS:
            waits = []
            for sem, val in targets:
                if sem is self.esem.get(eng):
                    continue
                if self.known[eng].get(sem, 0) >= val:
                    continue
                self.known[eng][sem] = val
                waits.append((sem, val))
            if waits:
                self.ops[eng].append((waits, None, None))

    def emit(self):
        nc = self.nc
        with nc.Block() as block:
            def run(engname):
                def body(e):
                    for waits, fn, incspec in self.ops[engname]:
                        for sem, val in waits:
                            e.wait_ge(sem, val)
                        if fn is None:
                            continue
                        ins = fn(e)
                        if incspec is not None:
                            ins.then_inc(incspec[0], incspec[1])
                return body

            block.tensor(run("pe"))
            block.scalar(run("act"))
            block.vector(run("dve"))
            block.gpsimd(run("pool"))
            block.sync(run("sp"))

    def dma(self, eng, out, in_, dsem, reads=(), writes=()):
        if eng == "pool":
            self.op(eng, lambda e: e.dma_start(out=out, in_=in_, max_dma_last_dim=4096), reads, writes, dsem=dsem)
        else:
            self.op(eng, lambda e: e.dma_start(out=out, in_=in_), reads, writes, dsem=dsem)

    def mm(self, out, lhsT, rhs, start, stop, reads, writes, inc=True, **kw):
        self.op("pe", lambda e: e.matmul(out, lhsT, rhs, start=start, stop=stop, **kw),
                reads, writes, inc=inc)

    def mm_group(self, out, pairs, reads, writes, **kw):
        n = len(pairs)
        for i, (l, r) in enumerate(pairs):
            self.mm(out, l, r, start=(i == 0), stop=(i == n - 1),
                    reads=reads, writes=writes, inc=(i == n - 1), **kw)

    def act(self, out, in_, func, reads, writes, **kw):
        self.op("act", lambda e: e.activation(out=out, in_=in_, func=func, **kw), reads, writes)

    def tt(self, eng, out, in0, in1, op, reads, writes):
        self.op(eng, lambda e: e.tensor_tensor(out=out, in0=in0, in1=in1, op=op), reads, writes)

    def ts(self, eng, out, in0, s1, s2, op0, op1, reads, writes):
        if s2 is None:
            self.op(eng, lambda e: e.tensor_scalar(out=out, in0=in0, scalar1=s1, scalar2=None, op0=op0),
                    reads, writes)
        else:
            self.op(eng, lambda e: e.tensor_scalar(out=out, in0=in0, scalar1=s1, scalar2=s2,
                                                   op0=op0, op1=op1), reads, writes)

    def stt(self, out, in0, scalar, in1, op0, op1, reads, writes):
        self.op("dve", lambda e: e.scalar_tensor_tensor(out=out, in0=in0, scalar=scalar, in1=in1,
                                                        op0=op0, op1=op1), reads, writes)

    def copy(self, eng, out, in_, reads, writes):
        if eng == "act":
            self.act(out, in_, AF.Copy, reads, writes)
        else:
            self.op(eng, lambda e: e.tensor_copy(out=out, in_=in_), reads, writes)

    def recip(self, out, in_, reads, writes):
        self.op("dve", lambda e: e.reciprocal(out=out, in_=in_), reads, writes)

    def memset(self, eng, out, val, writes):
        self.op(eng, lambda e: e.memset(out, val), (), writes)


def split_mm_group(P, out, pairs, reads, writes, chunk=2):
    units = []
    n = len(pairs)
    for c0 in range(0, n, chunk):
        def f(c0=c0):
            for i in range(c0, min(n, c0 + chunk)):
                P.mm(out, pairs[i][0], pairs[i][1], start=(i == 0), stop=(i == n - 1),
                     reads=reads, writes=writes, inc=(i == n - 1))
        units.append(f)
    return units


class Arena:
    def __init__(self, buf, nf32):
        self.buf = buf
        self.n = nf32
        self.off = 0

    def reset(self, to=0):
        self.off = to

    def alloc(self, name, shape, dtype, nres=1):
        n = 1
        for s_ in shape:
            n *= s_
        esz = 4 if dtype in (F32, I32) else 2
        nf = (n * esz + 3) // 4
        nf = (nf + 3) // 4 * 4
        assert self.off + nf <= self.n, "arena overflow %s: %d + %d > %d" % (name, self.off, nf, self.n)
        ap = self.buf.t[:, self.off:self.off + nf]
        self.off += nf
        if dtype != F32:
            ap = ap.bitcast(dtype)
        ap = ap[:, 0:n]
        if len(shape) == 2:
            ap = ap.rearrange("p (a b) -> p a b", b=shape[1])
        elif len(shape) == 3:
            ap = ap.rearrange("p (a b c) -> p a b c", b=shape[1], c=shape[2])
        return Buf(ap, nres, name)


CB_IDENT, CB_ONES, CB_TRI_INCL, CB_TRI_REST, CB_M_STRICT, CB_M_INCL, CB_NEG_STRICT, CB_NEG_INCL = range(8)
NEG_BIG = -30000.0
ARENA_F32 = 25216


class Ctx:
    pass


def cbs(C, i):
    return C.cb[:, 128 * i:128 * (i + 1)]


def emit_norm(C, gidx):
    P = C.P
    for tg in range(4):
        cols = slice(512 * tg, 512 * tg + 512)
        ss = C.PS[6 + tg % 2]
        for k in range(8):
            sq = C.sq[k % 2]
            P.act(sq[:, :], C.xT[:, k, cols], AF.Square, [C.xT.r[4 * k + tg]], sq.r)
            P.mm(ss[:, :], cbs(C, CB_ONES), sq[:, :], start=(k == 0), stop=(k == 7),
                 reads=sq.r + C.cb.r, writes=ss.r)
        rsb = C.rsb
        P.act(rsb[:, :], ss[:, :], AF.Sqrt, ss.r, rsb.r, bias=C.epsb[:, 0:1], scale=1.0 / D)
        rstd = C.rstd[tg % 2]
        P.recip(rstd[:, :], rsb[:, :], rsb.r, rstd.r)
        for k in range(8):
            P.stt(C.xnT[:, k, cols], C.xT[:, k, cols], C.gn[:, gidx * 8 + k:gidx * 8 + k + 1], rstd[:, :],
                  ALU.mult, ALU.mult, [C.xT.r[4 * k + tg]] + rstd.r + C.gn.r, [C.xnT.r[4 * k + tg]])


class OutProj:
    def __init__(self, C, bufs, wsrc, nk, rhs_fn, tgs, gidx):
        self.C, self.bufs, self.wsrc, self.nk, self.rhs_fn, self.tgs, self.gidx = C, bufs, wsrc, nk, rhs_fn, tgs, gidx
        self.loaded = set()

    def load(self, dc):
        if dc in self.loaded or dc >= 8:
            return
        self.loaded.add(dc)
        C = self.C
        w = self.bufs[0][dc % 2]
        C.P.dma("pool", w[:, :], self.wsrc(dc), C.wsem[dc % 2], writes=w.r)

    def run_mm(self, fofs=0, ssb=2, bg=None):
        C, nk, rhs_fn, tgs, gidx = self.C, self.nk, self.rhs_fn, self.tgs, self.gidx
        P = C.P
        wbuf, fT, tmpb = self.bufs
        self.fofs = fofs
        self.ssb = ssb
        self.load(0)
        for dc in range(8):
            w = wbuf[dc % 2]
            for ti, tg in enumerate(tgs):
                ps = C.PS[ti]
                pairs = []
                reads = list(w.r)
                for kc in range(nk):
                    ap, rr = rhs_fn(kc, ti)
                    pairs.append((w[:, kc * 128:(kc + 1) * 128], ap))
                    reads += rr
                P.mm_group(ps[:, :], pairs, reads, ps.r)
                if ti == 0:
                    self.load(dc + 1)
                fo = fofs + ti * 512
                P.copy("dve", fT[:, dc, fo:fo + 512], ps[:, :], ps.r, [fT.r[(fofs // 512 + ti) * 8 + dc]])
                sq = C.sq[ti]
                P.act(sq[:, :], ps[:, :], AF.Square, ps.r, sq.r)
                ss = C.PS[ssb + ti]
                P.mm(ss[:, :], cbs(C, CB_ONES), sq[:, :], start=(dc == 0), stop=(dc == 7),
                     reads=sq.r + C.cb.r, writes=ss.r)
                if bg:
                    bg.popleft()()
                    if dc >= 1 and bg:
                        bg.popleft()()

    def final_units(self, rstd_bufs=None):
        C, tgs, gidx = self.C, self.tgs, self.gidx
        P = C.P
        wbuf, fT, tmpb = self.bufs
        fofs, ssb = self.fofs, self.ssb
        rstds = rstd_bufs or C.rstd
        units = []

        def rs(ti):
            def f():
                ss = C.PS[ssb + ti]
                rsb = C.rsb2[ti]
                P.act(rsb[:, :], ss[:, :], AF.Sqrt, ss.r, rsb.r, bias=C.epsb[:, 0:1], scale=1.0 / D)
                P.recip(rstds[ti][:, :], rsb[:, :], rsb.r, rstds[ti].r)
            return f

        def fin(dc, ti, tg):
            def f():
                cols = slice(512 * tg, 512 * tg + 512)
                rstd = rstds[ti]
                tmp = tmpb[ti]
                fo = fofs + ti * 512
                fr = [fT.r[(fofs // 512 + ti) * 8 + dc]]
                P.tt("pool", tmp[:, :], fT[:, dc, fo:fo + 512], rstd[:, :], ALU.mult, fr + rstd.r, tmp.r)
                P.stt(C.xT[:, dc, cols], tmp[:, :], C.gn[:, gidx * 8 + dc:gidx * 8 + dc + 1], C.xT[:, dc, cols],
                      ALU.mult, ALU.add, [C.xT.r[4 * dc + tg]] + tmp.r + C.gn.r, [C.xT.r[4 * dc + tg]])
            return f

        for ti, tg in enumerate(tgs):
            units.append(rs(ti))
        for dc in range(8):
            for ti, tg in enumerate(tgs):
                units.append(fin(dc, ti, tg))
        return units

    def run(self):
        self.run_mm()
        for u in self.final_units():
            u()


def emit_attn_out_proj(C, A, wsrc, nk, oT, gidx):
    wbuf = [A.alloc("wo%d" % i, [nk * 128], BF16) for i in range(2)]
    fT = A.alloc("fT", [8, 2048], BF16, nres=32)
    tmpb = [A.alloc("tmp%d" % i, [512], F32) for i in range(2)]
    rstd2 = [A.alloc("rstdb%d" % i, [512], F32) for i in range(2)]
    ops = []
    for half in range(2):
        def rhs_fn(kc, ti, half=half):
            tg = 2 * half + ti
            return oT[:, kc, 512 * tg:512 * tg + 512], [oT.r[4 * kc + tg]]
        ops.append(OutProj(C, (wbuf, fT, tmpb), wsrc, nk, rhs_fn, [2 * half, 2 * half + 1], gidx))
    ops[0].run_mm(fofs=0, ssb=2)
    f0 = deque(ops[0].final_units())
    ops[1].run_mm(fofs=1024, ssb=6, bg=f0)
    while f0:
        f0.popleft()()
    for u in ops[1].final_units(rstd_bufs=rstd2):
        u()


def emit_out_proj(C, bufs, wsrc, nk, rhs_fn, tgs, gidx):
    OutProj(C, bufs, wsrc, nk, rhs_fn, tgs, gidx).run()


def alloc_outproj_bufs(C, nk, tmpb=None):
    A = C.A
    wbuf = [A.alloc("wo%d" % i, [nk * 128], BF16) for i in range(2)]
    fT = A.alloc("fT", [8, 1024], BF16, nres=16)
    if tmpb is None:
        tmpb = [A.alloc("tmp%d" % i, [512], F32) for i in range(2)]
    return wbuf, fT, tmpb


def emit_ffn(C, l):
    P = C.P
    A = C.A
    A.reset()
    mT = A.alloc("mT", [NPAIR, 1024], BF16, nres=NPAIR * 2)
    wup = [A.alloc("wup%d" % i, [8, 256], BF16) for i in range(2)]
    hs = [[A.alloc("hs%d%d" % (i, j), [514], F32) for j in range(2)] for i in range(2)]
    acc = [[A.alloc("acc%d%d" % (i, j), [512], F32) for j in range(2)] for i in range(2)]
    tails = A.alloc("tails", [44, 2], F32, nres=44)
    tmpx = A.alloc("tmpx", [512], F32)
    opb = alloc_outproj_bufs(C, NPAIR, tmpb=[tmpx, tmpx])
    nload = [0]

    def load_w(seq):
        if seq >= 2 * NPAIR or seq < nload[0]:
            return
        nload[0] = seq + 1
        i = seq % NPAIR
        w = wup[seq % 2]
        P.dma("pool", w[:, :, :], C.d_wup[l, i].rearrange("p (a b) -> p a b", b=256), C.usem[seq % 2], writes=w.r)

    def finish(u):
        half, i, t2, par = u
        ag, au = acc[par]
        P.act(ag[:, :], ag[:, :], AF.Silu, ag.r, ag.r)
        P.tt("pool", mT[:, i, 512 * t2:512 * t2 + 512], ag[:, :], au[:, :], ALU.mult,
             ag.r + au.r, [mT.r[2 * i + t2]])

    load_w(0)
    emit_norm(C, 4 * l + 2)
    cnt = 0
    pending = deque()
    for half in range(2):
        def rhs_fn(kc, ti):
            return mT[:, kc, 512 * ti:512 * ti + 512], [mT.r[2 * kc + ti]]
        op = OutProj(C, opb, lambda dc: C.d_wdn[l, dc], NPAIR, rhs_fn, [2 * half, 2 * half + 1], 4 * l + 3)
        prev = None
        for i in range(NPAIR):
            seq = half * NPAIR + i
            w = wup[seq % 2]
            for t2 in range(2):
                tg = 2 * half + t2
                cols = slice(512 * tg, 512 * tg + 512)
                par = cnt % 2
                pss = []
                for gu in range(2):
                    ps = C.PS[(2 * cnt + gu) % 8]
                    pairs = [(w[:, k, 128 * gu:128 * gu + 128], C.xnT[:, k, cols]) for k in range(8)]
                    P.mm_group(ps[:, :], pairs, w.r + [C.xnT.r[4 * k + tg] for k in range(8)], ps.r)
                    pss.append(ps)
                if t2 == 0:
                    load_w(seq + 1)
                    if i == NPAIR - 2:
                        op.load(0)
                cnt += 1
                for gu in range(2):
                    ci = i + NPAIR * gu
                    pbase = (l * 44 + ci) * 4
                    h = hs[par][gu]
                    a = acc[par][gu]
                    ps = pss[gu]
                    P.copy("act", h[:, 2:514], ps[:, :], ps.r, h.r)
                    P.act(a[:, :], ps[:, :], AF.Identity, ps.r + C.cvp.r, a.r,
                          bias=C.cvp[:, pbase + 3:pbase + 4], scale=C.cvp[:, pbase + 2:pbase + 3])
                for gu in range(2):
                    ci = i + NPAIR * gu
                    h = hs[par][gu]
                    if tg == 0:
                        P.memset("pool", h[:, 0:2], 0.0, h.r)
                    else:
                        P.copy("pool", h[:, 0:2], tails[:, ci, :], [tails.r[ci]], h.r)
                if tg < 3:
                    for gu in range(2):
                        ci = i + NPAIR * gu
                        h = hs[par][gu]
                        P.copy("pool", tails[:, ci, :], h[:, 512:514], h.r, [tails.r[ci]])
                if prev is not None:
                    finish(prev)
                if pending:
                    pending.popleft()()
                for tap in (1, 0):
                    for gu in range(2):
                        ci = i + NPAIR * gu
                        pbase = (l * 44 + ci) * 4
                        h = hs[par][gu]
                        a = acc[par][gu]
                        P.stt(a[:, :], h[:, tap:tap + 512], C.cvp[:, pbase + tap:pbase + tap + 1], a[:, :],
                              ALU.mult, ALU.add, h.r + a.r + C.cvp.r, a.r)
                prev = (half, i, t2, par)
        finish(prev)
        op.run_mm()
        pending = deque(op.final_units())
        pending.popleft()()
        pending.popleft()()
        if half == 1:
            while pending:
                pending.popleft()()
    P.barrier()


class Chain:
    pass


def emit_transposes(C, o_pair, oT, p, banks=(6, 7)):
    P = C.P
    for tb in range(4):
        ps = C.PS[banks[tb % 2]]
        psb = ps.t[:, :].bitcast(BF16)
        for j in range(4):
            tile = 4 * tb + j
            P.op("pe", (lambda o, i: (lambda e: e.transpose(o, i, cbs(C, CB_IDENT))))(
                psb[:, j * 128:(j + 1) * 128], o_pair[:, tile, :]),
                 [o_pair.r[tile // 4]] + C.cb.r, ps.r)
        P.copy("dve", oT[:, p, 512 * tb:512 * tb + 512], psb[:, 0:512], ps.r, [oT.r[4 * p + tb]])


def sb_steps():
    steps = []
    for QG in range(4):
        nk = 4 * QG + 4
        for jb in range(nk - 1, -1, -1):
            r = jb - 4 * QG
            ta = 4 * QG + max(0, r)
            steps.append(dict(QG=QG, jb=jb, ta=ta, tb=4 * QG + 3, diag=(jb if r >= 0 else None),
                              first=(jb == nk - 1), last=(jb == 0)))
    return steps


def emit_sb_attention(C, chains, bg):
    P = C.P
    steps = sb_steps()
    n = len(steps)

    def q0_of(k):
        st = steps[k]
        return (st["ta"] - 4 * st["QG"]) * 128

    def Zm(k):
        st = steps[k]
        q0 = q0_of(k)
        for ch in chains:
            Zb = ch.Zb[k % 2]
            kap, kr = ch.kT(st["jb"])
            qap, qr = ch.qT(st["ta"] * 128, 512 - q0)
            if st["diag"] is None:
                P.mm(Zb[:, q0:512], kap, qap, True, True, kr + qr, Zb.r)
            else:
                P.mm(Zb[:, q0:512], kap, qap, True, False, kr + qr, Zb.r, inc=False)
                P.mm(Zb[:, q0:q0 + 128], cbs(C, CB_IDENT), cbs(C, CB_NEG_STRICT), False, True,
                     C.cb.r, Zb.r, skip_group_check=True)

    def Em(k):
        q0 = q0_of(k)
        for ch in chains:
            E = ch.E[k % 3]
            Zb = ch.Zb[k % 2]
            P.act(E[:, q0:512], Zb[:, q0:512], AF.Exp, Zb.r, E.r)

    def Lm(k):
        q0 = q0_of(k)
        for ch in chains:
            E = ch.E[k % 3]
            Lp = ch.Lp[k % 2]
            P.act(Lp[:, q0:512], E[:, q0:512], AF.Ln, E.r, Lp.r, bias=1.0, scale=1.0)

    def TRIm(k):
        st = steps[k]
        q0 = q0_of(k)
        for ch in chains:
            Lp = ch.Lp[k % 2]
            P.mm(ch.Ab[:, q0:512], cbs(C, CB_TRI_INCL), Lp[:, q0:512], st["first"], False,
                 Lp.r + C.cb.r, ch.Ab.r, skip_group_check=True)

    def Xm(k):
        q0 = q0_of(k)
        for ch in chains:
            P.act(ch.X[:, q0:512], ch.Ab[:, q0:512], AF.Exp, ch.Ab.r, ch.X.r)

    def RESTm(k):
        st = steps[k]
        q0 = q0_of(k)
        if st["last"]:
            return
        for ch in chains:
            Lp = ch.Lp[k % 2]
            P.mm(ch.Ab[:, q0:512], cbs(C, CB_TRI_REST), Lp[:, q0:512], False, False,
                 Lp.r + C.cb.r, ch.Ab.r, skip_group_check=True)

    def aTm(k):
        q0 = q0_of(k)
        for ch in chains:
            aT = ch.aT[k % 2]
            E = ch.E[k % 3]
            P.tt("dve", aT[:, q0:512], E[:, q0:512], ch.X[:, q0:512], ALU.mult, E.r + ch.X.r, aT.r)

    def Om(k):
        st = steps[k]
        for ci_, ch in enumerate(chains):
            aT = ch.aT[k % 2]
            vap, vr = ch.V(st["jb"])
            tiles = list(range(st["ta"], st["tb"] + 1))
            for ti_, t in enumerate(tiles):
                i = t - 4 * st["QG"]
                oc = ch.ocol + i * 64
                P.mm(ch.Ob[:, oc:oc + 64], aT[:, i * 128:(i + 1) * 128], vap,
                     start=(st["first"] and ti_ == 0 and ci_ == 0), stop=False, reads=aT.r + vr,
                     writes=ch.Ob.r, inc=(ti_ == len(tiles) - 1), skip_group_check=True)
        if st["last"]:
            for ch in chains:
                for i in range(4):
                    t = 4 * st["QG"] + i
                    ap, rr = ch.o_dst(t)
                    oc = ch.ocol + i * 64
                    P.copy("dve", ap, ch.Ob[:, oc:oc + 64], ch.Ob.r, rr)

    Zm(0)
    Em(0)
    Lm(0)
    TRIm(0)
    if n > 1:
        Zm(1)
        Em(1)
    if n > 2:
        Zm(2)
    for k in range(n):
        Xm(k)
        RESTm(k)
        if k + 1 < n:
            Lm(k + 1)
            TRIm(k + 1)
        aTm(k)
        if k + 2 < n:
            Em(k + 2)
        Om(k)
        if k + 3 < n:
            Zm(k + 3)
        nb_ = -(-len(bg) // max(1, n - 1 - k)) if bg else 0
        for _ in range(nb_):
            if bg:
                bg.popleft()()
    while bg:
        bg.popleft()()


def emit_sb_layer(C, l, j):
    P = C.P
    A = C.A
    A.reset()
    oT = A.alloc("oT", [8, 2048], BF16, nres=32)
    mark = A.off
    wq = [A.alloc("wq%d" % i, [8, 384], BF16) for i in range(2)]
    qT = [A.alloc("qT%d" % i, [2048], BF16, nres=4) for i in range(2)]
    kT = [A.alloc("kT%d" % i, [2048], BF16, nres=4) for i in range(2)]
    Vv = [A.alloc("V%d" % i, [2048], BF16, nres=4) for i in range(2)]
    o_pair = [A.alloc("op0", [16, 128], BF16, nres=4)] * 2
    chains = []
    for c in range(2):
        ch = Chain()
        ch.E = [A.alloc("E%d%d" % (c, i), [512], F32) for i in range(3)]
        ch.X = A.alloc("X%d" % c, [512], F32)
        ch.Lp = [A.alloc("Lp%d%d" % (c, i), [512], BF16) for i in range(2)]
        ch.aT = [A.alloc("aT%d%d" % (c, i), [512], BF16) for i in range(2)]
        ch.Zb = [C.PS[2 * c], C.PS[2 * c + 1]]
        ch.Ab = C.PS[4 + c]
        ch.Ob = C.PS[6]
        ch.ocol = 256 * c
        chains.append(ch)

    def proj_units(p):
        s_ = p % 2
        w = wq[s_]
        units = []
        units.append(lambda: P.dma("pool", w[:, :, :], C.d_sbqkv[j, p].rearrange("p (a b) -> p a b", b=384),
                                   C.usem[s_], writes=w.r))
        ucnt = [0]

        def qk_units(which, tg):
            ps = C.PS[7]
            ucnt[0] += 1
            cols = slice(512 * tg, 512 * tg + 512)
            pairs = [(w[:, k, 128 * which:128 * which + 128], C.xnT[:, k, cols]) for k in range(8)]
            us = split_mm_group(P, ps[:, :], pairs, w.r + [C.xnT.r[4 * k + tg] for k in range(8)], ps.r, 2)
            if which == 0:
                us.append(lambda: P.ts("dve", qT[s_][:, cols], ps[:, :], 0.125, None, ALU.mult, None, ps.r,
                                       [qT[s_].r[tg]]))
            else:
                us.append(lambda: P.copy("dve", kT[s_][:, cols], ps[:, :], ps.r, [kT[s_].r[tg]]))
            return us

        def v_units(vb):
            ps = C.PS[7]
            ucnt[0] += 1
            us = []
            for jj in range(4):
                tile = 4 * vb + jj
                tcols = slice(128 * tile, 128 * tile + 128)
                pairs = [(C.xnT[:, k, tcols], w[:, k, 256:384]) for k in range(8)]
                us += split_mm_group(P, ps[:, jj * 128:(jj + 1) * 128], pairs,
                                     w.r + [C.xnT.r[4 * k + vb] for k in range(8)], ps.r, 4)
            us.append(lambda: P.copy("dve", Vv[s_][:, 512 * vb:512 * vb + 512], ps[:, :], ps.r, [Vv[s_].r[vb]]))
            return us

        for tg in range(4):
            units += qk_units(1, tg)
            units += v_units(tg)
            units += qk_units(0, tg)
        return units

    u0 = proj_units(0)
    u0[0]()
    emit_norm(C, 4 * l + 0)
    for u in u0[1:]:
        u()
    for p in range(8):
        s_ = p % 2
        for c, ch in enumerate(chains):
            rows = slice(64 * c, 64 * c + 64)
            ch.kT = (lambda rows, s_: (lambda jb: (kT[s_][rows, 128 * jb:128 * jb + 128], [kT[s_].r[jb // 4]])))(rows, s_)
            ch.qT = (lambda rows, s_: (lambda t0, n: (qT[s_][rows, t0:t0 + n], [qT[s_].r[t0 // 512]])))(rows, s_)
            ch.V = (lambda c, s_: (lambda jb: (Vv[s_][:, 128 * jb + 64 * c:128 * jb + 64 * c + 64],
                                               [Vv[s_].r[jb // 4]])))(c, s_)
            ch.o_dst = (lambda c, s_: (lambda t: (o_pair[s_][:, t, 64 * c:64 * c + 64], [o_pair[s_].r[t // 4]])))(c, s_)
        bg = deque(proj_units(p + 1)) if p < 7 else deque()
        emit_sb_attention(C, chains, bg)
        emit_transposes(C, o_pair[s_], oT, p, banks=(7, 7))
    P.barrier()
    A.reset(mark)
    emit_attn_out_proj(C, A, lambda dc: C.d_sbwo[j, dc], 8, oT, 4 * l + 1)
    P.barrier()


def emit_softmax_attention(C, chains, bg, bg_every=3, look=1):
    P = C.P
    n = len(chains[0].steps)

    nbuf = look + 1

    def front(k):
        b = k % nbuf
        for ch in chains:
            st = ch.steps[k]
            q0 = (st["ta"] - 4 * st["QG"]) * 128
            N = (st["tb"] - st["ta"] + 1) * 128
            kap, kr = ch.kT(st["g"], st["jb"])
            qap, qr = ch.qT(st["g"], st["ta"] * 128, N)
            if st.get("negmask") is None:
                P.mm(ch.Zb[b][:, q0:q0 + N], kap, qap, True, True, kr + qr, ch.Zb[b].r)
            else:
                P.mm(ch.Zb[b][:, q0:q0 + N], kap, qap, True, False, kr + qr, ch.Zb[b].r, inc=False)
                P.mm(ch.Zb[b][:, q0:q0 + 128], cbs(C, CB_IDENT), st["negmask"], False, True,
                     C.cb.r, ch.Zb[b].r, skip_group_check=True)
        for ch in chains:
            st = ch.steps[k]
            q0 = (st["ta"] - 4 * st["QG"]) * 128
            N = (st["tb"] - st["ta"] + 1) * 128
            PT = ch.PT[b]
            P.act(PT[:, q0:q0 + N], ch.Zb[b][:, q0:q0 + N], AF.Exp, ch.Zb[b].r, PT.r)
        for ci_, ch in enumerate(chains):
            st = ch.steps[k]
            if st["mask"] is None:
                continue
            q0 = (st["ta"] - 4 * st["QG"]) * 128
            PT = ch.PT[b]
            map_, mr, mn = st["mask"]
            eng = "pool" if (ci_ + k) % 2 == 0 else "dve"
            P.tt(eng, PT[:, q0:q0 + mn], PT[:, q0:q0 + mn], map_, ALU.mult, PT.r + mr, PT.r)

    def back(k):
        b = k % nbuf
        for ch in chains:
            st = ch.steps[k]
            PT = ch.PT[b]
            vap, vr = ch.V(st["g"], st["jb"])
            tiles = list(range(st["ta"], st["tb"] + 1))
            for ti_, t in enumerate(tiles):
                i = t - 4 * st["QG"]
                P.mm(ch.Ob[:, i * 65:(i + 1) * 65], PT[:, i * 128:(i + 1) * 128], vap,
                     start=(st["first"] and ti_ == 0), stop=False, reads=PT.r + vr, writes=ch.Ob.r,
                     inc=(ti_ == len(tiles) - 1), skip_group_check=True)
        for ch in chains:
            st = ch.steps[k]
            if st["last"]:
                for i in range(4):
                    t = 4 * st["QG"] + i
                    rd = ch.rden
                    P.recip(rd[:, 0:1], ch.Ob[:, i * 65 + 64:i * 65 + 65], ch.Ob.r, rd.r)
                    ap, rr = ch.o_dst(t)
                    P.ts("dve", ap, ch.Ob[:, i * 65:i * 65 + 64], rd[:, 0:1], None, ALU.mult, None,
                         ch.Ob.r + rd.r, rr)

    for j_ in range(min(look, n)):
        front(j_)
    for k in range(n):
        if k + look < n:
            front(k + look)
        back(k)
        if bg and k % bg_every == bg_every - 1:
            bg.popleft()()
    while bg:
        bg.popleft()()


def emit_sin_table(C, out_ap, out_res, ang, tmp, tmpi, rows, turns_off):
    P = C.P
    TWO_PI = 2.0 * math.pi
    a = ang[rows, :]
    t = tmp[rows, :]
    ti = tmpi[rows, :]
    P.ts("dve", t, a, 1.0 / TWO_PI, turns_off + 0.5, ALU.mult, ALU.add, ang.r, tmp.r)
    P.copy("dve", ti, t, tmp.r, tmpi.r)
    P.copy("dve", t, ti, tmpi.r, tmp.r)
    P.stt(t, t, -TWO_PI, a, ALU.mult, ALU.add, tmp.r + ang.r, tmp.r)
    if turns_off != 0.0:
        P.ts("dve", t, t, TWO_PI * turns_off, None, ALU.add, None, tmp.r, tmp.r)
    for _ in range(2):
        P.ts("dve", ti.bitcast(F32), t, -math.pi, TWO_PI, ALU.is_lt, ALU.mult, tmp.r, tmpi.r)
        P.tt("dve", t, t, ti.bitcast(F32), ALU.add, tmp.r + tmpi.r, tmp.r)
        P.ts("dve", ti.bitcast(F32), t, math.pi, -TWO_PI, ALU.is_gt, ALU.mult, tmp.r, tmpi.r)
        P.tt("dve", t, t, ti.bitcast(F32), ALU.add, tmp.r + tmpi.r, tmp.r)
    P.ts("dve", t, t, math.pi, -math.pi, ALU.min, ALU.max, tmp.r, tmp.r)
    P.act(out_ap, t, AF.Sin, tmp.r, out_res)


def emit_mla_layer(C, l):
    P = C.P
    A = C.A
    SC = 96.0 ** -0.5
    A.reset()
    R64 = slice(64, 96)
    cqn = A.alloc("cqn", [3, 2048], BF16, nres=12)
    ckvn = A.alloc("ckvn", [2, 2048], BF16, nres=8)
    Ct = A.alloc("Ct", [2048], F32)
    St = A.alloc("St", [2048], F32)
    kr = A.alloc("kr", [2048], BF16, nres=4)
    mark0 = A.off
    win = A.alloc("win", [8, 768], BF16)
    cq_t = [A.alloc("cqt%d" % i, [512], F32) for i in range(3)]
    ang = A.alloc("ang", [2048], F32)
    tmp = A.alloc("rtmp", [2048], F32)
    tmpi = A.alloc("rtmpi", [2048], I32)
    t1 = A.alloc("t1", [512], F32)
    t2 = A.alloc("t2", [512], F32)
    qng = A.alloc("qng", [8], F32)
    P.dma("pool", win[:, :, :], C.d_mlawin.rearrange("p (a b) -> p a b", b=768), C.usem[0], writes=win.r)
    P.dma("sp", qng[:, :], C.d_mlang, C.usem[1], writes=qng.r)
    P.dma("sp", tmpi[0:96, :], C.d_pos.broadcast_to([96, 2048]), C.usem[1], writes=tmpi.r)
    emit_norm(C, 4 * l + 0)
    P.copy("dve", ang[R64, :], tmpi[R64, :], tmpi.r, ang.r)
    P.ts("dve", ang[R64, :], ang[R64, :], C.cf[R64, 0:1], None, ALU.mult, None, ang.r + C.cf.r, ang.r)
    emit_sin_table(C, Ct[R64, :], Ct.r, ang, tmp, tmpi, R64, 0.25)
    emit_sin_table(C, St[R64, :], St.r, ang, tmp, tmpi, R64, 0.0)
    P.ts("dve", St[R64, :], St[R64, :], C.cf[R64, 1:2], None, ALU.mult, None, St.r + C.cf.r, St.r)
    for tg in range(4):
        cols = slice(512 * tg, 512 * tg + 512)
        xr = [C.xnT.r[4 * k + tg] for k in range(8)]
        for (nch, c0, dst, gcol, ssb) in ((3, 0, cqn, 0, 6), (2, 384, ckvn, 3, 7)):
            ss = C.PS[ssb]
            for ch_ in range(nch):
                ps = C.PS[ch_ % 2]
                pairs = [(win[:, k, c0 + 128 * ch_:c0 + 128 * ch_ + 128], C.xnT[:, k, cols]) for k in range(8)]
                P.mm_group(ps[:, :], pairs, win.r + xr, ps.r)
                P.copy("dve", cq_t[ch_][:, :], ps[:, :], ps.r, cq_t[ch_].r)
                sq = C.sq[ch_ % 2]
                P.act(sq[:, :], ps[:, :], AF.Square, ps.r, sq.r)
                P.mm(ss[:, :], cbs(C, CB_ONES), sq[:, :], start=(ch_ == 0), stop=(ch_ == nch - 1),
                     reads=sq.r + C.cb.r, writes=ss.r)
            P.act(C.rsb[:, :], ss[:, :], AF.Sqrt, ss.r, C.rsb.r, bias=C.epsb[:, 0:1], scale=1.0 / (128 * nch))
            rstd = C.rstd[0]
            P.recip(rstd[:, :], C.rsb[:, :], C.rsb.r, rstd.r)
            for ch_ in range(nch):
                P.stt(dst[:, ch_, cols], cq_t[ch_][:, :], qng[:, gcol + ch_:gcol + ch_ + 1], rstd[:, :],
                      ALU.mult, ALU.mult, cq_t[ch_].r + rstd.r + qng.r, [dst.r[4 * ch_ + tg]])
        psA = C.PS[2]
        psB = C.PS[3]
        P.mm_group(psA[0:96, :], [(win[:, k, 576:672], C.xnT[:, k, cols]) for k in range(8)], win.r + xr, psA.r)
        P.mm_group(psB[0:96, :], [(win[:, k, 672:768], C.xnT[:, k, cols]) for k in range(8)], win.r + xr, psB.r)
        P.tt("dve", t1[R64, :], psA[R64, :], Ct[R64, cols], ALU.mult, psA.r + Ct.r, t1.r)
        P.tt("dve", t2[R64, :], psB[R64, :], St[R64, cols], ALU.mult, psB.r + St.r, t2.r)
        P.tt("pool", kr[R64, cols], t1[R64, :], t2[R64, :], ALU.add, t1.r + t2.r, [kr.r[tg]])
    P.barrier()
    A.reset(mark0)
    oT = C.xnT
    wh = [A.alloc("wh%d" % i, [3 * 192 + 2 * 128], BF16) for i in range(2)]
    qf = [A.alloc("qf%d" % i, [2048], BF16, nres=4) for i in range(2)]
    kf = [A.alloc("kf%d" % i, [2048], BF16, nres=4) for i in range(2)]
    Vh = [A.alloc("Vh%d" % i, [16, 65], BF16, nres=4) for i in range(2)]
    o_pair = [A.alloc("op%d" % i, [16, 128], BF16, nres=4) for i in range(2)]
    t1 = A.alloc("t1b", [512], F32)
    t2 = A.alloc("t2b", [512], F32)
    chains = []
    for c in range(2):
        ch = Chain()
        ch.PT = [A.alloc("PT%d%d" % (c, i), [512], BF16) for i in range(2)]
        ch.rden = A.alloc("rden%d" % c, [4], F32)
        ch.Zb = [C.PS[c], C.PS[2 + c]]
        ch.Ob = C.PS[4 + c]
        steps = []
        for QG in ([0, 3] if c == 0 else [1, 2]):
            nk = 4 * QG + 4
            for jb in range(nk):
                r = jb - 4 * QG
                ta = 4 * QG + max(0, r)
                steps.append(dict(QG=QG, jb=jb, ta=ta, tb=4 * QG + 3, first=(jb == 0), last=(jb == nk - 1),
                                  g=0, mask=None, negmask=(cbs(C, CB_NEG_INCL) if r >= 0 else None)))
        ch.steps = steps
        chains.append(ch)
    for s_ in range(2):
        P.memset("pool", Vh[s_][:, :, 64:65], 1.0, Vh[s_].r)

    def head_units(h):
        s_ = h % 2
        w = wh[s_]
        wq_ = w[:, 0:576].rearrange("p (a b) -> p a b", b=192)
        wkv = w[:, 576:832].rearrange("p (a b) -> p a b", b=128)
        units = []
        units.append(lambda: P.dma("pool", w[:, :], C.d_mlawh[h], C.usem[s_], writes=w.r))
        ucnt = [0]

        def q_unit(tg):
            def f():
                cols = slice(512 * tg, 512 * tg + 512)
                psA = C.PS[6]
                psB = C.PS[7]
                cr = [cqn.r[4 * kc + tg] for kc in range(3)]
                P.mm_group(psA[0:96, :], [(wq_[:, kc, 0:96], cqn[:, kc, cols]) for kc in range(3)], w.r + cr, psA.r)
                P.mm_group(psB[0:96, :], [(wq_[:, kc, 96:192], cqn[:, kc, cols]) for kc in range(3)], w.r + cr, psB.r)
                P.ts("dve", qf[s_][0:64, cols], psA[0:64, :], SC, None, ALU.mult, None, psA.r, [qf[s_].r[tg]])
                P.stt(t1[R64, :], psA[R64, :], SC, Ct[R64, cols], ALU.mult, ALU.mult, psA.r + Ct.r, t1.r)
                P.stt(t2[R64, :], psB[R64, :], SC, St[R64, cols], ALU.mult, ALU.mult, psB.r + St.r, t2.r)
                P.tt("pool", qf[s_][R64, cols], t1[R64, :], t2[R64, :], ALU.add, t1.r + t2.r, [qf[s_].r[tg]])
            return f

        def k_unit(tg):
            def f():
                cols = slice(512 * tg, 512 * tg + 512)
                ps = C.PS[6 + tg % 2]
                cr = [ckvn.r[4 * kc + tg] for kc in range(2)]
                P.mm_group(ps[0:64, :], [(wkv[:, kc, 0:64], ckvn[:, kc, cols]) for kc in range(2)], w.r + cr, ps.r)
                P.copy("dve", kf[s_][0:64, cols], ps[0:64, :], ps.r, [kf[s_].r[tg]])
                P.copy("pool", kf[s_][R64, cols], kr[R64, cols], [kr.r[tg]], [kf[s_].r[tg]])
            return f

        def v_unit(vb):
            def f():
                ps = C.PS[6 + vb % 2]
                for jj in range(8):
                    tile = 8 * vb + jj
                    tcols = slice(128 * tile, 128 * tile + 128)
                    pairs = [(ckvn[:, kc, tcols], wkv[:, kc, 64:128]) for kc in range(2)]
                    P.mm_group(ps[:, jj * 64:(jj + 1) * 64], pairs,
                               w.r + [ckvn.r[4 * kc + tile // 4] for kc in range(2)], ps.r)
                P.copy("dve", Vh[s_][:, 8 * vb:8 * vb + 8, 0:64],
                       ps[:, :].rearrange("p (a b) -> p a b", b=64), ps.r, [Vh[s_].r[2 * vb], Vh[s_].r[2 * vb + 1]])
            return f

        for tg in range(4):
            units.append(k_unit(tg))
            units.append(q_unit(tg))
        units.append(v_unit(0))
        units.append(v_unit(1))
        return units

    for u in head_units(0):
        u()
    for h in range(16):
        s_ = h % 2
        ps_ = (h // 2) % 2
        for ch in chains:
            ch.kT = (lambda s_: (lambda g, jb: (kf[s_][0:96, 128 * jb:128 * jb + 128], [kf[s_].r[jb // 4]])))(s_)
            ch.qT = (lambda s_: (lambda g, t0, n: (qf[s_][0:96, t0:t0 + n], [qf[s_].r[t0 // 512]])))(s_)
            ch.V = (lambda s_: (lambda g, jb: (Vh[s_][:, jb, :], [Vh[s_].r[jb // 4]])))(s_)
            ch.o_dst = (lambda hh, ps_: (lambda t: (o_pair[ps_][:, t, 64 * hh:64 * hh + 64],
                                                    [o_pair[ps_].r[t // 4]])))(h % 2, ps_)
        bg = deque(head_units(h + 1)) if h < 15 else deque()
        emit_softmax_attention(C, chains, bg, bg_every=1)
        if h % 2 == 1:
            emit_transposes(C, o_pair[ps_], oT, h // 2)
    P.barrier()
    A.reset(mark0)
    emit_attn_out_proj(C, A, lambda dc: C.d_mlawo[dc], 8, oT, 4 * l + 1)
    P.barrier()


DIL_W = (256, 640, 2048)
DIL_OFF = (0, 256, 896)
DIL_BACK = (1, 4, 15)


def emit_dil_layer(C, l):
    P = C.P
    A = C.A
    A.reset()
    oT = A.alloc("oT", [4, 2048], BF16, nres=16)
    mark = A.off
    valid = A.alloc("valid", [2944], BF16)
    wq = [A.alloc("wq%d" % i, [8, 384], BF16) for i in range(2)]
    qT = [A.alloc("qT%d" % g, [2048], BF16, nres=4) for g in range(3)]
    kT = [A.alloc("kT%d" % g, [2048], BF16, nres=4) for g in range(3)]
    Vv = [A.alloc("V%d" % g, [16, 130], BF16, nres=4) for g in range(3)]
    M = [A.alloc("M%d" % hh, [2944], BF16, nres=3) for hh in range(2)]
    stage = [A.alloc("stage0", [1024], F32)] * 2
    scnt = [0]
    o_pair = A.alloc("op", [16, 128], BF16, nres=4)
    for g in range(3):
        for hh in range(2):
            P.memset("pool", Vv[g][:, :, 65 * hh + 64:65 * hh + 65], 1.0, Vv[g].r)
    chains = []
    for c in range(2):
        ch = Chain()
        ch.PT = [A.alloc("PT%d%d" % (c, i), [512], BF16) for i in range(3)]
        ch.rden = A.alloc("rden%d" % c, [4], F32)
        ch.Zb = [C.PS[c], C.PS[2 + c], C.PS[6 + c]]
        ch.Ob = C.PS[4 + c]
        steps = []
        for QG in range(4):
            lst = []
            for g in range(3):
                for jb in range(max(0, 4 * QG - DIL_BACK[g]), 4 * QG + 4):
                    ta = max(jb, 4 * QG)
                    tb = min(jb + DIL_BACK[g], 4 * QG + 3)
                    if tb < ta:
                        continue
                    x0 = DIL_OFF[g] + (ta - jb) * 128
                    N = (tb - ta + 1) * 128
                    lst.append(dict(QG=QG, jb=jb, ta=ta, tb=tb, first=False, last=False, g=g,
                                    mask=(M[c][:, x0:x0 + N], [M[c].r[g]], N)))
            lst[0]["first"] = True
            lst[-1]["last"] = True
            steps += lst
        ch.steps = steps
        chains.append(ch)
    wissued = [0]

    def issue_w(idx):
        if idx >= 12 or idx < wissued[0]:
            return
        wissued[0] = idx + 1
        w_ = wq[idx % 2]
        P.dma("pool", w_[:, :, :], C.d_dilqkv[idx].rearrange("p (a b) -> p a b", b=384),
              C.usem[idx % 2], writes=w_.r)

    for p in range(4):
        issue_w(3 * p)
        issue_w(3 * p + 1)
        if p == 0:
            P.dma("pool", valid[:, :], C.d_dilvalid, C.wsem[1], writes=valid.r)
            emit_norm(C, 4 * l + 0)
        for hh in range(2):
            for g in range(3):
                for c0 in range(0, DIL_W[g], 1024):
                    W = min(1024, DIL_W[g] - c0)
                    o = DIL_OFF[g] + c0
                    stg = stage[scnt[0] % 2]
                    scnt[0] += 1
                    P.dma("sp", stg[:, 0:W], C.d_dilslab[g * 8 + 2 * p + hh][:, c0:c0 + W], C.wsem[scnt[0] % 2],
                          writes=stg.r)
                    P.act(stg[:, 0:W], stg[:, 0:W], AF.Exp, stg.r, stg.r)
                    P.tt("pool", M[hh][:, o:o + W], stg[:, 0:W], valid[:, o:o + W], ALU.mult,
                         stg.r + valid.r, [M[hh].r[g]])
        for g in range(3):
            issue_w(3 * p + g)
            w = wq[(3 * p + g) % 2]
            u = 0
            for tg in range(4):
                cols = slice(512 * tg, 512 * tg + 512)
                xr = [C.xnT.r[4 * k + tg] for k in range(8)]
                for which in range(2):
                    ps = C.PS[6 + u % 2]
                    u += 1
                    pairs = [(w[:, k, 128 * which:128 * which + 128], C.xnT[:, k, cols]) for k in range(8)]
                    P.mm_group(ps[:, :], pairs, w.r + xr, ps.r)
                    if which == 0:
                        P.ts("dve", qT[g][:, cols], ps[:, :], 0.125, None, ALU.mult, None, ps.r, [qT[g].r[tg]])
                    else:
                        P.copy("dve", kT[g][:, cols], ps[:, :], ps.r, [kT[g].r[tg]])
                ps = C.PS[6 + u % 2]
                u += 1
                for jj in range(4):
                    tile = 4 * tg + jj
                    tcols = slice(128 * tile, 128 * tile + 128)
                    pairs = [(C.xnT[:, k, tcols], w[:, k, 256:384]) for k in range(8)]
                    P.mm_group(ps[:, jj * 128:(jj + 1) * 128], pairs, w.r + xr, ps.r)
                for hh in range(2):
                    src = ps[:, :].rearrange("p (a b) -> p a b", b=128)[:, :, 64 * hh:64 * hh + 64]
                    P.copy("dve", Vv[g][:, 4 * tg:4 * tg + 4, 65 * hh:65 * hh + 64], src,
                           ps.r, [Vv[g].r[tg]])
            issue_w(3 * p + g + 2)
            if g == 0 and p > 0:
                emit_transposes(C, o_pair, oT, p - 1)
        for c, ch in enumerate(chains):
            rows = slice(64 * c, 64 * c + 64)
            ch.kT = (lambda rows: (lambda g, jb: (kT[g][rows, 128 * jb:128 * jb + 128], [kT[g].r[jb // 4]])))(rows)
            ch.qT = (lambda rows: (lambda g, t0, n: (qT[g][rows, t0:t0 + n],
                                                     [qT[g].r[i] for i in range(t0 // 512, (t0 + n - 1) // 512 + 1)])))(rows)
            ch.V = (lambda c: (lambda g, jb: (Vv[g][:, jb, 65 * c:65 * c + 65], [Vv[g].r[jb // 4]])))(c)
            ch.o_dst = (lambda c: (lambda t: (o_pair[:, t, 64 * c:64 * c + 64], [o_pair.r[t // 4]])))(c)
        emit_softmax_attention(C, chains, deque(), look=2)
    emit_transposes(C, o_pair, oT, 3)
    P.barrier()
    A.reset(mark)
    emit_attn_out_proj(C, A, lambda dc: C.d_dilwo[dc], 4, oT, 4 * l + 1)
    P.barrier()


def build_nc(stages):
    nc = bass.Bass("TRN2", target_bir_lowering=False)
    C = Ctx()
    C.nc = nc

    def din(name, shape, dt=F32):
        return nc.dram_tensor(name, list(shape), dt, kind="ExternalInput").ap()

    d_xT = din("xT", [D, S])
    d_cst = din("cst", [128, 1024])
    d_gn = din("gains", [128, 128])
    d_cvp = din("convp", [128, 704])
    C.d_wup = din("wup", [4, NPAIR, 128, 2048])
    C.d_wdn = din("wdn", [4, 8, 128, DFF])
    C.d_sbqkv = din("sbqkv", [2, 8, 128, 3072])
    C.d_sbwo = din("sbwo", [2, 8, 128, 1024])
    C.d_pos = din("pos", [1, S], I32)
    d_cf = din("cf", [128, 8])
    C.d_mlawin = din("mlawin", [128, 8 * 768])
    C.d_mlang = din("mlang", [128, 8])
    C.d_mlawh = din("mlawh", [16, 128, 832])
    C.d_mlawo = din("mlawo", [8, 128, 1024])
    C.d_dilqkv = din("dilqkv", [12, 128, 3072])
    C.d_dilwo = din("dilwo", [8, 128, 512])
    C.d_dilslab = din("dilslab", [24, 128, 2048])
    C.d_dilvalid = din("dilvalid", [128, 2944])
    d_out = nc.dram_tensor("outT", [D, S], F32, kind="ExternalOutput").ap()

    with ExitStack() as es:
        P = Prog(nc, es)
        C.P = P
        C.xT = P.sbuf("xT_sb", [128, 8, S], F32, nres=32)
        C.xnT = P.sbuf("xnT_sb", [128, 8, S], BF16, nres=32)
        C.cb = P.sbuf("cb", [128, 1024], BF16)
        C.gn = P.sbuf("gn", [128, 128], F32)
        C.cvp = P.sbuf("cvp", [128, 704], F32)
        C.sq = [P.sbuf("sq%d" % i, [128, 512], BF16) for i in range(2)]
        C.rsb = P.sbuf("rsb", [128, 512], F32)
        C.rsb2 = [C.rsb, C.rsb]
        C.rstd = [P.sbuf("rstd%d" % i, [128, 512], F32) for i in range(2)]
        C.cf = P.sbuf("cf_sb", [128, 8], F32)
        C.epsb = P.sbuf("epsb", [128, 1], F32)
        C.oneb = P.sbuf("oneb", [128, 1], F32)
        arena = P.sbuf("arena", [128, ARENA_F32], F32)
        C.A = Arena(arena, ARENA_F32)
        C.PS = [P.psum("ps%d" % i, [128, 512], F32) for i in range(8)]
        C.wsem = [P.dsem("wsem%d" % i) for i in range(2)]
        C.usem = [P.dsem("usem%d" % i) for i in range(2)]
        s_in = P.dsem("s_in")
        s_x = [P.dsem("s_x%d" % i) for i in range(2)]
        s_out = P.dsem("s_out")

        P.dma("pool", C.cb[:, :], d_cst, s_in, writes=C.cb.r)
        P.dma("sp", C.gn[:, :], d_gn, s_in, writes=C.gn.r)
        P.dma("sp", C.cvp[:, :], d_cvp, s_in, writes=C.cvp.r)
        P.dma("sp", C.cf[:, :], d_cf, s_in, writes=C.cf.r)
        P.memset("pool", C.epsb[:, :], EPS, C.epsb.r)
        P.memset("pool", C.oneb[:, :], 1.0, C.oneb.r)
        for k in range(8):
            P.dma("sp", C.xT[:, k, :], d_xT[128 * k:128 * k + 128, :], s_x[k % 2],
                  writes=[C.xT.r[4 * k + tg] for tg in range(4)])

        for st in stages:
            kind, l = st
            if kind == "norm":
                emit_norm(C, l)
            elif kind == "ffn":
                emit_ffn(C, l)
            elif kind == "mix":
                if l % 3 == 0:
                    emit_sb_layer(C, l, l // 3)
                elif l % 3 == 1:
                    emit_dil_layer(C, l)
                else:
                    emit_mla_layer(C, l)
        P.barrier()
        fin = Res("fin")
        for k in range(8):
            P.dma("sp", d_out[128 * k:128 * k + 128, :], C.xT[:, k, :], s_out,
                  reads=[C.xT.r[4 * k + tg] for tg in range(4)], writes=[fin])
        P.op("sp", None, reads=[fin])
        P.emit()
        C.stats = dict(ecnt=dict(P.ecnt), n_wait=P.n_wait)
    return nc, C


def lhsT_stream_layout(W):
    K = W.shape[0]
    nk = K // 128
    return np.ascontiguousarray(W.reshape(nk, 128, 8, 128).transpose(2, 1, 0, 3).reshape(8, 128, K))


def prep_shared(inp):
    f = lambda a: np.asarray(a, dtype=np.float32)
    out = {}
    i = np.arange(128)
    cst = np.zeros((128, 1024), np.float32)
    cst[:, 0:128] = np.eye(128)
    cst[:, 128:256] = 1.0
    cst[:, 256:384] = -1.0 * (i[:, None] >= i[None, :])
    cst[:, 384:512] = -1.0 * (i[:, None] < i[None, :])
    cst[:, 512:640] = (i[:, None] < i[None, :])
    cst[:, 640:768] = (i[:, None] <= i[None, :])
    cst[:, 768:896] = NEG_BIG * (i[:, None] >= i[None, :])
    cst[:, 896:1024] = NEG_BIG * (i[:, None] > i[None, :])
    out["cst"] = cst
    ng = f(inp["norm_gains"])
    out["gains"] = np.ascontiguousarray(ng.reshape(16, 8, 128).transpose(2, 0, 1).reshape(128, 128))
    cw = f(inp["ffn_conv_w"])
    cbias = f(inp["ffn_conv_b"])
    cv = np.concatenate([cw, cbias[:, None, :]], axis=1)
    cv = cv.reshape(4, 4, 44, 128).transpose(3, 0, 2, 1)
    out["convp"] = np.ascontiguousarray(cv.reshape(128, 704))
    wu = f(inp["ffn_w_up"])
    g = wu[:, :, :DFF].reshape(4, 8, 128, NPAIR, 128)
    u = wu[:, :, DFF:].reshape(4, 8, 128, NPAIR, 128)
    gu = np.stack([g, u], axis=4)
    out["wup"] = np.ascontiguousarray(gu.transpose(0, 3, 2, 1, 4, 5).reshape(4, NPAIR, 128, 2048))
    wd = f(inp["ffn_w_down"])
    out["wdn"] = np.stack([lhsT_stream_layout(wd[l]) for l in range(4)])
    sq = f(inp["sb_w_qkv"])
    t = sq.reshape(2, 8, 128, 3, 8, 128)
    out["sbqkv"] = np.ascontiguousarray(t.transpose(0, 4, 2, 1, 3, 5).reshape(2, 8, 128, 3072))
    so = f(inp["sb_w_o"])
    out["sbwo"] = np.stack([lhsT_stream_layout(so[jj]) for jj in range(2)])
    cfm = np.zeros((128, 8), np.float32)
    freqs = (np.float32(10000.0) ** (-np.arange(16, dtype=np.float32) / np.float32(16))).astype(np.float32)
    cfm[64:80, 0] = freqs
    cfm[80:96, 0] = freqs
    cfm[64:80, 1] = -1.0
    cfm[80:96, 1] = 1.0
    out["cf"] = cfm
    wi = f(inp["mla_w_in"])[0]
    wi2 = np.concatenate([wi, wi[:, 576:640], wi[:, 656:672], wi[:, 640:656]], axis=1)
    out["mlawin"] = np.ascontiguousarray(wi2.reshape(8, 128, 768).transpose(1, 0, 2).reshape(128, 8 * 768))
    ng_ = np.zeros((128, 8), np.float32)
    ng_[:, 0:3] = f(inp["mla_q_norm"])[0].reshape(3, 128).T
    ng_[:, 3:5] = f(inp["mla_kv_norm"])[0].reshape(2, 128).T
    out["mlang"] = ng_
    wqb = f(inp["mla_w_qb"])[0]
    wkvb = f(inp["mla_w_kvb"])[0]
    whs = []
    for h in range(16):
        a = wqb[:, 96 * h:96 * h + 96]
        b = np.concatenate([wqb[:, 96 * h:96 * h + 64], wqb[:, 96 * h + 80:96 * h + 96],
                            wqb[:, 96 * h + 64:96 * h + 80]], axis=1)
        q2 = np.concatenate([a, b], axis=1).reshape(3, 128, 192).transpose(1, 0, 2).reshape(128, 576)
        kv = wkvb[:, 128 * h:128 * h + 128].reshape(2, 128, 128).transpose(1, 0, 2).reshape(128, 256)
        whs.append(np.concatenate([q2, kv], axis=1))
    out["mlawh"] = np.ascontiguousarray(np.stack(whs))
    out["mlawo"] = lhsT_stream_layout(f(inp["mla_w_o"])[0])
    dq = f(inp["dil_w_qkv"])[0]
    t = dq.reshape(8, 128, 3, 3, 4, 128)
    out["dilqkv"] = np.ascontiguousarray(t.transpose(4, 3, 1, 0, 2, 5).reshape(12, 128, 3072))
    out["dilwo"] = lhsT_stream_layout(f(inp["dil_w_o"])[0])
    rb = f(inp["rel_bias"])
    jk = np.arange(128)[:, None]
    xx = np.arange(2048)[None, :]
    delta = xx - jk
    dpos = np.maximum(delta, 0)
    dflt = np.maximum(dpos.astype(np.float32), np.float32(1.0))
    large = 16 + (np.log(dflt / np.float32(16)) / np.float32(math.log(2048 / 16)) * np.float32(16)).astype(np.int32)
    large = np.minimum(large, 31)
    bucket = np.where(dpos < 16, dpos, large)
    slab = np.zeros((24, 128, 2048), np.float32)
    for gh in range(24):
        slab[gh] = rb[bucket, gh]
    out["dilslab"] = slab
    valid = np.zeros((128, 2944), np.float32)
    for g, (W, r) in enumerate(((256, 1), (640, 4), (2048, 16))):
        dl = delta[:, :W]
        valid[:, DIL_OFF[g]:DIL_OFF[g] + W] = (dl >= 0) & (dl % r == 0) & (dl <= 128 * r)
    out["dilvalid"] = valid
    return out


ALL_STAGES = [("mix", 0), ("ffn", 0), ("mix", 1), ("ffn", 1), ("mix", 2), ("ffn", 2), ("mix", 3), ("ffn", 3)]
_CACHE = {}


def run(inputs, stages, n_cores=8, trace=False):
    key = tuple(stages)
    if key not in _CACHE:
        _CACHE[key] = build_nc(stages)
    nc, C = _CACHE[key]
    shared = prep_shared(inputs)
    x = np.asarray(inputs["x"], dtype=np.float32)
    in_maps = []
    for b in range(n_cores):
        m = dict(shared)
        m["xT"] = np.ascontiguousarray(x[b].T)
        m["pos"] = np.ascontiguousarray(np.asarray(inputs["positions"])[b][None, :].astype(np.int32))
        in_maps.append(m)
    res = run_bass_kernel_spmd(nc, in_maps, core_ids=list(range(n_cores)), trace=trace)
    out = np.stack([np.ascontiguousarray(r["outT"].T) for r in res.results])
    return out, res


def kernel(**inputs):
    out, _ = run(inputs, ALL_STAGES, 8)
    return out.astype(np.float32)
```

```python
import math
from collections import deque
from contextlib import ExitStack

import numpy as np
import concourse.bass as bass
import concourse.mybir as mybir
from concourse.bass_utils import run_bass_kernel_spmd

F32 = mybir.dt.float32
BF16 = mybir.dt.bfloat16
I32 = mybir.dt.int32
AF = mybir.ActivationFunctionType
ALU = mybir.AluOpType

S = 2048
D = 1024
DFF = 2816
NPAIR = 22
EPS = 1e-6
ENGS = ("pe", "act", "dve", "pool", "sp")
import os as _os
SAME_ENGINE_SYNC = _os.environ.get("NOSES", "") == ""


class Res:
    __slots__ = ("name", "w", "rs", "excl")

    def __init__(self, name=""):
        self.name = name
        self.w = None
        self.rs = {}
        self.excl = False


class Buf:
    def __init__(self, t, nres=1, name=""):
        self.t = t
        self.r = [Res("%s.%d" % (name, i)) for i in range(nres)]

    def __getitem__(self, idx):
        return self.t[idx]


class Prog:
    def __init__(self, nc, es, same_engine_sync=SAME_ENGINE_SYNC):
        self.nc = nc
        self.es = es
        self.ops = {e: [] for e in ENGS}
        self.ecnt = {e: 0 for e in ENGS}
        self.esem = {}
        for e in ("pe", "act", "dve", "pool"):
            self.esem[e] = es.enter_context(nc.semaphore("es_" + e))
        self.known = {e: {} for e in ENGS}
        self.dcnt = {}
        self.same_engine_sync = same_engine_sync
        self.n_wait = 0

    def sbuf(self, name, shape, dtype, nres=1):
        t = self.es.enter_context(self.nc.sbuf_tensor(name, list(shape), dtype))
        return Buf(t, nres, name)

    def psum(self, name, shape, dtype, nres=1):
        t = self.es.enter_context(self.nc.psum_tensor(name, list(shape), dtype))
        b = Buf(t, nres, name)
        for r in b.r:
            r.excl = True
        return b

    def dsem(self, name):
        s = self.es.enter_context(self.nc.semaphore(name))
        self.dcnt[s] = 0
        return s

    def op(self, eng, fn, reads=(), writes=(), inc=True, dsem=None):
        waits = {}
        kn = self.known[eng]
        own = self.esem.get(eng)
        if any(r.excl for r in reads):
            writes = list(writes) + [r for r in reads if r.excl and r not in writes]
            reads = [r for r in reads if not r.excl]

        def need(m):
            if m is None:
                return
            sem, val = m
            if sem in self.dcnt:
                val = self.dcnt[sem]
            elif sem is own:
                if eng == "pe" or not self.same_engine_sync:
                    return
            if kn.get(sem, 0) >= val:
                return
            if waits.get(sem, 0) < val:
                waits[sem] = val

        for r in reads:
            need(r.w)
        for w in writes:
            need(w.w)
            for m in w.rs.items():
                need(m)
        for sem, val in waits.items():
            kn[sem] = val
        if dsem is not None:
            self.dcnt[dsem] += 16
            marker = (dsem, self.dcnt[dsem])
            incspec = (dsem, 16)
        elif eng == "sp":
            marker = None
            incspec = None
        elif inc:
            self.ecnt[eng] += 1
            marker = (own, self.ecnt[eng])
            incspec = (own, 1)
        else:
            marker = (own, self.ecnt[eng] + 1)
            incspec = None
        if marker is not None:
            for r in reads:
                if r.rs.get(marker[0], 0) < marker[1]:
                    r.rs[marker[0]] = marker[1]
            for w in writes:
                w.w = marker
                w.rs = {}
        self.n_wait += len(waits)
        self.ops[eng].append((list(waits.items()), fn, incspec))

    def barrier(self):
        targets = [(self.esem[e], self.ecnt[e]) for e in self.esem if self.ecnt[e] > 0]
        targets += [(s, c) for s, c in self.dcnt.items() if c > 0]
        for eng in ENGS:
            waits = []
            for sem, val in targets:
                if sem is self.esem.get(eng):
                    continue
                if self.known[eng].get(sem, 0) >= val:
                    continue
                self.known[eng][sem] = val
                waits.append((sem, val))
            if waits:
                self.ops[eng].append((waits, None, None))

    def emit(self):
        nc = self.nc
        with nc.Block() as block:
            def run(engname):
                def body(e):
                    for waits, fn, incspec in self.ops[engname]:
                        for sem, val in waits:
                            e.wait_ge(sem, val)
                        if fn is None:
                            continue
                        ins = fn(e)
                        if incspec is not None:
                            ins.then_inc(incspec[0], incspec[1])
                return body

            block.tensor(run("pe"))
            block.scalar(run("act"))
            block.vector(run("dve"))
            block.gpsimd(run("pool"))
            block.sync(run("sp"))

    def dma(self, eng, out, in_, dsem, reads=(), writes=()):
        if eng == "pool":
            self.op(eng, lambda e: e.dma_start(out=out, in_=in_, max_dma_last_dim=4096), reads, writes, dsem=dsem)
        else:
            self.op(eng, lambda e: e.dma_start(out=out, in_=in_), reads, writes, dsem=dsem)

    def mm(self, out, lhsT, rhs, start, stop, reads, writes, inc=True, **kw):
        self.op("pe", lambda e: e.matmul(out, lhsT, rhs, start=start, stop=stop, **kw),
                reads, writes, inc=inc)

    def mm_group(self, out, pairs, reads, writes, **kw):
        n = len(pairs)
        for i, (l, r) in enumerate(pairs):
            self.mm(out, l, r, start=(i == 0), stop=(i == n - 1),
                    reads=reads, writes=writes, inc=(i == n - 1), **kw)

    def act(self, out, in_, func, reads, writes, **kw):
        self.op("act", lambda e: e.activation(out=out, in_=in_, func=func, **kw), reads, writes)

    def tt(self, eng, out, in0, in1, op, reads, writes):
        self.op(eng, lambda e: e.tensor_tensor(out=out, in0=in0, in1=in1, op=op), reads, writes)

    def ts(self, eng, out, in0, s1, s2, op0, op1, reads, writes):
        if s2 is None:
            self.op(eng, lambda e: e.tensor_scalar(out=out, in0=in0, scalar1=s1, scalar2=None, op0=op0),
                    reads, writes)
        else:
            self.op(eng, lambda e: e.tensor_scalar(out=out, in0=in0, scalar1=s1, scalar2=s2,
                                                   op0=op0, op1=op1), reads, writes)

    def stt(self, out, in0, scalar, in1, op0, op1, reads, writes):
        self.op("dve", lambda e: e.scalar_tensor_tensor(out=out, in0=in0, scalar=scalar, in1=in1,
                                                        op0=op0, op1=op1), reads, writes)

    def copy(self, eng, out, in_, reads, writes):
        if eng == "act":
            self.act(out, in_, AF.Copy, reads, writes)
        else:
            self.op(eng, lambda e: e.tensor_copy(out=out, in_=in_), reads, writes)

    def recip(self, out, in_, reads, writes):
        self.op("dve", lambda e: e.reciprocal(out=out, in_=in_), reads, writes)

    def memset(self, eng, out, val, writes):
        self.op(eng, lambda e: e.memset(out, val), (), writes)


def split_mm_group(P, out, pairs, reads, writes, chunk=2):
    units = []
    n = len(pairs)
    for c0 in range(0, n, chunk):
        def f(c0=c0):
            for i in range(c0, min(n, c0 + chunk)):
                P.mm(out, pairs[i][0], pairs[i][1], start=(i == 0), stop=(i == n - 1),
                     reads=reads, writes=writes, inc=(i == n - 1))
        units.append(f)
    return units


class Arena:
    def __init__(self, buf, nf32):
        self.buf = buf
        self.n = nf32
        self.off = 0

    def reset(self, to=0):
        self.off = to

    def alloc(self, name, shape, dtype, nres=1):
        n = 1
        for s_ in shape:
            n *= s_
        esz = 4 if dtype in (F32, I32) else 2
        nf = (n * esz + 3) // 4
        nf = (nf + 3) // 4 * 4
        assert self.off + nf <= self.n, "arena overflow %s: %d + %d > %d" % (name, self.off, nf, self.n)
        ap = self.buf.t[:, self.off:self.off + nf]
        self.off += nf
        if dtype != F32:
            ap = ap.bitcast(dtype)
        ap = ap[:, 0:n]
        if len(shape) == 2:
            ap = ap.rearrange("p (a b) -> p a b", b=shape[1])
        elif len(shape) == 3:
            ap = ap.rearrange("p (a b c) -> p a b c", b=shape[1], c=shape[2])
        return Buf(ap, nres, name)


CB_IDENT, CB_ONES, CB_TRI_INCL, CB_TRI_REST, CB_M_STRICT, CB_M_INCL, CB_NEG_STRICT, CB_NEG_INCL = range(8)
NEG_BIG = -30000.0
ARENA_F32 = 25216


class Ctx:
    pass


def cbs(C, i):
    return C.cb[:, 128 * i:128 * (i + 1)]


def emit_norm(C, gidx):
    P = C.P
    for tg in range(4):
        cols = slice(512 * tg, 512 * tg + 512)
        ss = C.PS[6 + tg % 2]
        for k in range(8):
            sq = C.sq[k % 2]
            P.act(sq[:, :], C.xT[:, k, cols], AF.Square, [C.xT.r[4 * k + tg]], sq.r)
            P.mm(ss[:, :], cbs(C, CB_ONES), sq[:, :], start=(k == 0), stop=(k == 7),
                 reads=sq.r + C.cb.r, writes=ss.r)
        rsb = C.rsb
        P.act(rsb[:, :], ss[:, :], AF.Sqrt, ss.r, rsb.r, bias=C.epsb[:, 0:1], scale=1.0 / D)
        rstd = C.rstd[tg % 2]
        P.recip(rstd[:, :], rsb[:, :], rsb.r, rstd.r)
        for k in range(8):
            P.stt(C.xnT[:, k, cols], C.xT[:, k, cols], C.gn[:, gidx * 8 + k:gidx * 8 + k + 1], rstd[:, :],
                  ALU.mult, ALU.mult, [C.xT.r[4 * k + tg]] + rstd.r + C.gn.r, [C.xnT.r[4 * k + tg]])


class OutProj:
    def __init__(self, C, bufs, wsrc, nk, rhs_fn, tgs, gidx):
        self.C, self.bufs, self.wsrc, self.nk, self.rhs_fn, self.tgs, self.gidx = C, bufs, wsrc, nk, rhs_fn, tgs, gidx
        self.loaded = set()

    def load(self, dc):
        if dc in self.loaded or dc >= 8:
            return
        self.loaded.add(dc)
        C = self.C
        w = self.bufs[0][dc % 2]
        C.P.dma("pool", w[:, :], self.wsrc(dc), C.wsem[dc % 2], writes=w.r)

    def run_mm(self, fofs=0, ssb=2, bg=None):
        C, nk, rhs_fn, tgs, gidx = self.C, self.nk, self.rhs_fn, self.tgs, self.gidx
        P = C.P
        wbuf, fT, tmpb = self.bufs
        self.fofs = fofs
        self.ssb = ssb
        self.load(0)
        for dc in range(8):
            w = wbuf[dc % 2]
            for ti, tg in enumerate(tgs):
                ps = C.PS[ti]
                pairs = []
                reads = list(w.r)
                for kc in range(nk):
                    ap, rr = rhs_fn(kc, ti)
                    pairs.append((w[:, kc * 128:(kc + 1) * 128], ap))
                    reads += rr
                P.mm_group(ps[:, :], pairs, reads, ps.r)
                if ti == 0:
                    self.load(dc + 1)
                fo = fofs + ti * 512
                P.copy("dve", fT[:, dc, fo:fo + 512], ps[:, :], ps.r, [fT.r[(fofs // 512 + ti) * 8 + dc]])
                sq = C.sq[ti]
                P.act(sq[:, :], ps[:, :], AF.Square, ps.r, sq.r)
                ss = C.PS[ssb + ti]
                P.mm(ss[:, :], cbs(C, CB_ONES), sq[:, :], start=(dc == 0), stop=(dc == 7),
                     reads=sq.r + C.cb.r, writes=ss.r)
                if bg:
                    bg.popleft()()
                    if dc >= 1 and bg:
                        bg.popleft()()

    def final_units(self, rstd_bufs=None):
        C, tgs, gidx = self.C, self.tgs, self.gidx
        P = C.P
        wbuf, fT, tmpb = self.bufs
        fofs, ssb = self.fofs, self.ssb
        rstds = rstd_bufs or C.rstd
        units = []

        def rs(ti):
            def f():
                ss = C.PS[ssb + ti]
                rsb = C.rsb2[ti]
                P.act(rsb[:, :], ss[:, :], AF.Sqrt, ss.r, rsb.r, bias=C.epsb[:, 0:1], scale=1.0 / D)
                P.recip(rstds[ti][:, :], rsb[:, :], rsb.r, rstds[ti].r)
            return f

        def fin(dc, ti, tg):
            def f():
                cols = slice(512 * tg, 512 * tg + 512)
                rstd = rstds[ti]
                tmp = tmpb[ti]
                fo = fofs + ti * 512
                fr = [fT.r[(fofs // 512 + ti) * 8 + dc]]
                P.tt("pool", tmp[:, :], fT[:, dc, fo:fo + 512], rstd[:, :], ALU.mult, fr + rstd.r, tmp.r)
                P.stt(C.xT[:, dc, cols], tmp[:, :], C.gn[:, gidx * 8 + dc:gidx * 8 + dc + 1], C.xT[:, dc, cols],
                      ALU.mult, ALU.add, [C.xT.r[4 * dc + tg]] + tmp.r + C.gn.r, [C.xT.r[4 * dc + tg]])
            return f

        for ti, tg in enumerate(tgs):
            units.append(rs(ti))
        for dc in range(8):
            for ti, tg in enumerate(tgs):
                units.append(fin(dc, ti, tg))
        return units

    def run(self):
        self.run_mm()
        for u in self.final_units():
            u()


def emit_attn_out_proj(C, A, wsrc, nk, oT, gidx):
    wbuf = [A.alloc("wo%d" % i, [nk * 128], BF16) for i in range(2)]
    fT = A.alloc("fT", [8, 2048], BF16, nres=32)
    tmpb = [A.alloc("tmp%d" % i, [512], F32) for i in range(2)]
    rstd2 = [A.alloc("rstdb%d" % i, [512], F32) for i in range(2)]
    ops = []
    for half in range(2):
        def rhs_fn(kc, ti, half=half):
            tg = 2 * half + ti
            return oT[:, kc, 512 * tg:512 * tg + 512], [oT.r[4 * kc + tg]]
        ops.append(OutProj(C, (wbuf, fT, tmpb), wsrc, nk, rhs_fn, [2 * half, 2 * half + 1], gidx))
    ops[0].run_mm(fofs=0, ssb=2)
    f0 = deque(ops[0].final_units())
    ops[1].run_mm(fofs=1024, ssb=6, bg=f0)
    while f0:
        f0.popleft()()
    for u in ops[1].final_units(rstd_bufs=rstd2):
        u()


def emit_out_proj(C, bufs, wsrc, nk, rhs_fn, tgs, gidx):
    OutProj(C, bufs, wsrc, nk, rhs_fn, tgs, gidx).run()


def alloc_outproj_bufs(C, nk, tmpb=None):
    A = C.A
    wbuf = [A.alloc("wo%d" % i, [nk * 128], BF16) for i in range(2)]
    fT = A.alloc("fT", [8, 1024], BF16, nres=16)
    if tmpb is None:
        tmpb = [A.alloc("tmp%d" % i, [512], F32) for i in range(2)]
    return wbuf, fT, tmpb


def emit_ffn(C, l):
    P = C.P
    A = C.A
    A.reset()
    mT = A.alloc("mT", [NPAIR, 1024], BF16, nres=NPAIR * 2)
    wup = [A.alloc("wup%d" % i, [8, 256], BF16) for i in range(2)]
    hs = [[A.alloc("hs%d%d" % (i, j), [514], F32) for j in range(2)] for i in range(2)]
    acc = [[A.alloc("acc%d%d" % (i, j), [512], F32) for j in range(2)] for i in range(2)]
    tails = A.alloc("tails", [44, 2], F32, nres=44)
    tmpx = A.alloc("tmpx", [512], F32)
    opb = alloc_outproj_bufs(C, NPAIR, tmpb=[tmpx, tmpx])
    nload = [0]

    def load_w(seq):
        if seq >= 2 * NPAIR or seq < nload[0]:
            return
        nload[0] = seq + 1
        i = seq % NPAIR
        w = wup[seq % 2]
        P.dma("pool", w[:, :, :], C.d_wup[l, i].rearrange("p (a b) -> p a b", b=256), C.usem[seq % 2], writes=w.r)

    def finish(u):
        half, i, t2, par = u
        ag, au = acc[par]
        P.act(ag[:, :], ag[:, :], AF.Silu, ag.r, ag.r)
        P.tt("pool", mT[:, i, 512 * t2:512 * t2 + 512], ag[:, :], au[:, :], ALU.mult,
             ag.r + au.r, [mT.r[2 * i + t2]])

    load_w(0)
    emit_norm(C, 4 * l + 2)
    cnt = 0
    pending = deque()
    for half in range(2):
        def rhs_fn(kc, ti):
            return mT[:, kc, 512 * ti:512 * ti + 512], [mT.r[2 * kc + ti]]
        op = OutProj(C, opb, lambda dc: C.d_wdn[l, dc], NPAIR, rhs_fn, [2 * half, 2 * half + 1], 4 * l + 3)
        prev = None
        for i in range(NPAIR):
            seq = half * NPAIR + i
            w = wup[seq % 2]
            for t2 in range(2):
                tg = 2 * half + t2
                cols = slice(512 * tg, 512 * tg + 512)
                par = cnt % 2
                pss = []
                for gu in range(2):
                    ps = C.PS[(2 * cnt + gu) % 8]
                    pairs = [(w[:, k, 128 * gu:128 * gu + 128], C.xnT[:, k, cols]) for k in range(8)]
                    P.mm_group(ps[:, :], pairs, w.r + [C.xnT.r[4 * k + tg] for k in range(8)], ps.r)
                    pss.append(ps)
                if t2 == 0:
                    load_w(seq + 1)
                    if i == NPAIR - 2:
                        op.load(0)
                cnt += 1
                for gu in range(2):
                    ci = i + NPAIR * gu
                    pbase = (l * 44 + ci) * 4
                    h = hs[par][gu]
                    a = acc[par][gu]
                    ps = pss[gu]
                    P.copy("act", h[:, 2:514], ps[:, :], ps.r, h.r)
                    P.act(a[:, :], ps[:, :], AF.Identity, ps.r + C.cvp.r, a.r,
                          bias=C.cvp[:, pbase + 3:pbase + 4], scale=C.cvp[:, pbase + 2:pbase + 3])
                for gu in range(2):
                    ci = i + NPAIR * gu
                    h = hs[par][gu]
                    if tg == 0:
                        P.memset("pool", h[:, 0:2], 0.0, h.r)
                    else:
                        P.copy("pool", h[:, 0:2], tails[:, ci, :], [tails.r[ci]], h.r)
                if tg < 3:
                    for gu in range(2):
                        ci = i + NPAIR * gu
                        h = hs[par][gu]
                        P.copy("pool", tails[:, ci, :], h[:, 512:514], h.r, [tails.r[ci]])
                if prev is not None:
                    finish(prev)
                if pending:
                    pending.popleft()()
                for tap in (1, 0):
                    for gu in range(2):
                        ci = i + NPAIR * gu
                        pbase = (l * 44 + ci) * 4
                        h = hs[par][gu]
                        a = acc[par][gu]
                        P.stt(a[:, :], h[:, tap:tap + 512], C.cvp[:, pbase + tap:pbase + tap + 1], a[:, :],
                              ALU.mult, ALU.add, h.r + a.r + C.cvp.r, a.r)
                prev = (half, i, t2, par)
        finish(prev)
        op.run_mm()
        pending = deque(op.final_units())
        pending.popleft()()
        pending.popleft()()
        if half == 1:
            while pending:
                pending.popleft()()
    P.barrier()


class Chain:
    pass


def emit_transposes(C, o_pair, oT, p, banks=(6, 7)):
    P = C.P
    for tb in range(4):
        ps = C.PS[banks[tb % 2]]
        psb = ps.t[:, :].bitcast(BF16)
        for j in range(4):
            tile = 4 * tb + j
            P.op("pe", (lambda o, i: (lambda e: e.transpose(o, i, cbs(C, CB_IDENT))))(
                psb[:, j * 128:(j + 1) * 128], o_pair[:, tile, :]),
                 [o_pair.r[tile // 4]] + C.cb.r, ps.r)
        P.copy("dve", oT[:, p, 512 * tb:512 * tb + 512], psb[:, 0:512], ps.r, [oT.r[4 * p + tb]])


def sb_steps():
    steps = []
    for QG in range(4):
        nk = 4 * QG + 4
        for jb in range(nk - 1, -1, -1):
            r = jb - 4 * QG
            ta = 4 * QG + max(0, r)
            steps.append(dict(QG=QG, jb=jb, ta=ta, tb=4 * QG + 3, diag=(jb if r >= 0 else None),
                              first=(jb == nk - 1), last=(jb == 0)))
    return steps


def emit_sb_attention(C, chains, bg):
    P = C.P
    steps = sb_steps()
    n = len(steps)

    def q0_of(k):
        st = steps[k]
        return (st["ta"] - 4 * st["QG"]) * 128

    def Zm(k):
        st = steps[k]
        q0 = q0_of(k)
        for ch in chains:
            Zb = ch.Zb[k % 2]
            kap, kr = ch.kT(st["jb"])
            qap, qr = ch.qT(st["ta"] * 128, 512 - q0)
            if st["diag"] is None:
                P.mm(Zb[:, q0:512], kap, qap, True, True, kr + qr, Zb.r)
            else:
                P.mm(Zb[:, q0:512], kap, qap, True, False, kr + qr, Zb.r, inc=False)
                P.mm(Zb[:, q0:q0 + 128], cbs(C, CB_IDENT), cbs(C, CB_NEG_STRICT), False, True,
                     C.cb.r, Zb.r, skip_group_check=True)

    def Em(k):
        q0 = q0_of(k)
        for ch in chains:
            E = ch.E[k % 3]
            Zb = ch.Zb[k % 2]
            P.act(E[:, q0:512], Zb[:, q0:512], AF.Exp, Zb.r, E.r)

    def Lm(k):
        q0 = q0_of(k)
        for ch in chains:
            E = ch.E[k % 3]
            Lp = ch.Lp[k % 2]
            P.act(Lp[:, q0:512], E[:, q0:512], AF.Ln, E.r, Lp.r, bias=1.0, scale=1.0)

    def TRIm(k):
        st = steps[k]
        q0 = q0_of(k)
        for ch in chains:
            Lp = ch.Lp[k % 2]
            P.mm(ch.Ab[:, q0:512], cbs(C, CB_TRI_INCL), Lp[:, q0:512], st["first"], False,
                 Lp.r + C.cb.r, ch.Ab.r, skip_group_check=True)

    def Xm(k):
        q0 = q0_of(k)
        for ch in chains:
            P.act(ch.X[:, q0:512], ch.Ab[:, q0:512], AF.Exp, ch.Ab.r, ch.X.r)

    def RESTm(k):
        st = steps[k]
        q0 = q0_of(k)
        if st["last"]:
            return
        for ch in chains:
            Lp = ch.Lp[k % 2]
            P.mm(ch.Ab[:, q0:512], cbs(C, CB_TRI_REST), Lp[:, q0:512], False, False,
                 Lp.r + C.cb.r, ch.Ab.r, skip_group_check=True)

    def aTm(k):
        q0 = q0_of(k)
        for ch in chains:
            aT = ch.aT[k % 2]
            E = ch.E[k % 3]
            P.tt("dve", aT[:, q0:512], E[:, q0:512], ch.X[:, q0:512], ALU.mult, E.r + ch.X.r, aT.r)

    def Om(k):
        st = steps[k]
        for ci_, ch in enumerate(chains):
            aT = ch.aT[k % 2]
            vap, vr = ch.V(st["jb"])
            tiles = list(range(st["ta"], st["tb"] + 1))
            for ti_, t in enumerate(tiles):
                i = t - 4 * st["QG"]
                oc = ch.ocol + i * 64
                P.mm(ch.Ob[:, oc:oc + 64], aT[:, i * 128:(i + 1) * 128], vap,
                     start=(st["first"] and ti_ == 0 and ci_ == 0), stop=False, reads=aT.r + vr,
                     writes=ch.Ob.r, inc=(ti_ == len(tiles) - 1), skip_group_check=True)
        if st["last"]:
            for ch in chains:
                ap, rr = ch.o_dst4(st["QG"])
                src = ch.Ob[:, ch.ocol:ch.ocol + 256].rearrange("p (a b) -> p a b", b=64)
                P.copy("dve", ap, src, ch.Ob.r, rr)

    Zm(0)
    Em(0)
    Lm(0)
    TRIm(0)
    if n > 1:
        Zm(1)
        Em(1)
    if n > 2:
        Zm(2)
    for k in range(n):
        Xm(k)
        RESTm(k)
        if k + 1 < n:
            Lm(k + 1)
            TRIm(k + 1)
        aTm(k)
        if k + 2 < n:
            Em(k + 2)
        Om(k)
        if k + 3 < n:
            Zm(k + 3)
        nb_ = -(-len(bg) // max(1, n - 1 - k)) if bg else 0
        for _ in range(nb_):
            if bg:
                bg.popleft()()
    while bg:
        bg.popleft()()


def emit_sb_layer(C, l, j):
    P = C.P
    A = C.A
    A.reset()
    oT = A.alloc("oT", [8, 2048], BF16, nres=32)
    mark = A.off
    wq = [A.alloc("wq%d" % i, [8, 384], BF16) for i in range(2)]
    qT = [A.alloc("qT%d" % i, [2048], BF16, nres=4) for i in range(2)]
    kT = [A.alloc("kT%d" % i, [2048], BF16, nres=4) for i in range(2)]
    Vv = [A.alloc("V%d" % i, [2048], BF16, nres=4) for i in range(2)]
    o_pair = [A.alloc("op0", [16, 128], BF16, nres=4)] * 2
    chains = []
    for c in range(2):
        ch = Chain()
        ch.E = [A.alloc("E%d%d" % (c, i), [512], F32) for i in range(3)]
        ch.X = A.alloc("X%d" % c, [512], F32)
        ch.Lp = [A.alloc("Lp%d%d" % (c, i), [512], BF16) for i in range(2)]
        ch.aT = [A.alloc("aT%d%d" % (c, i), [512], BF16) for i in range(2)]
        ch.Zb = [C.PS[2 * c], C.PS[2 * c + 1]]
        ch.Ab = C.PS[4 + c]
        ch.Ob = C.PS[6]
        ch.ocol = 256 * c
        chains.append(ch)

    def proj_units(p):
        s_ = p % 2
        w = wq[s_]
        units = []
        units.append(lambda: P.dma("pool", w[:, :, :], C.d_sbqkv[j, p].rearrange("p (a b) -> p a b", b=384),
                                   C.usem[s_], writes=w.r))
        ucnt = [0]

        def qk_units(which, tg):
            ps = C.PS[7]
            ucnt[0] += 1
            cols = slice(512 * tg, 512 * tg + 512)
            pairs = [(w[:, k, 128 * which:128 * which + 128], C.xnT[:, k, cols]) for k in range(8)]
            us = split_mm_group(P, ps[:, :], pairs, w.r + [C.xnT.r[4 * k + tg] for k in range(8)], ps.r, 2)
            if which == 0:
                us.append(lambda: P.ts("dve", qT[s_][:, cols], ps[:, :], 0.125, None, ALU.mult, None, ps.r,
                                       [qT[s_].r[tg]]))
            else:
                us.append(lambda: P.copy("dve", kT[s_][:, cols], ps[:, :], ps.r, [kT[s_].r[tg]]))
            return us

        def v_units(vb):
            ps = C.PS[7]
            ucnt[0] += 1
            us = []
            for jj in range(4):
                tile = 4 * vb + jj
                tcols = slice(128 * tile, 128 * tile + 128)
                pairs = [(C.xnT[:, k, tcols], w[:, k, 256:384]) for k in range(8)]
                us += split_mm_group(P, ps[:, jj * 128:(jj + 1) * 128], pairs,
                                     w.r + [C.xnT.r[4 * k + vb] for k in range(8)], ps.r, 4)
            us.append(lambda: P.copy("dve", Vv[s_][:, 512 * vb:512 * vb + 512], ps[:, :], ps.r, [Vv[s_].r[vb]]))
            return us

        for tg in range(4):
            units += qk_units(1, tg)
            units += v_units(tg)
            units += qk_units(0, tg)
        return units

    u0 = proj_units(0)
    u0[0]()
    emit_norm(C, 4 * l + 0)
    for u in u0[1:]:
        u()
    for p in range(8):
        s_ = p % 2
        for c, ch in enumerate(chains):
            rows = slice(64 * c, 64 * c + 64)
            ch.kT = (lambda rows, s_: (lambda jb: (kT[s_][rows, 128 * jb:128 * jb + 128], [kT[s_].r[jb // 4]])))(rows, s_)
            ch.qT = (lambda rows, s_: (lambda t0, n: (qT[s_][rows, t0:t0 + n], [qT[s_].r[t0 // 512]])))(rows, s_)
            ch.V = (lambda c, s_: (lambda jb: (Vv[s_][:, 128 * jb + 64 * c:128 * jb + 64 * c + 64],
                                               [Vv[s_].r[jb // 4]])))(c, s_)
            ch.o_dst4 = (lambda c, s_: (lambda QG: (o_pair[s_][:, 4 * QG:4 * QG + 4, 64 * c:64 * c + 64],
                                                    [o_pair[s_].r[QG]])))(c, s_)
        bg = deque(proj_units(p + 1)) if p < 7 else deque()
        emit_sb_attention(C, chains, bg)
        emit_transposes(C, o_pair[s_], oT, p, banks=(7, 7))
    P.barrier()
    A.reset(mark)
    emit_attn_out_proj(C, A, lambda dc: C.d_sbwo[j, dc], 8, oT, 4 * l + 1)
    P.barrier()


def emit_softmax_attention(C, chains, bg, bg_every=3, look=1):
    P = C.P
    n = len(chains[0].steps)

    nbuf = look + 1

    def front(k):
        b = k % nbuf
        for ch in chains:
            st = ch.steps[k]
            q0 = (st["ta"] - 4 * st["QG"]) * 128
            N = (st["tb"] - st["ta"] + 1) * 128
            kap, kr = ch.kT(st["g"], st["jb"])
            qap, qr = ch.qT(st["g"], st["ta"] * 128, N)
            if st.get("negmask") is None:
                P.mm(ch.Zb[b][:, q0:q0 + N], kap, qap, True, True, kr + qr, ch.Zb[b].r)
            else:
                P.mm(ch.Zb[b][:, q0:q0 + N], kap, qap, True, False, kr + qr, ch.Zb[b].r, inc=False)
                P.mm(ch.Zb[b][:, q0:q0 + 128], cbs(C, CB_IDENT), st["negmask"], False, True,
                     C.cb.r, ch.Zb[b].r, skip_group_check=True)
        for ch in chains:
            st = ch.steps[k]
            q0 = (st["ta"] - 4 * st["QG"]) * 128
            N = (st["tb"] - st["ta"] + 1) * 128
            PT = ch.PT[b]
            P.act(PT[:, q0:q0 + N], ch.Zb[b][:, q0:q0 + N], AF.Exp, ch.Zb[b].r, PT.r)
        for ci_, ch in enumerate(chains):
            st = ch.steps[k]
            if st["mask"] is None:
                continue
            q0 = (st["ta"] - 4 * st["QG"]) * 128
            PT = ch.PT[b]
            map_, mr, mn = st["mask"]
            eng = "pool" if (ci_ + k) % 2 == 0 else "dve"
            P.tt(eng, PT[:, q0:q0 + mn], PT[:, q0:q0 + mn], map_, ALU.mult, PT.r + mr, PT.r)

    def back(k):
        b = k % nbuf
        for ch in chains:
            st = ch.steps[k]
            PT = ch.PT[b]
            vap, vr = ch.V(st["g"], st["jb"])
            tiles = list(range(st["ta"], st["tb"] + 1))
            for ti_, t in enumerate(tiles):
                i = t - 4 * st["QG"]
                P.mm(ch.Ob[:, i * 65:(i + 1) * 65], PT[:, i * 128:(i + 1) * 128], vap,
                     start=(st["first"] and ti_ == 0), stop=False, reads=PT.r + vr, writes=ch.Ob.r,
                     inc=(ti_ == len(tiles) - 1), skip_group_check=True)
        for ch in chains:
            st = ch.steps[k]
            if st["last"]:
                rd = ch.rden
                ov = ch.Ob[:, 0:260].rearrange("p (a b) -> p a b", b=65)
                rdv = rd[:, 0:4].rearrange("p (a b) -> p a b", b=1)
                P.recip(rdv, ov[:, :, 64:65], ch.Ob.r, rd.r)
                ap, rr = ch.o_dst4(st["QG"])
                P.tt("dve", ap, ov[:, :, 0:64], rdv.broadcast_to([128, 4, 64]), ALU.mult, ch.Ob.r + rd.r, rr)

    for j_ in range(min(look, n)):
        front(j_)
    for k in range(n):
        if k + look < n:
            front(k + look)
        back(k)
        if bg and k % bg_every == bg_every - 1:
            bg.popleft()()
    while bg:
        bg.popleft()()


def emit_sin_table(C, out_ap, out_res, ang, tmp, tmpi, rows, turns_off):
    P = C.P
    TWO_PI = 2.0 * math.pi
    a = ang[rows, :]
    t = tmp[rows, :]
    ti = tmpi[rows, :]
    P.ts("dve", t, a, 1.0 / TWO_PI, turns_off + 0.5, ALU.mult, ALU.add, ang.r, tmp.r)
    P.copy("dve", ti, t, tmp.r, tmpi.r)
    P.copy("dve", t, ti, tmpi.r, tmp.r)
    P.stt(t, t, -TWO_PI, a, ALU.mult, ALU.add, tmp.r + ang.r, tmp.r)
    if turns_off != 0.0:
        P.ts("dve", t, t, TWO_PI * turns_off, None, ALU.add, None, tmp.r, tmp.r)
    for _ in range(2):
        P.ts("dve", ti.bitcast(F32), t, -math.pi, TWO_PI, ALU.is_lt, ALU.mult, tmp.r, tmpi.r)
        P.tt("dve", t, t, ti.bitcast(F32), ALU.add, tmp.r + tmpi.r, tmp.r)
        P.ts("dve", ti.bitcast(F32), t, math.pi, -TWO_PI, ALU.is_gt, ALU.mult, tmp.r, tmpi.r)
        P.tt("dve", t, t, ti.bitcast(F32), ALU.add, tmp.r + tmpi.r, tmp.r)
    P.ts("dve", t, t, math.pi, -math.pi, ALU.min, ALU.max, tmp.r, tmp.r)
    P.act(out_ap, t, AF.Sin, tmp.r, out_res)


def emit_mla_layer(C, l):
    P = C.P
    A = C.A
    SC = 96.0 ** -0.5
    A.reset()
    R64 = slice(64, 96)
    cqn = A.alloc("cqn", [3, 2048], BF16, nres=12)
    ckvn = A.alloc("ckvn", [2, 2048], BF16, nres=8)
    Ct = A.alloc("Ct", [2048], F32)
    St = A.alloc("St", [2048], F32)
    kr = A.alloc("kr", [2048], BF16, nres=4)
    mark0 = A.off
    win = A.alloc("win", [8, 768], BF16)
    cq_t = [A.alloc("cqt%d" % i, [512], F32) for i in range(3)]
    ang = A.alloc("ang", [2048], F32)
    tmp = A.alloc("rtmp", [2048], F32)
    tmpi = A.alloc("rtmpi", [2048], I32)
    t1 = A.alloc("t1", [512], F32)
    t2 = A.alloc("t2", [512], F32)
    qng = A.alloc("qng", [8], F32)
    P.dma("pool", win[:, :, :], C.d_mlawin.rearrange("p (a b) -> p a b", b=768), C.usem[0], writes=win.r)
    P.dma("sp", qng[:, :], C.d_mlang, C.usem[1], writes=qng.r)
    P.dma("sp", tmpi[0:96, :], C.d_pos.broadcast_to([96, 2048]), C.usem[1], writes=tmpi.r)
    emit_norm(C, 4 * l + 0)
    P.copy("dve", ang[R64, :], tmpi[R64, :], tmpi.r, ang.r)
    P.ts("dve", ang[R64, :], ang[R64, :], C.cf[R64, 0:1], None, ALU.mult, None, ang.r + C.cf.r, ang.r)
    emit_sin_table(C, Ct[R64, :], Ct.r, ang, tmp, tmpi, R64, 0.25)
    emit_sin_table(C, St[R64, :], St.r, ang, tmp, tmpi, R64, 0.0)
    P.ts("dve", St[R64, :], St[R64, :], C.cf[R64, 1:2], None, ALU.mult, None, St.r + C.cf.r, St.r)
    for tg in range(4):
        cols = slice(512 * tg, 512 * tg + 512)
        xr = [C.xnT.r[4 * k + tg] for k in range(8)]
        for (nch, c0, dst, gcol, ssb) in ((3, 0, cqn, 0, 6), (2, 384, ckvn, 3, 7)):
            ss = C.PS[ssb]
            for ch_ in range(nch):
                ps = C.PS[ch_ % 2]
                pairs = [(win[:, k, c0 + 128 * ch_:c0 + 128 * ch_ + 128], C.xnT[:, k, cols]) for k in range(8)]
                P.mm_group(ps[:, :], pairs, win.r + xr, ps.r)
                P.copy("dve", cq_t[ch_][:, :], ps[:, :], ps.r, cq_t[ch_].r)
                sq = C.sq[ch_ % 2]
                P.act(sq[:, :], ps[:, :], AF.Square, ps.r, sq.r)
                P.mm(ss[:, :], cbs(C, CB_ONES), sq[:, :], start=(ch_ == 0), stop=(ch_ == nch - 1),
                     reads=sq.r + C.cb.r, writes=ss.r)
            P.act(C.rsb[:, :], ss[:, :], AF.Sqrt, ss.r, C.rsb.r, bias=C.epsb[:, 0:1], scale=1.0 / (128 * nch))
            rstd = C.rstd[0]
            P.recip(rstd[:, :], C.rsb[:, :], C.rsb.r, rstd.r)
            for ch_ in range(nch):
                P.stt(dst[:, ch_, cols], cq_t[ch_][:, :], qng[:, gcol + ch_:gcol + ch_ + 1], rstd[:, :],
                      ALU.mult, ALU.mult, cq_t[ch_].r + rstd.r + qng.r, [dst.r[4 * ch_ + tg]])
        psA = C.PS[2]
        psB = C.PS[3]
        P.mm_group(psA[0:96, :], [(win[:, k, 576:672], C.xnT[:, k, cols]) for k in range(8)], win.r + xr, psA.r)
        P.mm_group(psB[0:96, :], [(win[:, k, 672:768], C.xnT[:, k, cols]) for k in range(8)], win.r + xr, psB.r)
        P.tt("dve", t1[R64, :], psA[R64, :], Ct[R64, cols], ALU.mult, psA.r + Ct.r, t1.r)
        P.tt("dve", t2[R64, :], psB[R64, :], St[R64, cols], ALU.mult, psB.r + St.r, t2.r)
        P.tt("pool", kr[R64, cols], t1[R64, :], t2[R64, :], ALU.add, t1.r + t2.r, [kr.r[tg]])
    P.barrier()
    A.reset(mark0)
    oT = C.xnT
    wh = [A.alloc("wh%d" % i, [3 * 192 + 2 * 128], BF16) for i in range(2)]
    qf = [A.alloc("qf%d" % i, [2048], BF16, nres=4) for i in range(2)]
    kf = [A.alloc("kf%d" % i, [2048], BF16, nres=4) for i in range(2)]
    Vh = [A.alloc("Vh%d" % i, [16, 65], BF16, nres=4) for i in range(2)]
    o_pair = [A.alloc("op%d" % i, [16, 128], BF16, nres=4) for i in range(2)]
    t1 = A.alloc("t1b", [512], F32)
    t2 = A.alloc("t2b", [512], F32)
    chains = []
    for c in range(2):
        ch = Chain()
        ch.PT = [A.alloc("PT%d%d" % (c, i), [512], BF16) for i in range(2)]
        ch.rden = A.alloc("rden%d" % c, [4], F32)
        ch.Zb = [C.PS[c], C.PS[2 + c]]
        ch.Ob = C.PS[4 + c]
        steps = []
        for QG in ([0, 3] if c == 0 else [1, 2]):
            nk = 4 * QG + 4
            for jb in range(nk):
                r = jb - 4 * QG
                ta = 4 * QG + max(0, r)
                steps.append(dict(QG=QG, jb=jb, ta=ta, tb=4 * QG + 3, first=(jb == 0), last=(jb == nk - 1),
                                  g=0, mask=None, negmask=(cbs(C, CB_NEG_INCL) if r >= 0 else None)))
        ch.steps = steps
        chains.append(ch)
    for s_ in range(2):
        P.memset("pool", Vh[s_][:, :, 64:65], 1.0, Vh[s_].r)

    def head_units(h):
        s_ = h % 2
        w = wh[s_]
        wq_ = w[:, 0:576].rearrange("p (a b) -> p a b", b=192)
        wkv = w[:, 576:832].rearrange("p (a b) -> p a b", b=128)
        units = []
        units.append(lambda: P.dma("pool", w[:, :], C.d_mlawh[h], C.usem[s_], writes=w.r))
        ucnt = [0]

        def q_unit(tg):
            def f():
                cols = slice(512 * tg, 512 * tg + 512)
                psA = C.PS[6]
                psB = C.PS[7]
                cr = [cqn.r[4 * kc + tg] for kc in range(3)]
                P.mm_group(psA[0:96, :], [(wq_[:, kc, 0:96], cqn[:, kc, cols]) for kc in range(3)], w.r + cr, psA.r)
                P.mm_group(psB[0:96, :], [(wq_[:, kc, 96:192], cqn[:, kc, cols]) for kc in range(3)], w.r + cr, psB.r)
                P.ts("dve", qf[s_][0:64, cols], psA[0:64, :], SC, None, ALU.mult, None, psA.r, [qf[s_].r[tg]])
                P.stt(t1[R64, :], psA[R64, :], SC, Ct[R64, cols], ALU.mult, ALU.mult, psA.r + Ct.r, t1.r)
                P.stt(t2[R64, :], psB[R64, :], SC, St[R64, cols], ALU.mult, ALU.mult, psB.r + St.r, t2.r)
                P.tt("pool", qf[s_][R64, cols], t1[R64, :], t2[R64, :], ALU.add, t1.r + t2.r, [qf[s_].r[tg]])
            return f

        def k_unit(tg):
            def f():
                cols = slice(512 * tg, 512 * tg + 512)
                ps = C.PS[6 + tg % 2]
                cr = [ckvn.r[4 * kc + tg] for kc in range(2)]
                P.mm_group(ps[0:64, :], [(wkv[:, kc, 0:64], ckvn[:, kc, cols]) for kc in range(2)], w.r + cr, ps.r)
                P.copy("dve", kf[s_][0:64, cols], ps[0:64, :], ps.r, [kf[s_].r[tg]])
                P.copy("pool", kf[s_][R64, cols], kr[R64, cols], [kr.r[tg]], [kf[s_].r[tg]])
            return f

        def v_unit(vb):
            def f():
                ps = C.PS[6 + vb % 2]
                for jj in range(8):
                    tile = 8 * vb + jj
                    tcols = slice(128 * tile, 128 * tile + 128)
                    pairs = [(ckvn[:, kc, tcols], wkv[:, kc, 64:128]) for kc in range(2)]
                    P.mm_group(ps[:, jj * 64:(jj + 1) * 64], pairs,
                               w.r + [ckvn.r[4 * kc + tile // 4] for kc in range(2)], ps.r)
                P.copy("dve", Vh[s_][:, 8 * vb:8 * vb + 8, 0:64],
                       ps[:, :].rearrange("p (a b) -> p a b", b=64), ps.r, [Vh[s_].r[2 * vb], Vh[s_].r[2 * vb + 1]])
            return f

        for tg in range(4):
            units.append(k_unit(tg))
            units.append(q_unit(tg))
        units.append(v_unit(0))
        units.append(v_unit(1))
        return units

    for u in head_units(0):
        u()
    for h in range(16):
        s_ = h % 2
        ps_ = (h // 2) % 2
        for ch in chains:
            ch.kT = (lambda s_: (lambda g, jb: (kf[s_][0:96, 128 * jb:128 * jb + 128], [kf[s_].r[jb // 4]])))(s_)
            ch.qT = (lambda s_: (lambda g, t0, n: (qf[s_][0:96, t0:t0 + n], [qf[s_].r[t0 // 512]])))(s_)
            ch.V = (lambda s_: (lambda g, jb: (Vh[s_][:, jb, :], [Vh[s_].r[jb // 4]])))(s_)
            ch.o_dst4 = (lambda hh, ps_: (lambda QG: (o_pair[ps_][:, 4 * QG:4 * QG + 4, 64 * hh:64 * hh + 64],
                                                      [o_pair[ps_].r[QG]])))(h % 2, ps_)
        bg = deque(head_units(h + 1)) if h < 15 else deque()
        emit_softmax_attention(C, chains, bg, bg_every=1)
        if h % 2 == 1:
            emit_transposes(C, o_pair[ps_], oT, h // 2)
    P.barrier()
    A.reset(mark0)
    emit_attn_out_proj(C, A, lambda dc: C.d_mlawo[dc], 8, oT, 4 * l + 1)
    P.barrier()


DIL_W = (256, 640, 2048)
DIL_OFF = (0, 256, 896)
DIL_BACK = (1, 4, 15)


def emit_dil_layer(C, l):
    P = C.P
    A = C.A
    A.reset()
    oT = A.alloc("oT", [4, 2048], BF16, nres=16)
    mark = A.off
    valid = A.alloc("valid", [2944], BF16)
    wq = [A.alloc("wq%d" % i, [8, 384], BF16) for i in range(2)]
    qT = [A.alloc("qT%d" % g, [2048], BF16, nres=4) for g in range(3)]
    kT = [A.alloc("kT%d" % g, [2048], BF16, nres=4) for g in range(3)]
    Vv = [A.alloc("V%d" % g, [16, 130], BF16, nres=4) for g in range(3)]
    M = [A.alloc("M%d" % hh, [2944], BF16, nres=3) for hh in range(2)]
    stage = [A.alloc("stage0", [1024], F32)] * 2
    scnt = [0]
    o_pair = A.alloc("op", [16, 128], BF16, nres=4)
    for g in range(3):
        for hh in range(2):
            P.memset("pool", Vv[g][:, :, 65 * hh + 64:65 * hh + 65], 1.0, Vv[g].r)
    chains = []
    for c in range(2):
        ch = Chain()
        ch.PT = [A.alloc("PT%d%d" % (c, i), [512], BF16) for i in range(3)]
        ch.rden = A.alloc("rden%d" % c, [4], F32)
        ch.Zb = [C.PS[c], C.PS[2 + c], C.PS[6 + c]]
        ch.Ob = C.PS[4 + c]
        steps = []
        for QG in range(4):
            lst = []
            for g in range(3):
                for jb in range(max(0, 4 * QG - DIL_BACK[g]), 4 * QG + 4):
                    ta = max(jb, 4 * QG)
                    tb = min(jb + DIL_BACK[g], 4 * QG + 3)
                    if tb < ta:
                        continue
                    x0 = DIL_OFF[g] + (ta - jb) * 128
                    N = (tb - ta + 1) * 128
                    lst.append(dict(QG=QG, jb=jb, ta=ta, tb=tb, first=False, last=False, g=g,
                                    mask=(M[c][:, x0:x0 + N], [M[c].r[g]], N)))
            lst[0]["first"] = True
            lst[-1]["last"] = True
            steps += lst
        ch.steps = steps
        chains.append(ch)
    wissued = [0]

    def issue_w(idx):
        if idx >= 12 or idx < wissued[0]:
            return
        wissued[0] = idx + 1
        w_ = wq[idx % 2]
        P.dma("pool", w_[:, :, :], C.d_dilqkv[idx].rearrange("p (a b) -> p a b", b=384),
              C.usem[idx % 2], writes=w_.r)

    for p in range(4):
        issue_w(3 * p)
        issue_w(3 * p + 1)
        if p == 0:
            P.dma("pool", valid[:, :], C.d_dilvalid, C.wsem[1], writes=valid.r)
            emit_norm(C, 4 * l + 0)
        for hh in range(2):
            for g in range(3):
                for c0 in range(0, DIL_W[g], 1024):
                    W = min(1024, DIL_W[g] - c0)
                    o = DIL_OFF[g] + c0
                    stg = stage[scnt[0] % 2]
                    scnt[0] += 1
                    P.dma("sp", stg[:, 0:W], C.d_dilslab[g * 8 + 2 * p + hh][:, c0:c0 + W], C.wsem[scnt[0] % 2],
                          writes=stg.r)
                    P.act(stg[:, 0:W], stg[:, 0:W], AF.Exp, stg.r, stg.r)
                    P.tt("pool", M[hh][:, o:o + W], stg[:, 0:W], valid[:, o:o + W], ALU.mult,
                         stg.r + valid.r, [M[hh].r[g]])
        for g in range(3):
            issue_w(3 * p + g)
            w = wq[(3 * p + g) % 2]
            u = 0
            for tg in range(4):
                cols = slice(512 * tg, 512 * tg + 512)
                xr = [C.xnT.r[4 * k + tg] for k in range(8)]
                for which in range(2):
                    ps = C.PS[6 + u % 2]
                    u += 1
                    pairs = [(w[:, k, 128 * which:128 * which + 128], C.xnT[:, k, cols]) for k in range(8)]
                    P.mm_group(ps[:, :], pairs, w.r + xr, ps.r)
                    if which == 0:
                        P.ts("dve", qT[g][:, cols], ps[:, :], 0.125, None, ALU.mult, None, ps.r, [qT[g].r[tg]])
                    else:
                        P.copy("dve", kT[g][:, cols], ps[:, :], ps.r, [kT[g].r[tg]])
                ps = C.PS[6 + u % 2]
                u += 1
                for jj in range(4):
                    tile = 4 * tg + jj
                    tcols = slice(128 * tile, 128 * tile + 128)
                    pairs = [(C.xnT[:, k, tcols], w[:, k, 256:384]) for k in range(8)]
                    P.mm_group(ps[:, jj * 128:(jj + 1) * 128], pairs, w.r + xr, ps.r)
                for hh in range(2):
                    src = ps[:, :].rearrange("p (a b) -> p a b", b=128)[:, :, 64 * hh:64 * hh + 64]
                    P.copy("dve", Vv[g][:, 4 * tg:4 * tg + 4, 65 * hh:65 * hh + 64], src,
                           ps.r, [Vv[g].r[tg]])
            issue_w(3 * p + g + 2)
            if g == 0 and p > 0:
                emit_transposes(C, o_pair, oT, p - 1)
        for c, ch in enumerate(chains):
            rows = slice(64 * c, 64 * c + 64)
            ch.kT = (lambda rows: (lambda g, jb: (kT[g][rows, 128 * jb:128 * jb + 128], [kT[g].r[jb // 4]])))(rows)
            ch.qT = (lambda rows: (lambda g, t0, n: (qT[g][rows, t0:t0 + n],
                                                     [qT[g].r[i] for i in range(t0 // 512, (t0 + n - 1) // 512 + 1)])))(rows)
            ch.V = (lambda c: (lambda g, jb: (Vv[g][:, jb, 65 * c:65 * c + 65], [Vv[g].r[jb // 4]])))(c)
            ch.o_dst4 = (lambda c: (lambda QG: (o_pair[:, 4 * QG:4 * QG + 4, 64 * c:64 * c + 64], [o_pair.r[QG]])))(c)
        emit_softmax_attention(C, chains, deque(), look=2)
    emit_transposes(C, o_pair, oT, 3)
    P.barrier()
    A.reset(mark)
    emit_attn_out_proj(C, A, lambda dc: C.d_dilwo[dc], 4, oT, 4 * l + 1)
    P.barrier()


def build_nc(stages):
    nc = bass.Bass("TRN2", target_bir_lowering=False)
    C = Ctx()
    C.nc = nc

    def din(name, shape, dt=F32):
        return nc.dram_tensor(name, list(shape), dt, kind="ExternalInput").ap()

    d_xT = din("xT", [D, S])
    d_cst = din("cst", [128, 1024])
    d_gn = din("gains", [128, 128])
    d_cvp = din("convp", [128, 704])
    C.d_wup = din("wup", [4, NPAIR, 128, 2048])
    C.d_wdn = din("wdn", [4, 8, 128, DFF])
    C.d_sbqkv = din("sbqkv", [2, 8, 128, 3072])
    C.d_sbwo = din("sbwo", [2, 8, 128, 1024])
    C.d_pos = din("pos", [1, S], I32)
    d_cf = din("cf", [128, 8])
    C.d_mlawin = din("mlawin", [128, 8 * 768])
    C.d_mlang = din("mlang", [128, 8])
    C.d_mlawh = din("mlawh", [16, 128, 832])
    C.d_mlawo = din("mlawo", [8, 128, 1024])
    C.d_dilqkv = din("dilqkv", [12, 128, 3072])
    C.d_dilwo = din("dilwo", [8, 128, 512])
    C.d_dilslab = din("dilslab", [24, 128, 2048])
    C.d_dilvalid = din("dilvalid", [128, 2944])
    d_out = nc.dram_tensor("outT", [D, S], F32, kind="ExternalOutput").ap()

    with ExitStack() as es:
        P = Prog(nc, es)
        C.P = P
        C.xT = P.sbuf("xT_sb", [128, 8, S], F32, nres=32)
        C.xnT = P.sbuf("xnT_sb", [128, 8, S], BF16, nres=32)
        C.cb = P.sbuf("cb", [128, 1024], BF16)
        C.gn = P.sbuf("gn", [128, 128], F32)
        C.cvp = P.sbuf("cvp", [128, 704], F32)
        C.sq = [P.sbuf("sq%d" % i, [128, 512], BF16) for i in range(2)]
        C.rsb = P.sbuf("rsb", [128, 512], F32)
        C.rsb2 = [C.rsb, C.rsb]
        C.rstd = [P.sbuf("rstd%d" % i, [128, 512], F32) for i in range(2)]
        C.cf = P.sbuf("cf_sb", [128, 8], F32)
        C.epsb = P.sbuf("epsb", [128, 1], F32)
        C.oneb = P.sbuf("oneb", [128, 1], F32)
        arena = P.sbuf("arena", [128, ARENA_F32], F32)
        C.A = Arena(arena, ARENA_F32)
        C.PS = [P.psum("ps%d" % i, [128, 512], F32) for i in range(8)]
        C.wsem = [P.dsem("wsem%d" % i) for i in range(2)]
        C.usem = [P.dsem("usem%d" % i) for i in range(2)]
        s_in = P.dsem("s_in")
        s_x = [P.dsem("s_x%d" % i) for i in range(2)]
        s_out = P.dsem("s_out")

        P.dma("pool", C.cb[:, :], d_cst, s_in, writes=C.cb.r)
        P.dma("sp", C.gn[:, :], d_gn, s_in, writes=C.gn.r)
        P.dma("sp", C.cvp[:, :], d_cvp, s_in, writes=C.cvp.r)
        P.dma("sp", C.cf[:, :], d_cf, s_in, writes=C.cf.r)
        P.memset("pool", C.epsb[:, :], EPS, C.epsb.r)
        P.memset("pool", C.oneb[:, :], 1.0, C.oneb.r)
        for k in range(8):
            P.dma("sp", C.xT[:, k, :], d_xT[128 * k:128 * k + 128, :], s_x[k % 2],
                  writes=[C.xT.r[4 * k + tg] for tg in range(4)])

        for st in stages:
            kind, l = st
            if kind == "norm":
                emit_norm(C, l)
            elif kind == "ffn":
                emit_ffn(C, l)
            elif kind == "mix":
                if l % 3 == 0:
                    emit_sb_layer(C, l, l // 3)
                elif l % 3 == 1:
                    emit_dil_layer(C, l)
                else:
                    emit_mla_layer(C, l)
        P.barrier()
        fin = Res("fin")
        for k in range(8):
            P.dma("sp", d_out[128 * k:128 * k + 128, :], C.xT[:, k, :], s_out,
                  reads=[C.xT.r[4 * k + tg] for tg in range(4)], writes=[fin])
        P.op("sp", None, reads=[fin])
        P.emit()
        C.stats = dict(ecnt=dict(P.ecnt), n_wait=P.n_wait)
    return nc, C


def lhsT_stream_layout(W):
    K = W.shape[0]
    nk = K // 128
    return np.ascontiguousarray(W.reshape(nk, 128, 8, 128).transpose(2, 1, 0, 3).reshape(8, 128, K))


def prep_shared(inp):
    f = lambda a: np.asarray(a, dtype=np.float32)
    out = {}
    i = np.arange(128)
    cst = np.zeros((128, 1024), np.float32)
    cst[:, 0:128] = np.eye(128)
    cst[:, 128:256] = 1.0
    cst[:, 256:384] = -1.0 * (i[:, None] >= i[None, :])
    cst[:, 384:512] = -1.0 * (i[:, None] < i[None, :])
    cst[:, 512:640] = (i[:, None] < i[None, :])
    cst[:, 640:768] = (i[:, None] <= i[None, :])
    cst[:, 768:896] = NEG_BIG * (i[:, None] >= i[None, :])
    cst[:, 896:1024] = NEG_BIG * (i[:, None] > i[None, :])
    out["cst"] = cst
    ng = f(inp["norm_gains"])
    out["gains"] = np.ascontiguousarray(ng.reshape(16, 8, 128).transpose(2, 0, 1).reshape(128, 128))
    cw = f(inp["ffn_conv_w"])
    cbias = f(inp["ffn_conv_b"])
    cv = np.concatenate([cw, cbias[:, None, :]], axis=1)
    cv = cv.reshape(4, 4, 44, 128).transpose(3, 0, 2, 1)
    out["convp"] = np.ascontiguousarray(cv.reshape(128, 704))
    wu = f(inp["ffn_w_up"])
    g = wu[:, :, :DFF].reshape(4, 8, 128, NPAIR, 128)
    u = wu[:, :, DFF:].reshape(4, 8, 128, NPAIR, 128)
    gu = np.stack([g, u], axis=4)
    out["wup"] = np.ascontiguousarray(gu.transpose(0, 3, 2, 1, 4, 5).reshape(4, NPAIR, 128, 2048))
    wd = f(inp["ffn_w_down"])
    out["wdn"] = np.stack([lhsT_stream_layout(wd[l]) for l in range(4)])
    sq = f(inp["sb_w_qkv"])
    t = sq.reshape(2, 8, 128, 3, 8, 128)
    out["sbqkv"] = np.ascontiguousarray(t.transpose(0, 4, 2, 1, 3, 5).reshape(2, 8, 128, 3072))
    so = f(inp["sb_w_o"])
    out["sbwo"] = np.stack([lhsT_stream_layout(so[jj]) for jj in range(2)])
    cfm = np.zeros((128, 8), np.float32)
    freqs = (np.float32(10000.0) ** (-np.arange(16, dtype=np.float32) / np.float32(16))).astype(np.float32)
    cfm[64:80, 0] = freqs
    cfm[80:96, 0] = freqs
    cfm[64:80, 1] = -1.0
    cfm[80:96, 1] = 1.0
    out["cf"] = cfm
    wi = f(inp["mla_w_in"])[0]
    wi2 = np.concatenate([wi, wi[:, 576:640], wi[:, 656:672], wi[:, 640:656]], axis=1)
    out["mlawin"] = np.ascontiguousarray(wi2.reshape(8, 128, 768).transpose(1, 0, 2).reshape(128, 8 * 768))
    ng_ = np.zeros((128, 8), np.float32)
    ng_[:, 0:3] = f(inp["mla_q_norm"])[0].reshape(3, 128).T
    ng_[:, 3:5] = f(inp["mla_kv_norm"])[0].reshape(2, 128).T
    out["mlang"] = ng_
    wqb = f(inp["mla_w_qb"])[0]
    wkvb = f(inp["mla_w_kvb"])[0]
    whs = []
    for h in range(16):
        a = wqb[:, 96 * h:96 * h + 96]
        b = np.concatenate([wqb[:, 96 * h:96 * h + 64], wqb[:, 96 * h + 80:96 * h + 96],
                            wqb[:, 96 * h + 64:96 * h + 80]], axis=1)
        q2 = np.concatenate([a, b], axis=1).reshape(3, 128, 192).transpose(1, 0, 2).reshape(128, 576)
        kv = wkvb[:, 128 * h:128 * h + 128].reshape(2, 128, 128).transpose(1, 0, 2).reshape(128, 256)
        whs.append(np.concatenate([q2, kv], axis=1))
    out["mlawh"] = np.ascontiguousarray(np.stack(whs))
    out["mlawo"] = lhsT_stream_layout(f(inp["mla_w_o"])[0])
    dq = f(inp["dil_w_qkv"])[0]
    t = dq.reshape(8, 128, 3, 3, 4, 128)
    out["dilqkv"] = np.ascontiguousarray(t.transpose(4, 3, 1, 0, 2, 5).reshape(12, 128, 3072))
    out["dilwo"] = lhsT_stream_layout(f(inp["dil_w_o"])[0])
    rb = f(inp["rel_bias"])
    jk = np.arange(128)[:, None]
    xx = np.arange(2048)[None, :]
    delta = xx - jk
    dpos = np.maximum(delta, 0)
    dflt = np.maximum(dpos.astype(np.float32), np.float32(1.0))
    large = 16 + (np.log(dflt / np.float32(16)) / np.float32(math.log(2048 / 16)) * np.float32(16)).astype(np.int32)
    large = np.minimum(large, 31)
    bucket = np.where(dpos < 16, dpos, large)
    slab = np.zeros((24, 128, 2048), np.float32)
    for gh in range(24):
        slab[gh] = rb[bucket, gh]
    out["dilslab"] = slab
    valid = np.zeros((128, 2944), np.float32)
    for g, (W, r) in enumerate(((256, 1), (640, 4), (2048, 16))):
        dl = delta[:, :W]
        valid[:, DIL_OFF[g]:DIL_OFF[g] + W] = (dl >= 0) & (dl % r == 0) & (dl <= 128 * r)
    out["dilvalid"] = valid
    return out


ALL_STAGES = [("mix", 0), ("ffn", 0), ("mix", 1), ("ffn", 1), ("mix", 2), ("ffn", 2), ("mix", 3), ("ffn", 3)]
_CACHE = {}


def run(inputs, stages, n_cores=8, trace=False):
    key = tuple(stages)
    if key not in _CACHE:
        _CACHE[key] = build_nc(stages)
    nc, C = _CACHE[key]
    shared = prep_shared(inputs)
    x = np.asarray(inputs["x"], dtype=np.float32)
    in_maps = []
    for b in range(n_cores):
        m = dict(shared)
        m["xT"] = np.ascontiguousarray(x[b].T)
        m["pos"] = np.ascontiguousarray(np.asarray(inputs["positions"])[b][None, :].astype(np.int32))
        in_maps.append(m)
    res = run_bass_kernel_spmd(nc, in_maps, core_ids=list(range(n_cores)), trace=trace)
    out = np.stack([np.ascontiguousarray(r["outT"].T) for r in res.results])
    return out, res


def kernel(**inputs):
    out, _ = run(inputs, ALL_STAGES, 8)
    return out.astype(np.float32)
```

```python
import math
from collections import deque
from contextlib import ExitStack

import numpy as np
import concourse.bass as bass
import concourse.mybir as mybir
from concourse.bass_utils import run_bass_kernel_spmd

F32 = mybir.dt.float32
BF16 = mybir.dt.bfloat16
I32 = mybir.dt.int32
AF = mybir.ActivationFunctionType
ALU = mybir.AluOpType

S = 2048
D = 1024
DFF = 2816
NPAIR = 22
EPS = 1e-6
ENGS = ("pe", "act", "dve", "pool", "sp")
import os as _os
SAME_ENGINE_SYNC = _os.environ.get("NOSES", "") == ""


class Res:
    __slots__ = ("name", "w", "rs", "excl")

    def __init__(self, name=""):
        self.name = name
        self.w = None
        self.rs = {}
        self.excl = False


class Buf:
    def __init__(self, t, nres=1, name=""):
        self.t = t
        self.r = [Res("%s.%d" % (name, i)) for i in range(nres)]

    def __getitem__(self, idx):
        return self.t[idx]


class Prog:
    def __init__(self, nc, es, same_engine_sync=SAME_ENGINE_SYNC):
        self.nc = nc
        self.es = es
        self.ops = {e: [] for e in ENGS}
        self.ecnt = {e: 0 for e in ENGS}
        self.esem = {}
        for e in ("pe", "act", "dve", "pool"):
            self.esem[e] = es.enter_context(nc.semaphore("es_" + e))
        self.known = {e: {} for e in ENGS}
        self.dcnt = {}
        self.same_engine_sync = same_engine_sync
        self.n_wait = 0

    def sbuf(self, name, shape, dtype, nres=1):
        t = self.es.enter_context(self.nc.sbuf_tensor(name, list(shape), dtype))
        return Buf(t, nres, name)

    def psum(self, name, shape, dtype, nres=1):
        t = self.es.enter_context(self.nc.psum_tensor(name, list(shape), dtype))
        b = Buf(t, nres, name)
        for r in b.r:
            r.excl = True
        return b

    def dsem(self, name):
        s = self.es.enter_context(self.nc.semaphore(name))
        self.dcnt[s] = 0
        return s

    def op(self, eng, fn, reads=(), writes=(), inc=True, dsem=None):
        waits = {}
        kn = self.known[eng]
        own = self.esem.get(eng)
        if any(r.excl for r in reads):
            writes = list(writes) + [r for r in reads if r.excl and r not in writes]
            reads = [r for r in reads if not r.excl]

        def need(m):
            if m is None:
                return
            sem, val = m
            if sem in self.dcnt:
                val = self.dcnt[sem]
            elif sem is own:
                if eng == "pe" or not self.same_engine_sync:
                    return
            if kn.get(sem, 0) >= val:
                return
            if waits.get(sem, 0) < val:
                waits[sem] = val

        for r in reads:
            need(r.w)
        for w in writes:
            need(w.w)
            for m in w.rs.items():
                need(m)
        for sem, val in waits.items():
            kn[sem] = val
        if dsem is not None:
            self.dcnt[dsem] += 16
            marker = (dsem, self.dcnt[dsem])
            incspec = (dsem, 16)
        elif eng == "sp":
            marker = None
            incspec = None
        elif inc:
            self.ecnt[eng] += 1
            marker = (own, self.ecnt[eng])
            incspec = (own, 1)
        else:
            marker = (own, self.ecnt[eng] + 1)
            incspec = None
        if marker is not None:
            for r in reads:
                if r.rs.get(marker[0], 0) < marker[1]:
                    r.rs[marker[0]] = marker[1]
            for w in writes:
                w.w = marker
                w.rs = {}
        self.n_wait += len(waits)
        self.ops[eng].append((list(waits.items()), fn, incspec))

    def barrier(self):
        targets = [(self.esem[e], self.ecnt[e]) for e in self.esem if self.ecnt[e] > 0]
        targets += [(s, c) for s, c in self.dcnt.items() if c > 0]
        for eng in ENGS:
            waits = []
            for sem, val in targets:
                if sem is self.esem.get(eng):
                    continue
                if self.known[eng].get(sem, 0) >= val:
                    continue
                self.known[eng][sem] = val
                waits.append((sem, val))
            if waits:
                self.ops[eng].append((waits, None, None))

    def emit(self):
        nc = self.nc
        with nc.Block() as block:
            def run(engname):
                def body(e):
                    for waits, fn, incspec in self.ops[engname]:
                        for sem, val in waits:
                            e.wait_ge(sem, val)
                        if fn is None:
                            continue
                        ins = fn(e)
                        if incspec is not None:
                            ins.then_inc(incspec[0], incspec[1])
                return body

            block.tensor(run("pe"))
            block.scalar(run("act"))
            block.vector(run("dve"))
            block.gpsimd(run("pool"))
            block.sync(run("sp"))

    def dma(self, eng, out, in_, dsem, reads=(), writes=()):
        if eng == "pool":
            self.op(eng, lambda e: e.dma_start(out=out, in_=in_, max_dma_last_dim=4096), reads, writes, dsem=dsem)
        else:
            self.op(eng, lambda e: e.dma_start(out=out, in_=in_), reads, writes, dsem=dsem)

    def mm(self, out, lhsT, rhs, start, stop, reads, writes, inc=True, **kw):
        self.op("pe", lambda e: e.matmul(out, lhsT, rhs, start=start, stop=stop, **kw),
                reads, writes, inc=inc)

    def mm_group(self, out, pairs, reads, writes, **kw):
        n = len(pairs)
        for i, (l, r) in enumerate(pairs):
            self.mm(out, l, r, start=(i == 0), stop=(i == n - 1),
                    reads=reads, writes=writes, inc=(i == n - 1), **kw)

    def act(self, out, in_, func, reads, writes, **kw):
        self.op("act", lambda e: e.activation(out=out, in_=in_, func=func, **kw), reads, writes)

    def tt(self, eng, out, in0, in1, op, reads, writes):
        self.op(eng, lambda e: e.tensor_tensor(out=out, in0=in0, in1=in1, op=op), reads, writes)

    def ts(self, eng, out, in0, s1, s2, op0, op1, reads, writes):
        if s2 is None:
            self.op(eng, lambda e: e.tensor_scalar(out=out, in0=in0, scalar1=s1, scalar2=None, op0=op0),
                    reads, writes)
        else:
            self.op(eng, lambda e: e.tensor_scalar(out=out, in0=in0, scalar1=s1, scalar2=s2,
                                                   op0=op0, op1=op1), reads, writes)

    def stt(self, out, in0, scalar, in1, op0, op1, reads, writes):
        self.op("dve", lambda e: e.scalar_tensor_tensor(out=out, in0=in0, scalar=scalar, in1=in1,
                                                        op0=op0, op1=op1), reads, writes)

    def copy(self, eng, out, in_, reads, writes):
        if eng == "act":
            self.act(out, in_, AF.Copy, reads, writes)
        else:
            self.op(eng, lambda e: e.tensor_copy(out=out, in_=in_), reads, writes)

    def recip(self, out, in_, reads, writes):
        self.op("dve", lambda e: e.reciprocal(out=out, in_=in_), reads, writes)

    def memset(self, eng, out, val, writes):
        self.op(eng, lambda e: e.memset(out, val), (), writes)


def split_mm_group(P, out, pairs, reads, writes, chunk=2):
    units = []
    n = len(pairs)
    for c0 in range(0, n, chunk):
        def f(c0=c0):
            for i in range(c0, min(n, c0 + chunk)):
                P.mm(out, pairs[i][0], pairs[i][1], start=(i == 0), stop=(i == n - 1),
                     reads=reads, writes=writes, inc=(i == n - 1))
        units.append(f)
    return units


class Arena:
    def __init__(self, buf, nf32):
        self.buf = buf
        self.n = nf32
        self.off = 0

    def reset(self, to=0):
        self.off = to

    def alloc(self, name, shape, dtype, nres=1):
        n = 1
        for s_ in shape:
            n *= s_
        esz = 4 if dtype in (F32, I32) else 2
        nf = (n * esz + 3) // 4
        nf = (nf + 3) // 4 * 4
        assert self.off + nf <= self.n, "arena overflow %s: %d + %d > %d" % (name, self.off, nf, self.n)
        ap = self.buf.t[:, self.off:self.off + nf]
        self.off += nf
        if dtype != F32:
            ap = ap.bitcast(dtype)
        ap = ap[:, 0:n]
        if len(shape) == 2:
            ap = ap.rearrange("p (a b) -> p a b", b=shape[1])
        elif len(shape) == 3:
            ap = ap.rearrange("p (a b c) -> p a b c", b=shape[1], c=shape[2])
        return Buf(ap, nres, name)


CB_IDENT, CB_ONES, CB_TRI_INCL, CB_TRI_REST, CB_M_STRICT, CB_M_INCL, CB_NEG_STRICT, CB_NEG_INCL = range(8)
NEG_BIG = -30000.0
ARENA_F32 = 25216


class Ctx:
    pass


def cbs(C, i):
    return C.cb[:, 128 * i:128 * (i + 1)]


def emit_norm(C, gidx):
    P = C.P
    for tg in range(4):
        cols = slice(512 * tg, 512 * tg + 512)
        ss = C.PS[6 + tg % 2]
        for k in range(8):
            sq = C.sq[k % 2]
            P.act(sq[:, :], C.xT[:, k, cols], AF.Square, [C.xT.r[4 * k + tg]], sq.r)
            P.mm(ss[:, :], cbs(C, CB_ONES), sq[:, :], start=(k == 0), stop=(k == 7),
                 reads=sq.r + C.cb.r, writes=ss.r)
        rsb = C.rsb
        P.act(rsb[:, :], ss[:, :], AF.Sqrt, ss.r, rsb.r, bias=C.epsb[:, 0:1], scale=1.0 / D)
        rstd = C.rstd[tg % 2]
        P.recip(rstd[:, :], rsb[:, :], rsb.r, rstd.r)
        for k in range(8):
            P.stt(C.xnT[:, k, cols], C.xT[:, k, cols], C.gn[:, gidx * 8 + k:gidx * 8 + k + 1], rstd[:, :],
                  ALU.mult, ALU.mult, [C.xT.r[4 * k + tg]] + rstd.r + C.gn.r, [C.xnT.r[4 * k + tg]])


class OutProj:
    def __init__(self, C, bufs, wsrc, nk, rhs_fn, tgs, gidx):
        self.C, self.bufs, self.wsrc, self.nk, self.rhs_fn, self.tgs, self.gidx = C, bufs, wsrc, nk, rhs_fn, tgs, gidx
        self.loaded = set()

    def load(self, dc):
        if dc in self.loaded or dc >= 8:
            return
        self.loaded.add(dc)
        C = self.C
        w = self.bufs[0][dc % 2]
        C.P.dma("pool", w[:, :], self.wsrc(dc), C.wsem[dc % 2], writes=w.r)

    def run_mm(self, fofs=0, ssb=2, bg=None):
        C, nk, rhs_fn, tgs, gidx = self.C, self.nk, self.rhs_fn, self.tgs, self.gidx
        P = C.P
        wbuf, fT, tmpb = self.bufs
        self.fofs = fofs
        self.ssb = ssb
        self.load(0)
        for dc in range(8):
            w = wbuf[dc % 2]
            for ti, tg in enumerate(tgs):
                ps = C.PS[ti]
                pairs = []
                reads = list(w.r)
                for kc in range(nk):
                    ap, rr = rhs_fn(kc, ti)
                    pairs.append((w[:, kc * 128:(kc + 1) * 128], ap))
                    reads += rr
                P.mm_group(ps[:, :], pairs, reads, ps.r)
                if ti == 0:
                    self.load(dc + 1)
                fo = fofs + ti * 512
                P.copy("dve", fT[:, dc, fo:fo + 512], ps[:, :], ps.r, [fT.r[(fofs // 512 + ti) * 8 + dc]])
                sq = C.sq[ti]
                P.act(sq[:, :], ps[:, :], AF.Square, ps.r, sq.r)
                ss = C.PS[ssb + ti]
                P.mm(ss[:, :], cbs(C, CB_ONES), sq[:, :], start=(dc == 0), stop=(dc == 7),
                     reads=sq.r + C.cb.r, writes=ss.r)
                if bg:
                    bg.popleft()()
                    if dc >= 1 and bg:
                        bg.popleft()()

    def final_units(self, rstd_bufs=None):
        C, tgs, gidx = self.C, self.tgs, self.gidx
        P = C.P
        wbuf, fT, tmpb = self.bufs
        fofs, ssb = self.fofs, self.ssb
        rstds = rstd_bufs or C.rstd
        units = []

        def rs(ti):
            def f():
                ss = C.PS[ssb + ti]
                rsb = C.rsb2[ti]
                P.act(rsb[:, :], ss[:, :], AF.Sqrt, ss.r, rsb.r, bias=C.epsb[:, 0:1], scale=1.0 / D)
                P.recip(rstds[ti][:, :], rsb[:, :], rsb.r, rstds[ti].r)
            return f

        def fin(dc, ti, tg):
            def f():
                cols = slice(512 * tg, 512 * tg + 512)
                rstd = rstds[ti]
                tmp = tmpb[ti]
                fo = fofs + ti * 512
                fr = [fT.r[(fofs // 512 + ti) * 8 + dc]]
                P.tt("pool" if ti == 0 else "dve", tmp[:, :], fT[:, dc, fo:fo + 512], rstd[:, :], ALU.mult,
                     fr + rstd.r, tmp.r)
                P.stt(C.xT[:, dc, cols], tmp[:, :], C.gn[:, gidx * 8 + dc:gidx * 8 + dc + 1], C.xT[:, dc, cols],
                      ALU.mult, ALU.add, [C.xT.r[4 * dc + tg]] + tmp.r + C.gn.r, [C.xT.r[4 * dc + tg]])
            return f

        for ti, tg in enumerate(tgs):
            units.append(rs(ti))
        for dc in range(8):
            for ti, tg in enumerate(tgs):
                units.append(fin(dc, ti, tg))
        return units

    def run(self):
        self.run_mm()
        for u in self.final_units():
            u()


def emit_attn_out_proj(C, A, wsrc, nk, oT, gidx):
    wbuf = [A.alloc("wo%d" % i, [nk * 128], BF16) for i in range(2)]
    fT = A.alloc("fT", [8, 2048], BF16, nres=32)
    tmpb = [A.alloc("tmp%d" % i, [512], F32) for i in range(2)]
    rstd2 = [A.alloc("rstdb%d" % i, [512], F32) for i in range(2)]
    ops = []
    for half in range(2):
        def rhs_fn(kc, ti, half=half):
            tg = 2 * half + ti
            return oT[:, kc, 512 * tg:512 * tg + 512], [oT.r[4 * kc + tg]]
        ops.append(OutProj(C, (wbuf, fT, tmpb), wsrc, nk, rhs_fn, [2 * half, 2 * half + 1], gidx))
    ops[0].run_mm(fofs=0, ssb=2)
    f0 = deque(ops[0].final_units())
    ops[1].run_mm(fofs=1024, ssb=6, bg=f0)
    while f0:
        f0.popleft()()
    for u in ops[1].final_units(rstd_bufs=rstd2):
        u()


def emit_out_proj(C, bufs, wsrc, nk, rhs_fn, tgs, gidx):
    OutProj(C, bufs, wsrc, nk, rhs_fn, tgs, gidx).run()


def alloc_outproj_bufs(C, nk, tmpb=None):
    A = C.A
    wbuf = [A.alloc("wo%d" % i, [nk * 128], BF16) for i in range(2)]
    fT = A.alloc("fT", [8, 1024], BF16, nres=16)
    if tmpb is None:
        tmpb = [A.alloc("tmp%d" % i, [512], F32) for i in range(2)]
    return wbuf, fT, tmpb


def emit_ffn(C, l):
    P = C.P
    A = C.A
    A.reset()
    mT = A.alloc("mT", [NPAIR, 1024], BF16, nres=NPAIR * 2)
    wup = [A.alloc("wup%d" % i, [8, 256], BF16) for i in range(2)]
    hs = [[A.alloc("hs%d%d" % (i, j), [514], F32) for j in range(2)] for i in range(2)]
    acc = [[A.alloc("acc%d%d" % (i, j), [512], F32) for j in range(2)] for i in range(2)]
    tails = A.alloc("tails", [44, 2], F32, nres=44)
    tmpx = A.alloc("tmpx", [512], F32)
    opb = alloc_outproj_bufs(C, NPAIR, tmpb=[tmpx, tmpx])
    nload = [0]

    def load_w(seq):
        if seq >= 2 * NPAIR or seq < nload[0]:
            return
        nload[0] = seq + 1
        i = seq % NPAIR
        w = wup[seq % 2]
        P.dma("pool", w[:, :, :], C.d_wup[l, i].rearrange("p (a b) -> p a b", b=256), C.usem[seq % 2], writes=w.r)

    def finish(u):
        half, i, t2, par = u
        ag, au = acc[par]
        P.act(ag[:, :], ag[:, :], AF.Silu, ag.r, ag.r)
        P.tt("pool", mT[:, i, 512 * t2:512 * t2 + 512], ag[:, :], au[:, :], ALU.mult,
             ag.r + au.r, [mT.r[2 * i + t2]])

    load_w(0)
    emit_norm(C, 4 * l + 2)
    cnt = 0
    pending = deque()
    for half in range(2):
        def rhs_fn(kc, ti):
            return mT[:, kc, 512 * ti:512 * ti + 512], [mT.r[2 * kc + ti]]
        op = OutProj(C, opb, lambda dc: C.d_wdn[l, dc], NPAIR, rhs_fn, [2 * half, 2 * half + 1], 4 * l + 3)
        prev = None
        for i in range(NPAIR):
            seq = half * NPAIR + i
            w = wup[seq % 2]
            for t2 in range(2):
                tg = 2 * half + t2
                cols = slice(512 * tg, 512 * tg + 512)
                par = cnt % 2
                pss = []
                for gu in range(2):
                    ps = C.PS[(2 * cnt + gu) % 8]
                    pairs = [(w[:, k, 128 * gu:128 * gu + 128], C.xnT[:, k, cols]) for k in range(8)]
                    P.mm_group(ps[:, :], pairs, w.r + [C.xnT.r[4 * k + tg] for k in range(8)], ps.r)
                    pss.append(ps)
                if t2 == 0:
                    load_w(seq + 1)
                    if i == NPAIR - 2:
                        op.load(0)
                cnt += 1
                for gu in range(2):
                    ci = i + NPAIR * gu
                    pbase = (l * 44 + ci) * 4
                    h = hs[par][gu]
                    a = acc[par][gu]
                    ps = pss[gu]
                    P.copy("act", h[:, 2:514], ps[:, :], ps.r, h.r)
                    P.act(a[:, :], ps[:, :], AF.Identity, ps.r + C.cvp.r, a.r,
                          bias=C.cvp[:, pbase + 3:pbase + 4], scale=C.cvp[:, pbase + 2:pbase + 3])
                for gu in range(2):
                    ci = i + NPAIR * gu
                    h = hs[par][gu]
                    if tg == 0:
                        P.memset("pool", h[:, 0:2], 0.0, h.r)
                    else:
                        P.copy("pool", h[:, 0:2], tails[:, ci, :], [tails.r[ci]], h.r)
                if tg < 3:
                    for gu in range(2):
                        ci = i + NPAIR * gu
                        h = hs[par][gu]
                        P.copy("pool", tails[:, ci, :], h[:, 512:514], h.r, [tails.r[ci]])
                if prev is not None:
                    finish(prev)
                if pending:
                    pending.popleft()()
                for tap in (1, 0):
                    for gu in range(2):
                        ci = i + NPAIR * gu
                        pbase = (l * 44 + ci) * 4
                        h = hs[par][gu]
                        a = acc[par][gu]
                        P.stt(a[:, :], h[:, tap:tap + 512], C.cvp[:, pbase + tap:pbase + tap + 1], a[:, :],
                              ALU.mult, ALU.add, h.r + a.r + C.cvp.r, a.r)
                prev = (half, i, t2, par)
        finish(prev)
        op.run_mm()
        pending = deque(op.final_units())
        pending.popleft()()
        pending.popleft()()
        if half == 1:
            while pending:
                pending.popleft()()
    P.barrier()


class Chain:
    pass


def emit_transposes(C, o_pair, oT, p, banks=(6, 7)):
    P = C.P
    for tb in range(4):
        ps = C.PS[banks[tb % 2]]
        psb = ps.t[:, :].bitcast(BF16)
        for j in range(4):
            tile = 4 * tb + j
            P.op("pe", (lambda o, i: (lambda e: e.transpose(o, i, cbs(C, CB_IDENT))))(
                psb[:, j * 128:(j + 1) * 128], o_pair[:, tile, :]),
                 [o_pair.r[tile // 4]] + C.cb.r, ps.r)
        P.copy("dve", oT[:, p, 512 * tb:512 * tb + 512], psb[:, 0:512], ps.r, [oT.r[4 * p + tb]])


def sb_steps():
    steps = []
    for QG in range(4):
        nk = 4 * QG + 4
        for jb in range(nk - 1, -1, -1):
            r = jb - 4 * QG
            ta = 4 * QG + max(0, r)
            steps.append(dict(QG=QG, jb=jb, ta=ta, tb=4 * QG + 3, diag=(jb if r >= 0 else None),
                              first=(jb == nk - 1), last=(jb == 0)))
    return steps


def emit_sb_attention(C, chains, bg):
    P = C.P
    steps = sb_steps()
    n = len(steps)

    def q0_of(k):
        st = steps[k]
        return (st["ta"] - 4 * st["QG"]) * 128

    def Zm(k):
        st = steps[k]
        q0 = q0_of(k)
        for ch in chains:
            Zb = ch.Zb[k % 2]
            kap, kr = ch.kT(st["jb"])
            qap, qr = ch.qT(st["ta"] * 128, 512 - q0)
            if st["diag"] is None:
                P.mm(Zb[:, q0:512], kap, qap, True, True, kr + qr, Zb.r)
            else:
                P.mm(Zb[:, q0:512], kap, qap, True, False, kr + qr, Zb.r, inc=False)
                P.mm(Zb[:, q0:q0 + 128], cbs(C, CB_IDENT), cbs(C, CB_NEG_STRICT), False, True,
                     C.cb.r, Zb.r, skip_group_check=True)

    def Em(k):
        q0 = q0_of(k)
        for ch in chains:
            E = ch.E[k % 3]
            Zb = ch.Zb[k % 2]
            P.act(E[:, q0:512], Zb[:, q0:512], AF.Exp, Zb.r, E.r)

    def Lm(k):
        q0 = q0_of(k)
        for ch in chains:
            E = ch.E[k % 3]
            Lp = ch.Lp[k % 2]
            P.act(Lp[:, q0:512], E[:, q0:512], AF.Ln, E.r, Lp.r, bias=1.0, scale=1.0)

    def TRIm(k):
        st = steps[k]
        q0 = q0_of(k)
        for ch in chains:
            Lp = ch.Lp[k % 2]
            P.mm(ch.Ab[:, q0:512], cbs(C, CB_TRI_INCL), Lp[:, q0:512], st["first"], False,
                 Lp.r + C.cb.r, ch.Ab.r, skip_group_check=True)

    def Xm(k):
        q0 = q0_of(k)
        for ch in chains:
            P.act(ch.X[:, q0:512], ch.Ab[:, q0:512], AF.Exp, ch.Ab.r, ch.X.r)

    def RESTm(k):
        st = steps[k]
        q0 = q0_of(k)
        if st["last"]:
            return
        for ch in chains:
            Lp = ch.Lp[k % 2]
            P.mm(ch.Ab[:, q0:512], cbs(C, CB_TRI_REST), Lp[:, q0:512], False, False,
                 Lp.r + C.cb.r, ch.Ab.r, skip_group_check=True)

    def aTm(k):
        q0 = q0_of(k)
        for ch in chains:
            aT = ch.aT[k % 2]
            E = ch.E[k % 3]
            P.tt("dve", aT[:, q0:512], E[:, q0:512], ch.X[:, q0:512], ALU.mult, E.r + ch.X.r, aT.r)

    def Om(k):
        st = steps[k]
        for ci_, ch in enumerate(chains):
            aT = ch.aT[k % 2]
            vap, vr = ch.V(st["jb"])
            tiles = list(range(st["ta"], st["tb"] + 1))
            for ti_, t in enumerate(tiles):
                i = t - 4 * st["QG"]
                oc = ch.ocol + i * 64
                P.mm(ch.Ob[:, oc:oc + 64], aT[:, i * 128:(i + 1) * 128], vap,
                     start=(st["first"] and ti_ == 0 and ci_ == 0), stop=False, reads=aT.r + vr,
                     writes=ch.Ob.r, inc=(ti_ == len(tiles) - 1), skip_group_check=True)
        if st["last"]:
            for ch in chains:
                ap, rr = ch.o_dst4(st["QG"])
                src = ch.Ob[:, ch.ocol:ch.ocol + 256].rearrange("p (a b) -> p a b", b=64)
                P.copy("dve", ap, src, ch.Ob.r, rr)

    Zm(0)
    Em(0)
    Lm(0)
    TRIm(0)
    if n > 1:
        Zm(1)
        Em(1)
    if n > 2:
        Zm(2)
    for k in range(n):
        Xm(k)
        RESTm(k)
        if k + 1 < n:
            Lm(k + 1)
            TRIm(k + 1)
        aTm(k)
        if k + 2 < n:
            Em(k + 2)
        Om(k)
        if k + 3 < n:
            Zm(k + 3)
        nb_ = -(-len(bg) // max(1, n - 1 - k)) if bg else 0
        for _ in range(nb_):
            if bg:
                bg.popleft()()
    while bg:
        bg.popleft()()


def emit_sb_layer(C, l, j):
    P = C.P
    A = C.A
    A.reset()
    oT = A.alloc("oT", [8, 2048], BF16, nres=32)
    mark = A.off
    wq = [A.alloc("wq%d" % i, [8, 384], BF16) for i in range(2)]
    qT = [A.alloc("qT%d" % i, [2048], BF16, nres=4) for i in range(2)]
    kT = [A.alloc("kT%d" % i, [2048], BF16, nres=4) for i in range(2)]
    Vv = [A.alloc("V%d" % i, [2048], BF16, nres=4) for i in range(2)]
    o_pair = [A.alloc("op0", [16, 128], BF16, nres=4)] * 2
    chains = []
    for c in range(2):
        ch = Chain()
        ch.E = [A.alloc("E%d%d" % (c, i), [512], F32) for i in range(3)]
        ch.X = A.alloc("X%d" % c, [512], F32)
        ch.Lp = [A.alloc("Lp%d%d" % (c, i), [512], BF16) for i in range(2)]
        ch.aT = [A.alloc("aT%d%d" % (c, i), [512], BF16) for i in range(2)]
        ch.Zb = [C.PS[2 * c], C.PS[2 * c + 1]]
        ch.Ab = C.PS[4 + c]
        ch.Ob = C.PS[6]
        ch.ocol = 256 * c
        chains.append(ch)

    def proj_units(p):
        s_ = p % 2
        w = wq[s_]
        units = []
        units.append(lambda: P.dma("pool", w[:, :, :], C.d_sbqkv[j, p].rearrange("p (a b) -> p a b", b=384),
                                   C.usem[s_], writes=w.r))
        ucnt = [0]

        def qk_units(which, tg):
            ps = C.PS[7]
            ucnt[0] += 1
            cols = slice(512 * tg, 512 * tg + 512)
            pairs = [(w[:, k, 128 * which:128 * which + 128], C.xnT[:, k, cols]) for k in range(8)]
            us = split_mm_group(P, ps[:, :], pairs, w.r + [C.xnT.r[4 * k + tg] for k in range(8)], ps.r, 2)
            if which == 0:
                us.append(lambda: P.ts("dve", qT[s_][:, cols], ps[:, :], 0.125, None, ALU.mult, None, ps.r,
                                       [qT[s_].r[tg]]))
            else:
                us.append(lambda: P.copy("dve", kT[s_][:, cols], ps[:, :], ps.r, [kT[s_].r[tg]]))
            return us

        def v_units(vb):
            ps = C.PS[7]
            ucnt[0] += 1
            us = []
            for jj in range(4):
                tile = 4 * vb + jj
                tcols = slice(128 * tile, 128 * tile + 128)
                pairs = [(C.xnT[:, k, tcols], w[:, k, 256:384]) for k in range(8)]
                us += split_mm_group(P, ps[:, jj * 128:(jj + 1) * 128], pairs,
                                     w.r + [C.xnT.r[4 * k + vb] for k in range(8)], ps.r, 4)
            us.append(lambda: P.copy("dve", Vv[s_][:, 512 * vb:512 * vb + 512], ps[:, :], ps.r, [Vv[s_].r[vb]]))
            return us

        for tg in range(4):
            units += qk_units(1, tg)
            units += v_units(tg)
            units += qk_units(0, tg)
        return units

    u0 = proj_units(0)
    u0[0]()
    emit_norm(C, 4 * l + 0)
    for u in u0[1:]:
        u()
    for p in range(8):
        s_ = p % 2
        for c, ch in enumerate(chains):
            rows = slice(64 * c, 64 * c + 64)
            ch.kT = (lambda rows, s_: (lambda jb: (kT[s_][rows, 128 * jb:128 * jb + 128], [kT[s_].r[jb // 4]])))(rows, s_)
            ch.qT = (lambda rows, s_: (lambda t0, n: (qT[s_][rows, t0:t0 + n], [qT[s_].r[t0 // 512]])))(rows, s_)
            ch.V = (lambda c, s_: (lambda jb: (Vv[s_][:, 128 * jb + 64 * c:128 * jb + 64 * c + 64],
                                               [Vv[s_].r[jb // 4]])))(c, s_)
            ch.o_dst4 = (lambda c, s_: (lambda QG: (o_pair[s_][:, 4 * QG:4 * QG + 4, 64 * c:64 * c + 64],
                                                    [o_pair[s_].r[QG]])))(c, s_)
        bg = deque(proj_units(p + 1)) if p < 7 else deque()
        emit_sb_attention(C, chains, bg)
        emit_transposes(C, o_pair[s_], oT, p, banks=(7, 7))
    P.barrier()
    A.reset(mark)
    emit_attn_out_proj(C, A, lambda dc: C.d_sbwo[j, dc], 8, oT, 4 * l + 1)
    P.barrier()


def emit_softmax_attention(C, chains, bg, bg_every=3, look=1):
    P = C.P
    n = len(chains[0].steps)

    nbuf = look + 1

    def front(k):
        b = k % nbuf
        for ch in chains:
            st = ch.steps[k]
            q0 = (st["ta"] - 4 * st["QG"]) * 128
            N = (st["tb"] - st["ta"] + 1) * 128
            kap, kr = ch.kT(st["g"], st["jb"])
            qap, qr = ch.qT(st["g"], st["ta"] * 128, N)
            if st.get("negmask") is None:
                P.mm(ch.Zb[b][:, q0:q0 + N], kap, qap, True, True, kr + qr, ch.Zb[b].r)
            else:
                P.mm(ch.Zb[b][:, q0:q0 + N], kap, qap, True, False, kr + qr, ch.Zb[b].r, inc=False)
                P.mm(ch.Zb[b][:, q0:q0 + 128], cbs(C, CB_IDENT), st["negmask"], False, True,
                     C.cb.r, ch.Zb[b].r, skip_group_check=True)
        for ch in chains:
            st = ch.steps[k]
            q0 = (st["ta"] - 4 * st["QG"]) * 128
            N = (st["tb"] - st["ta"] + 1) * 128
            PT = ch.PT[b]
            P.act(PT[:, q0:q0 + N], ch.Zb[b][:, q0:q0 + N], AF.Exp, ch.Zb[b].r, PT.r)
        for ci_, ch in enumerate(chains):
            st = ch.steps[k]
            if st["mask"] is None:
                continue
            q0 = (st["ta"] - 4 * st["QG"]) * 128
            PT = ch.PT[b]
            map_, mr, mn = st["mask"]
            eng = "pool" if (ci_ + k) % 2 == 0 else "dve"
            P.tt(eng, PT[:, q0:q0 + mn], PT[:, q0:q0 + mn], map_, ALU.mult, PT.r + mr, PT.r)

    def back(k):
        b = k % nbuf
        for ch in chains:
            st = ch.steps[k]
            PT = ch.PT[b]
            vap, vr = ch.V(st["g"], st["jb"])
            tiles = list(range(st["ta"], st["tb"] + 1))
            for ti_, t in enumerate(tiles):
                i = t - 4 * st["QG"]
                P.mm(ch.Ob[:, i * 65:(i + 1) * 65], PT[:, i * 128:(i + 1) * 128], vap,
                     start=(st["first"] and ti_ == 0), stop=False, reads=PT.r + vr, writes=ch.Ob.r,
                     inc=(ti_ == len(tiles) - 1), skip_group_check=True)
        for ch in chains:
            st = ch.steps[k]
            if st["last"]:
                rd = ch.rden
                ov = ch.Ob[:, 0:260].rearrange("p (a b) -> p a b", b=65)
                rdv = rd[:, 0:4].rearrange("p (a b) -> p a b", b=1)
                P.recip(rdv, ov[:, :, 64:65], ch.Ob.r, rd.r)
                ap, rr = ch.o_dst4(st["QG"])
                P.tt("dve", ap, ov[:, :, 0:64], rdv.broadcast_to([128, 4, 64]), ALU.mult, ch.Ob.r + rd.r, rr)

    for j_ in range(min(look, n)):
        front(j_)
    for k in range(n):
        if k + look < n:
            front(k + look)
        back(k)
        if bg and k % bg_every == bg_every - 1:
            bg.popleft()()
    while bg:
        bg.popleft()()


def emit_sin_table(C, out_ap, out_res, ang, tmp, tmpi, rows, turns_off):
    P = C.P
    TWO_PI = 2.0 * math.pi
    a = ang[rows, :]
    t = tmp[rows, :]
    ti = tmpi[rows, :]
    P.ts("dve", t, a, 1.0 / TWO_PI, turns_off + 0.5, ALU.mult, ALU.add, ang.r, tmp.r)
    P.copy("dve", ti, t, tmp.r, tmpi.r)
    P.copy("dve", t, ti, tmpi.r, tmp.r)
    P.stt(t, t, -TWO_PI, a, ALU.mult, ALU.add, tmp.r + ang.r, tmp.r)
    if turns_off != 0.0:
        P.ts("dve", t, t, TWO_PI * turns_off, None, ALU.add, None, tmp.r, tmp.r)
    P.ts("dve", ti.bitcast(F32), t, -math.pi, TWO_PI, ALU.is_lt, ALU.mult, tmp.r, tmpi.r)
    P.tt("dve", t, t, ti.bitcast(F32), ALU.add, tmp.r + tmpi.r, tmp.r)
    P.ts("dve", ti.bitcast(F32), t, math.pi, -TWO_PI, ALU.is_gt, ALU.mult, tmp.r, tmpi.r)
    P.tt("dve", t, t, ti.bitcast(F32), ALU.add, tmp.r + tmpi.r, tmp.r)
    P.ts("dve", t, t, math.pi, -math.pi, ALU.min, ALU.max, tmp.r, tmp.r)
    P.act(out_ap, t, AF.Sin, tmp.r, out_res)


def emit_mla_layer(C, l):
    P = C.P
    A = C.A
    SC = 96.0 ** -0.5
    A.reset()
    R64 = slice(64, 96)
    cqn = A.alloc("cqn", [3, 2048], BF16, nres=12)
    ckvn = A.alloc("ckvn", [2, 2048], BF16, nres=8)
    Ct = A.alloc("Ct", [2048], F32)
    St = A.alloc("St", [2048], F32)
    kr = A.alloc("kr", [2048], BF16, nres=4)
    mark0 = A.off
    win = A.alloc("win", [8, 768], BF16)
    cq_t = [A.alloc("cqt%d" % i, [512], F32) for i in range(3)]
    ang = A.alloc("ang", [2048], F32)
    tmp = A.alloc("rtmp", [2048], F32)
    tmpi = A.alloc("rtmpi", [2048], I32)
    t1 = A.alloc("t1", [512], F32)
    t2 = A.alloc("t2", [512], F32)
    qng = A.alloc("qng", [8], F32)
    P.dma("pool", win[:, :, :], C.d_mlawin.rearrange("p (a b) -> p a b", b=768), C.usem[0], writes=win.r)
    P.dma("sp", qng[:, :], C.d_mlang, C.usem[1], writes=qng.r)
    P.dma("sp", tmpi[0:96, :], C.d_pos.broadcast_to([96, 2048]), C.usem[1], writes=tmpi.r)
    emit_norm(C, 4 * l + 0)
    P.copy("dve", ang[R64, :], tmpi[R64, :], tmpi.r, ang.r)
    P.ts("dve", ang[R64, :], ang[R64, :], C.cf[R64, 0:1], None, ALU.mult, None, ang.r + C.cf.r, ang.r)
    emit_sin_table(C, Ct[R64, :], Ct.r, ang, tmp, tmpi, R64, 0.25)
    emit_sin_table(C, St[R64, :], St.r, ang, tmp, tmpi, R64, 0.0)
    P.ts("dve", St[R64, :], St[R64, :], C.cf[R64, 1:2], None, ALU.mult, None, St.r + C.cf.r, St.r)
    for tg in range(4):
        cols = slice(512 * tg, 512 * tg + 512)
        xr = [C.xnT.r[4 * k + tg] for k in range(8)]
        for (nch, c0, dst, gcol, ssb) in ((3, 0, cqn, 0, 6), (2, 384, ckvn, 3, 7)):
            ss = C.PS[ssb]
            for ch_ in range(nch):
                ps = C.PS[ch_ % 2]
                pairs = [(win[:, k, c0 + 128 * ch_:c0 + 128 * ch_ + 128], C.xnT[:, k, cols]) for k in range(8)]
                P.mm_group(ps[:, :], pairs, win.r + xr, ps.r)
                P.copy("dve", cq_t[ch_][:, :], ps[:, :], ps.r, cq_t[ch_].r)
                sq = C.sq[ch_ % 2]
                P.act(sq[:, :], ps[:, :], AF.Square, ps.r, sq.r)
                P.mm(ss[:, :], cbs(C, CB_ONES), sq[:, :], start=(ch_ == 0), stop=(ch_ == nch - 1),
                     reads=sq.r + C.cb.r, writes=ss.r)
            P.act(C.rsb[:, :], ss[:, :], AF.Sqrt, ss.r, C.rsb.r, bias=C.epsb[:, 0:1], scale=1.0 / (128 * nch))
            rstd = C.rstd[0]
            P.recip(rstd[:, :], C.rsb[:, :], C.rsb.r, rstd.r)
            for ch_ in range(nch):
                P.stt(dst[:, ch_, cols], cq_t[ch_][:, :], qng[:, gcol + ch_:gcol + ch_ + 1], rstd[:, :],
                      ALU.mult, ALU.mult, cq_t[ch_].r + rstd.r + qng.r, [dst.r[4 * ch_ + tg]])
        psA = C.PS[2]
        psB = C.PS[3]
        P.mm_group(psA[0:96, :], [(win[:, k, 576:672], C.xnT[:, k, cols]) for k in range(8)], win.r + xr, psA.r)
        P.mm_group(psB[0:96, :], [(win[:, k, 672:768], C.xnT[:, k, cols]) for k in range(8)], win.r + xr, psB.r)
        P.tt("dve", t1[R64, :], psA[R64, :], Ct[R64, cols], ALU.mult, psA.r + Ct.r, t1.r)
        P.tt("dve", t2[R64, :], psB[R64, :], St[R64, cols], ALU.mult, psB.r + St.r, t2.r)
        P.tt("pool", kr[R64, cols], t1[R64, :], t2[R64, :], ALU.add, t1.r + t2.r, [kr.r[tg]])
    P.barrier()
    A.reset(mark0)
    oT = C.xnT
    wh = [A.alloc("wh%d" % i, [3 * 192 + 2 * 128], BF16) for i in range(2)]
    qf = [A.alloc("qf%d" % i, [2048], BF16, nres=4) for i in range(2)]
    kf = [A.alloc("kf%d" % i, [2048], BF16, nres=4) for i in range(2)]
    Vh = [A.alloc("Vh%d" % i, [16, 65], BF16, nres=4) for i in range(2)]
    o_pair = [A.alloc("op%d" % i, [16, 128], BF16, nres=4) for i in range(2)]
    t1 = A.alloc("t1b", [512], F32)
    t2 = A.alloc("t2b", [512], F32)
    chains = []
    for c in range(2):
        ch = Chain()
        ch.PT = [A.alloc("PT%d%d" % (c, i), [512], BF16) for i in range(2)]
        ch.rden = A.alloc("rden%d" % c, [4], F32)
        ch.Zb = [C.PS[c], C.PS[2 + c]]
        ch.Ob = C.PS[4 + c]
        steps = []
        for QG in ([0, 3] if c == 0 else [1, 2]):
            nk = 4 * QG + 4
            for jb in range(nk):
                r = jb - 4 * QG
                ta = 4 * QG + max(0, r)
                steps.append(dict(QG=QG, jb=jb, ta=ta, tb=4 * QG + 3, first=(jb == 0), last=(jb == nk - 1),
                                  g=0, mask=None, negmask=(cbs(C, CB_NEG_INCL) if r >= 0 else None)))
        ch.steps = steps
        chains.append(ch)
    for s_ in range(2):
        P.memset("pool", Vh[s_][:, :, 64:65], 1.0, Vh[s_].r)

    def head_units(h):
        s_ = h % 2
        w = wh[s_]
        wq_ = w[:, 0:576].rearrange("p (a b) -> p a b", b=192)
        wkv = w[:, 576:832].rearrange("p (a b) -> p a b", b=128)
        units = []
        units.append(lambda: P.dma("pool", w[:, :], C.d_mlawh[h], C.usem[s_], writes=w.r))
        ucnt = [0]

        def q_unit(tg):
            def f():
                cols = slice(512 * tg, 512 * tg + 512)
                psA = C.PS[6]
                psB = C.PS[7]
                cr = [cqn.r[4 * kc + tg] for kc in range(3)]
                P.mm_group(psA[0:96, :], [(wq_[:, kc, 0:96], cqn[:, kc, cols]) for kc in range(3)], w.r + cr, psA.r)
                P.mm_group(psB[0:96, :], [(wq_[:, kc, 96:192], cqn[:, kc, cols]) for kc in range(3)], w.r + cr, psB.r)
                P.ts("dve", qf[s_][0:64, cols], psA[0:64, :], SC, None, ALU.mult, None, psA.r, [qf[s_].r[tg]])
                P.stt(t1[R64, :], psA[R64, :], SC, Ct[R64, cols], ALU.mult, ALU.mult, psA.r + Ct.r, t1.r)
                P.stt(t2[R64, :], psB[R64, :], SC, St[R64, cols], ALU.mult, ALU.mult, psB.r + St.r, t2.r)
                P.tt("pool", qf[s_][R64, cols], t1[R64, :], t2[R64, :], ALU.add, t1.r + t2.r, [qf[s_].r[tg]])
            return f

        def k_unit(tg):
            def f():
                cols = slice(512 * tg, 512 * tg + 512)
                ps = C.PS[6 + tg % 2]
                cr = [ckvn.r[4 * kc + tg] for kc in range(2)]
                P.mm_group(ps[0:64, :], [(wkv[:, kc, 0:64], ckvn[:, kc, cols]) for kc in range(2)], w.r + cr, ps.r)
                P.copy("dve", kf[s_][0:64, cols], ps[0:64, :], ps.r, [kf[s_].r[tg]])
                P.copy("pool", kf[s_][R64, cols], kr[R64, cols], [kr.r[tg]], [kf[s_].r[tg]])
            return f

        def v_unit(vb):
            def f():
                ps = C.PS[6 + vb % 2]
                for jj in range(8):
                    tile = 8 * vb + jj
                    tcols = slice(128 * tile, 128 * tile + 128)
                    pairs = [(ckvn[:, kc, tcols], wkv[:, kc, 64:128]) for kc in range(2)]
                    P.mm_group(ps[:, jj * 64:(jj + 1) * 64], pairs,
                               w.r + [ckvn.r[4 * kc + tile // 4] for kc in range(2)], ps.r)
                P.copy("dve", Vh[s_][:, 8 * vb:8 * vb + 8, 0:64],
                       ps[:, :].rearrange("p (a b) -> p a b", b=64), ps.r, [Vh[s_].r[2 * vb], Vh[s_].r[2 * vb + 1]])
            return f

        for tg in range(4):
            units.append(k_unit(tg))
            units.append(q_unit(tg))
        units.append(v_unit(0))
        units.append(v_unit(1))
        return units

    for u in head_units(0):
        u()
    for h in range(16):
        s_ = h % 2
        ps_ = (h // 2) % 2
        for ch in chains:
            ch.kT = (lambda s_: (lambda g, jb: (kf[s_][0:96, 128 * jb:128 * jb + 128], [kf[s_].r[jb // 4]])))(s_)
            ch.qT = (lambda s_: (lambda g, t0, n: (qf[s_][0:96, t0:t0 + n], [qf[s_].r[t0 // 512]])))(s_)
            ch.V = (lambda s_: (lambda g, jb: (Vh[s_][:, jb, :], [Vh[s_].r[jb // 4]])))(s_)
            ch.o_dst4 = (lambda hh, ps_: (lambda QG: (o_pair[ps_][:, 4 * QG:4 * QG + 4, 64 * hh:64 * hh + 64],
                                                      [o_pair[ps_].r[QG]])))(h % 2, ps_)
        bg = deque(head_units(h + 1)) if h < 15 else deque()
        emit_softmax_attention(C, chains, bg, bg_every=1)
        if h % 2 == 1:
            emit_transposes(C, o_pair[ps_], oT, h // 2)
    P.barrier()
    A.reset(mark0)
    emit_attn_out_proj(C, A, lambda dc: C.d_mlawo[dc], 8, oT, 4 * l + 1)
    P.barrier()


DIL_W = (256, 640, 2048)
DIL_OFF = (0, 256, 896)
DIL_BACK = (1, 4, 15)


def emit_dil_layer(C, l):
    P = C.P
    A = C.A
    A.reset()
    oT = A.alloc("oT", [4, 2048], BF16, nres=16)
    mark = A.off
    valid = A.alloc("valid", [2944], BF16)
    wq = [A.alloc("wq%d" % i, [8, 384], BF16) for i in range(2)]
    qT = [A.alloc("qT%d" % g, [2048], BF16, nres=4) for g in range(3)]
    kT = [A.alloc("kT%d" % g, [2048], BF16, nres=4) for g in range(3)]
    Vv = [A.alloc("V%d" % g, [16, 130], BF16, nres=4) for g in range(3)]
    M = [A.alloc("M%d" % hh, [2944], BF16, nres=3) for hh in range(2)]
    stage = [A.alloc("stage0", [1024], F32)] * 2
    scnt = [0]
    o_pair = A.alloc("op", [16, 128], BF16, nres=4)
    for g in range(3):
        for hh in range(2):
            P.memset("pool", Vv[g][:, :, 65 * hh + 64:65 * hh + 65], 1.0, Vv[g].r)
    chains = []
    for c in range(2):
        ch = Chain()
        ch.PT = [A.alloc("PT%d%d" % (c, i), [512], BF16) for i in range(3)]
        ch.rden = A.alloc("rden%d" % c, [4], F32)
        ch.Zb = [C.PS[c], C.PS[2 + c], C.PS[6 + c]]
        ch.Ob = C.PS[4 + c]
        steps = []
        for QG in range(4):
            lst = []
            for g in range(3):
                for jb in range(max(0, 4 * QG - DIL_BACK[g]), 4 * QG + 4):
                    ta = max(jb, 4 * QG)
                    tb = min(jb + DIL_BACK[g], 4 * QG + 3)
                    if tb < ta:
                        continue
                    x0 = DIL_OFF[g] + (ta - jb) * 128
                    N = (tb - ta + 1) * 128
                    lst.append(dict(QG=QG, jb=jb, ta=ta, tb=tb, first=False, last=False, g=g,
                                    mask=(M[c][:, x0:x0 + N], [M[c].r[g]], N)))
            lst[0]["first"] = True
            lst[-1]["last"] = True
            steps += lst
        ch.steps = steps
        chains.append(ch)
    wissued = [0]

    def issue_w(idx):
        if idx >= 12 or idx < wissued[0]:
            return
        wissued[0] = idx + 1
        w_ = wq[idx % 2]
        P.dma("pool", w_[:, :, :], C.d_dilqkv[idx].rearrange("p (a b) -> p a b", b=384),
              C.usem[idx % 2], writes=w_.r)

    for p in range(4):
        issue_w(3 * p)
        issue_w(3 * p + 1)
        if p == 0:
            P.dma("pool", valid[:, :], C.d_dilvalid, C.wsem[1], writes=valid.r)
            emit_norm(C, 4 * l + 0)
        for hh in range(2):
            for g in range(3):
                for c0 in range(0, DIL_W[g], 1024):
                    W = min(1024, DIL_W[g] - c0)
                    o = DIL_OFF[g] + c0
                    stg = stage[scnt[0] % 2]
                    scnt[0] += 1
                    P.dma("sp", stg[:, 0:W], C.d_dilslab[g * 8 + 2 * p + hh][:, c0:c0 + W], C.wsem[scnt[0] % 2],
                          writes=stg.r)
                    P.act(stg[:, 0:W], stg[:, 0:W], AF.Exp, stg.r, stg.r)
                    P.tt("pool", M[hh][:, o:o + W], stg[:, 0:W], valid[:, o:o + W], ALU.mult,
                         stg.r + valid.r, [M[hh].r[g]])
        for g in range(3):
            issue_w(3 * p + g)
            w = wq[(3 * p + g) % 2]
            u = 0
            for tg in range(4):
                cols = slice(512 * tg, 512 * tg + 512)
                xr = [C.xnT.r[4 * k + tg] for k in range(8)]
                for which in range(2):
                    ps = C.PS[6 + u % 2]
                    u += 1
                    pairs = [(w[:, k, 128 * which:128 * which + 128], C.xnT[:, k, cols]) for k in range(8)]
                    P.mm_group(ps[:, :], pairs, w.r + xr, ps.r)
                    if which == 0:
                        P.ts("dve", qT[g][:, cols], ps[:, :], 0.125, None, ALU.mult, None, ps.r, [qT[g].r[tg]])
                    else:
                        P.copy("dve", kT[g][:, cols], ps[:, :], ps.r, [kT[g].r[tg]])
                ps = C.PS[6 + u % 2]
                u += 1
                for jj in range(4):
                    tile = 4 * tg + jj
                    tcols = slice(128 * tile, 128 * tile + 128)
                    pairs = [(C.xnT[:, k, tcols], w[:, k, 256:384]) for k in range(8)]
                    P.mm_group(ps[:, jj * 128:(jj + 1) * 128], pairs, w.r + xr, ps.r)
                for hh in range(2):
                    src = ps[:, :].rearrange("p (a b) -> p a b", b=128)[:, :, 64 * hh:64 * hh + 64]
                    P.copy("dve", Vv[g][:, 4 * tg:4 * tg + 4, 65 * hh:65 * hh + 64], src,
                           ps.r, [Vv[g].r[tg]])
            issue_w(3 * p + g + 2)
            if g == 0 and p > 0:
                emit_transposes(C, o_pair, oT, p - 1)
        for c, ch in enumerate(chains):
            rows = slice(64 * c, 64 * c + 64)
            ch.kT = (lambda rows: (lambda g, jb: (kT[g][rows, 128 * jb:128 * jb + 128], [kT[g].r[jb // 4]])))(rows)
            ch.qT = (lambda rows: (lambda g, t0, n: (qT[g][rows, t0:t0 + n],
                                                     [qT[g].r[i] for i in range(t0 // 512, (t0 + n - 1) // 512 + 1)])))(rows)
            ch.V = (lambda c: (lambda g, jb: (Vv[g][:, jb, 65 * c:65 * c + 65], [Vv[g].r[jb // 4]])))(c)
            ch.o_dst4 = (lambda c: (lambda QG: (o_pair[:, 4 * QG:4 * QG + 4, 64 * c:64 * c + 64], [o_pair.r[QG]])))(c)
        emit_softmax_attention(C, chains, deque(), look=2)
    emit_transposes(C, o_pair, oT, 3)
    P.barrier()
    A.reset(mark)
    emit_attn_out_proj(C, A, lambda dc: C.d_dilwo[dc], 4, oT, 4 * l + 1)
    P.barrier()


def build_nc(stages):
    nc = bass.Bass("TRN2", target_bir_lowering=False)
    C = Ctx()
    C.nc = nc

    def din(name, shape, dt=F32):
        return nc.dram_tensor(name, list(shape), dt, kind="ExternalInput").ap()

    d_xT = din("xT", [D, S])
    d_cst = din("cst", [128, 1024])
    d_gn = din("gains", [128, 128])
    d_cvp = din("convp", [128, 704])
    C.d_wup = din("wup", [4, NPAIR, 128, 2048])
    C.d_wdn = din("wdn", [4, 8, 128, DFF])
    C.d_sbqkv = din("sbqkv", [2, 8, 128, 3072])
    C.d_sbwo = din("sbwo", [2, 8, 128, 1024])
    C.d_pos = din("pos", [1, S], I32)
    d_cf = din("cf", [128, 8])
    C.d_mlawin = din("mlawin", [128, 8 * 768])
    C.d_mlang = din("mlang", [128, 8])
    C.d_mlawh = din("mlawh", [16, 128, 832])
    C.d_mlawo = din("mlawo", [8, 128, 1024])
    C.d_dilqkv = din("dilqkv", [12, 128, 3072])
    C.d_dilwo = din("dilwo", [8, 128, 512])
    C.d_dilslab = din("dilslab", [24, 128, 2048])
    C.d_dilvalid = din("dilvalid", [128, 2944])
    d_out = nc.dram_tensor("outT", [D, S], F32, kind="ExternalOutput").ap()

    with ExitStack() as es:
        P = Prog(nc, es)
        C.P = P
        C.xT = P.sbuf("xT_sb", [128, 8, S], F32, nres=32)
        C.xnT = P.sbuf("xnT_sb", [128, 8, S], BF16, nres=32)
        C.cb = P.sbuf("cb", [128, 1024], BF16)
        C.gn = P.sbuf("gn", [128, 128], F32)
        C.cvp = P.sbuf("cvp", [128, 704], F32)
        C.sq = [P.sbuf("sq%d" % i, [128, 512], BF16) for i in range(2)]
        C.rsb = P.sbuf("rsb", [128, 512], F32)
        C.rsb2 = [C.rsb, C.rsb]
        C.rstd = [P.sbuf("rstd%d" % i, [128, 512], F32) for i in range(2)]
        C.cf = P.sbuf("cf_sb", [128, 8], F32)
        C.epsb = P.sbuf("epsb", [128, 1], F32)
        C.oneb = P.sbuf("oneb", [128, 1], F32)
        arena = P.sbuf("arena", [128, ARENA_F32], F32)
        C.A = Arena(arena, ARENA_F32)
        C.PS = [P.psum("ps%d" % i, [128, 512], F32) for i in range(8)]
        C.wsem = [P.dsem("wsem%d" % i) for i in range(2)]
        C.usem = [P.dsem("usem%d" % i) for i in range(2)]
        s_in = P.dsem("s_in")
        s_x = [P.dsem("s_x%d" % i) for i in range(2)]
        s_out = P.dsem("s_out")

        P.dma("pool", C.cb[:, :], d_cst, s_in, writes=C.cb.r)
        P.dma("sp", C.gn[:, :], d_gn, s_in, writes=C.gn.r)
        P.dma("sp", C.cvp[:, :], d_cvp, s_in, writes=C.cvp.r)
        P.dma("sp", C.cf[:, :], d_cf, s_in, writes=C.cf.r)
        P.memset("pool", C.epsb[:, :], EPS, C.epsb.r)
        P.memset("pool", C.oneb[:, :], 1.0, C.oneb.r)
        for k in range(8):
            P.dma("sp", C.xT[:, k, :], d_xT[128 * k:128 * k + 128, :], s_x[k % 2],
                  writes=[C.xT.r[4 * k + tg] for tg in range(4)])

        for st in stages:
            kind, l = st
            if kind == "norm":
                emit_norm(C, l)
            elif kind == "ffn":
                emit_ffn(C, l)
            elif kind == "mix":
                if l % 3 == 0:
                    emit_sb_layer(C, l, l // 3)
                elif l % 3 == 1:
                    emit_dil_layer(C, l)
                else:
                    emit_mla_layer(C, l)
        P.barrier()
        fin = Res("fin")
        for k in range(8):
            P.dma("sp", d_out[128 * k:128 * k + 128, :], C.xT[:, k, :], s_out,
                  reads=[C.xT.r[4 * k + tg] for tg in range(4)], writes=[fin])
        P.op("sp", None, reads=[fin])
        P.emit()
        C.stats = dict(ecnt=dict(P.ecnt), n_wait=P.n_wait)
    return nc, C


def lhsT_stream_layout(W):
    K = W.shape[0]
    nk = K // 128
    return np.ascontiguousarray(W.reshape(nk, 128, 8, 128).transpose(2, 1, 0, 3).reshape(8, 128, K))


def prep_shared(inp):
    f = lambda a: np.asarray(a, dtype=np.float32)
    out = {}
    i = np.arange(128)
    cst = np.zeros((128, 1024), np.float32)
    cst[:, 0:128] = np.eye(128)
    cst[:, 128:256] = 1.0
    cst[:, 256:384] = -1.0 * (i[:, None] >= i[None, :])
    cst[:, 384:512] = -1.0 * (i[:, None] < i[None, :])
    cst[:, 512:640] = (i[:, None] < i[None, :])
    cst[:, 640:768] = (i[:, None] <= i[None, :])
    cst[:, 768:896] = NEG_BIG * (i[:, None] >= i[None, :])
    cst[:, 896:1024] = NEG_BIG * (i[:, None] > i[None, :])
    out["cst"] = cst
    ng = f(inp["norm_gains"])
    out["gains"] = np.ascontiguousarray(ng.reshape(16, 8, 128).transpose(2, 0, 1).reshape(128, 128))
    cw = f(inp["ffn_conv_w"])
    cbias = f(inp["ffn_conv_b"])
    cv = np.concatenate([cw, cbias[:, None, :]], axis=1)
    cv = cv.reshape(4, 4, 44, 128).transpose(3, 0, 2, 1)
    out["convp"] = np.ascontiguousarray(cv.reshape(128, 704))
    wu = f(inp["ffn_w_up"])
    g = wu[:, :, :DFF].reshape(4, 8, 128, NPAIR, 128)
    u = wu[:, :, DFF:].reshape(4, 8, 128, NPAIR, 128)
    gu = np.stack([g, u], axis=4)
    out["wup"] = np.ascontiguousarray(gu.transpose(0, 3, 2, 1, 4, 5).reshape(4, NPAIR, 128, 2048))
    wd = f(inp["ffn_w_down"])
    out["wdn"] = np.stack([lhsT_stream_layout(wd[l]) for l in range(4)])
    sq = f(inp["sb_w_qkv"])
    t = sq.reshape(2, 8, 128, 3, 8, 128)
    out["sbqkv"] = np.ascontiguousarray(t.transpose(0, 4, 2, 1, 3, 5).reshape(2, 8, 128, 3072))
    so = f(inp["sb_w_o"])
    out["sbwo"] = np.stack([lhsT_stream_layout(so[jj]) for jj in range(2)])
    cfm = np.zeros((128, 8), np.float32)
    freqs = (np.float32(10000.0) ** (-np.arange(16, dtype=np.float32) / np.float32(16))).astype(np.float32)
    cfm[64:80, 0] = freqs
    cfm[80:96, 0] = freqs
    cfm[64:80, 1] = -1.0
    cfm[80:96, 1] = 1.0
    out["cf"] = cfm
    wi = f(inp["mla_w_in"])[0]
    wi2 = np.concatenate([wi, wi[:, 576:640], wi[:, 656:672], wi[:, 640:656]], axis=1)
    out["mlawin"] = np.ascontiguousarray(wi2.reshape(8, 128, 768).transpose(1, 0, 2).reshape(128, 8 * 768))
    ng_ = np.zeros((128, 8), np.float32)
    ng_[:, 0:3] = f(inp["mla_q_norm"])[0].reshape(3, 128).T
    ng_[:, 3:5] = f(inp["mla_kv_norm"])[0].reshape(2, 128).T
    out["mlang"] = ng_
    wqb = f(inp["mla_w_qb"])[0]
    wkvb = f(inp["mla_w_kvb"])[0]
    whs = []
    for h in range(16):
        a = wqb[:, 96 * h:96 * h + 96]
        b = np.concatenate([wqb[:, 96 * h:96 * h + 64], wqb[:, 96 * h + 80:96 * h + 96],
                            wqb[:, 96 * h + 64:96 * h + 80]], axis=1)
        q2 = np.concatenate([a, b], axis=1).reshape(3, 128, 192).transpose(1, 0, 2).reshape(128, 576)
        kv = wkvb[:, 128 * h:128 * h + 128].reshape(2, 128, 128).transpose(1, 0, 2).reshape(128, 256)
        whs.append(np.concatenate([q2, kv], axis=1))
    out["mlawh"] = np.ascontiguousarray(np.stack(whs))
    out["mlawo"] = lhsT_stream_layout(f(inp["mla_w_o"])[0])
    dq = f(inp["dil_w_qkv"])[0]
    t = dq.reshape(8, 128, 3, 3, 4, 128)
    out["dilqkv"] = np.ascontiguousarray(t.transpose(4, 3, 1, 0, 2, 5).reshape(12, 128, 3072))
    out["dilwo"] = lhsT_stream_layout(f(inp["dil_w_o"])[0])
    rb = f(inp["rel_bias"])
    jk = np.arange(128)[:, None]
    xx = np.arange(2048)[None, :]
    delta = xx - jk
    dpos = np.maximum(delta, 0)
    dflt = np.maximum(dpos.astype(np.float32), np.float32(1.0))
    large = 16 + (np.log(dflt / np.float32(16)) / np.float32(math.log(2048 / 16)) * np.float32(16)).astype(np.int32)
    large = np.minimum(large, 31)
    bucket = np.where(dpos < 16, dpos, large)
    slab = np.zeros((24, 128, 2048), np.float32)
    for gh in range(24):
        slab[gh] = rb[bucket, gh]
    out["dilslab"] = slab
    valid = np.zeros((128, 2944), np.float32)
    for g, (W, r) in enumerate(((256, 1), (640, 4), (2048, 16))):
        dl = delta[:, :W]
        valid[:, DIL_OFF[g]:DIL_OFF[g] + W] = (dl >= 0) & (dl % r == 0) & (dl <= 128 * r)
    out["dilvalid"] = valid
    return out


ALL_STAGES = [("mix", 0), ("ffn", 0), ("mix", 1), ("ffn", 1), ("mix", 2), ("ffn", 2), ("mix", 3), ("ffn", 3)]
_CACHE = {}


def run(inputs, stages, n_cores=8, trace=False):
    key = tuple(stages)
    if key not in _CACHE:
        _CACHE[key] = build_nc(stages)
    nc, C = _CACHE[key]
    shared = prep_shared(inputs)
    x = np.asarray(inputs["x"], dtype=np.float32)
    in_maps = []
    for b in range(n_cores):
        m = dict(shared)
        m["xT"] = np.ascontiguousarray(x[b].T)
        m["pos"] = np.ascontiguousarray(np.asarray(inputs["positions"])[b][None, :].astype(np.int32))
        in_maps.append(m)
    res = run_bass_kernel_spmd(nc, in_maps, core_ids=list(range(n_cores)), trace=trace)
    out = np.stack([np.ascontiguousarray(r["outT"].T) for r in res.results])
    return out, res


def kernel(**inputs):
    out, _ = run(inputs, ALL_STAGES, 8)
    return out.astype(np.float32)
```

```python
import math
from collections import deque
from contextlib import ExitStack

import numpy as np
import concourse.bass as bass
import concourse.mybir as mybir
from concourse.bass_utils import run_bass_kernel_spmd

F32 = mybir.dt.float32
BF16 = mybir.dt.bfloat16
I32 = mybir.dt.int32
AF = mybir.ActivationFunctionType
ALU = mybir.AluOpType

S = 2048
D = 1024
DFF = 2816
NPAIR = 22
EPS = 1e-6
ENGS = ("pe", "act", "dve", "pool", "sp")
import os as _os
SAME_ENGINE_SYNC = _os.environ.get("NOSES", "") == ""


class Res:
    __slots__ = ("name", "w", "rs", "excl")

    def __init__(self, name=""):
        self.name = name
        self.w = None
        self.rs = {}
        self.excl = False


class Buf:
    def __init__(self, t, nres=1, name=""):
        self.t = t
        self.r = [Res("%s.%d" % (name, i)) for i in range(nres)]

    def __getitem__(self, idx):
        return self.t[idx]


class Prog:
    def __init__(self, nc, es, same_engine_sync=SAME_ENGINE_SYNC):
        self.nc = nc
        self.es = es
        self.ops = {e: [] for e in ENGS}
        self.ecnt = {e: 0 for e in ENGS}
        self.esem = {}
        for e in ("pe", "act", "dve", "pool"):
            self.esem[e] = es.enter_context(nc.semaphore("es_" + e))
        self.known = {e: {} for e in ENGS}
        self.dcnt = {}
        self.same_engine_sync = same_engine_sync
        self.n_wait = 0

    def sbuf(self, name, shape, dtype, nres=1):
        t = self.es.enter_context(self.nc.sbuf_tensor(name, list(shape), dtype))
        return Buf(t, nres, name)

    def psum(self, name, shape, dtype, nres=1):
        t = self.es.enter_context(self.nc.psum_tensor(name, list(shape), dtype))
        b = Buf(t, nres, name)
        for r in b.r:
            r.excl = True
        return b

    def dsem(self, name):
        s = self.es.enter_context(self.nc.semaphore(name))
        self.dcnt[s] = 0
        return s

    def op(self, eng, fn, reads=(), writes=(), inc=True, dsem=None):
        waits = {}
        kn = self.known[eng]
        own = self.esem.get(eng)
        if any(r.excl for r in reads):
            writes = list(writes) + [r for r in reads if r.excl and r not in writes]
            reads = [r for r in reads if not r.excl]

        def need(m):
            if m is None:
                return
            sem, val = m
            if sem in self.dcnt:
                val = self.dcnt[sem]
            elif sem is own:
                if eng == "pe" or not self.same_engine_sync:
                    return
            if kn.get(sem, 0) >= val:
                return
            if waits.get(sem, 0) < val:
                waits[sem] = val

        for r in reads:
            need(r.w)
        for w in writes:
            need(w.w)
            for m in w.rs.items():
                need(m)
        for sem, val in waits.items():
            kn[sem] = val
        if dsem is not None:
            self.dcnt[dsem] += 16
            marker = (dsem, self.dcnt[dsem])
            incspec = (dsem, 16)
        elif eng == "sp":
            marker = None
            incspec = None
        elif inc:
            self.ecnt[eng] += 1
            marker = (own, self.ecnt[eng])
            incspec = (own, 1)
        else:
            marker = (own, self.ecnt[eng] + 1)
            incspec = None
        if marker is not None:
            for r in reads:
                if r.rs.get(marker[0], 0) < marker[1]:
                    r.rs[marker[0]] = marker[1]
            for w in writes:
                w.w = marker
                w.rs = {}
        self.n_wait += len(waits)
        self.ops[eng].append((list(waits.items()), fn, incspec))

    def barrier(self):
        targets = [(self.esem[e], self.ecnt[e]) for e in self.esem if self.ecnt[e] > 0]
        targets += [(s, c) for s, c in self.dcnt.items() if c > 0]
        for eng in ENGS:
            waits = []
            for sem, val in targets:
                if sem is self.esem.get(eng):
                    continue
                if self.known[eng].get(sem, 0) >= val:
                    continue
                self.known[eng][sem] = val
                waits.append((sem, val))
            if waits:
                self.ops[eng].append((waits, None, None))

    def emit(self):
        nc = self.nc
        with nc.Block() as block:
            def run(engname):
                def body(e):
                    for waits, fn, incspec in self.ops[engname]:
                        for sem, val in waits:
                            e.wait_ge(sem, val)
                        if fn is None:
                            continue
                        ins = fn(e)
                        if incspec is not None:
                            ins.then_inc(incspec[0], incspec[1])
                return body

            block.tensor(run("pe"))
            block.scalar(run("act"))
            block.vector(run("dve"))
            block.gpsimd(run("pool"))
            block.sync(run("sp"))

    def dma(self, eng, out, in_, dsem, reads=(), writes=()):
        if eng == "pool":
            self.op(eng, lambda e: e.dma_start(out=out, in_=in_, max_dma_last_dim=4096), reads, writes, dsem=dsem)
        else:
            self.op(eng, lambda e: e.dma_start(out=out, in_=in_), reads, writes, dsem=dsem)

    def mm(self, out, lhsT, rhs, start, stop, reads, writes, inc=True, **kw):
        self.op("pe", lambda e: e.matmul(out, lhsT, rhs, start=start, stop=stop, **kw),
                reads, writes, inc=inc)

    def mm_group(self, out, pairs, reads, writes, **kw):
        n = len(pairs)
        for i, (l, r) in enumerate(pairs):
            self.mm(out, l, r, start=(i == 0), stop=(i == n - 1),
                    reads=reads, writes=writes, inc=(i == n - 1), **kw)

    def act(self, out, in_, func, reads, writes, **kw):
        self.op("act", lambda e: e.activation(out=out, in_=in_, func=func, **kw), reads, writes)

    def tt(self, eng, out, in0, in1, op, reads, writes):
        self.op(eng, lambda e: e.tensor_tensor(out=out, in0=in0, in1=in1, op=op), reads, writes)

    def ts(self, eng, out, in0, s1, s2, op0, op1, reads, writes):
        if s2 is None:
            self.op(eng, lambda e: e.tensor_scalar(out=out, in0=in0, scalar1=s1, scalar2=None, op0=op0),
                    reads, writes)
        else:
            self.op(eng, lambda e: e.tensor_scalar(out=out, in0=in0, scalar1=s1, scalar2=s2,
                                                   op0=op0, op1=op1), reads, writes)

    def stt(self, out, in0, scalar, in1, op0, op1, reads, writes):
        self.op("dve", lambda e: e.scalar_tensor_tensor(out=out, in0=in0, scalar=scalar, in1=in1,
                                                        op0=op0, op1=op1), reads, writes)

    def copy(self, eng, out, in_, reads, writes):
        if eng == "act":
            self.act(out, in_, AF.Copy, reads, writes)
        else:
            self.op(eng, lambda e: e.tensor_copy(out=out, in_=in_), reads, writes)

    def recip(self, out, in_, reads, writes):
        self.op("dve", lambda e: e.reciprocal(out=out, in_=in_), reads, writes)

    def memset(self, eng, out, val, writes):
        self.op(eng, lambda e: e.memset(out, val), (), writes)


def split_mm_group(P, out, pairs, reads, writes, chunk=2):
    units = []
    n = len(pairs)
    for c0 in range(0, n, chunk):
        def f(c0=c0):
            for i in range(c0, min(n, c0 + chunk)):
                P.mm(out, pairs[i][0], pairs[i][1], start=(i == 0), stop=(i == n - 1),
                     reads=reads, writes=writes, inc=(i == n - 1))
        units.append(f)
    return units


class Arena:
    def __init__(self, buf, nf32):
        self.buf = buf
        self.n = nf32
        self.off = 0

    def reset(self, to=0):
        self.off = to

    def alloc(self, name, shape, dtype, nres=1):
        n = 1
        for s_ in shape:
            n *= s_
        esz = 4 if dtype in (F32, I32) else 2
        nf = (n * esz + 3) // 4
        nf = (nf + 3) // 4 * 4
        assert self.off + nf <= self.n, "arena overflow %s: %d + %d > %d" % (name, self.off, nf, self.n)
        ap = self.buf.t[:, self.off:self.off + nf]
        self.off += nf
        if dtype != F32:
            ap = ap.bitcast(dtype)
        ap = ap[:, 0:n]
        if len(shape) == 2:
            ap = ap.rearrange("p (a b) -> p a b", b=shape[1])
        elif len(shape) == 3:
            ap = ap.rearrange("p (a b c) -> p a b c", b=shape[1], c=shape[2])
        return Buf(ap, nres, name)


CB_IDENT, CB_ONES, CB_TRI_INCL, CB_TRI_REST, CB_M_STRICT, CB_M_INCL, CB_NEG_STRICT, CB_NEG_INCL = range(8)
NEG_BIG = -30000.0
ARENA_F32 = 25216


class Ctx:
    pass


def cbs(C, i):
    return C.cb[:, 128 * i:128 * (i + 1)]


def emit_norm(C, gidx):
    P = C.P
    for tg in range(4):
        cols = slice(512 * tg, 512 * tg + 512)
        ss = C.PS[6 + tg % 2]
        for k in range(8):
            sq = C.sq[k % 2]
            P.act(sq[:, :], C.xT[:, k, cols], AF.Square, [C.xT.r[4 * k + tg]], sq.r)
            P.mm(ss[:, :], cbs(C, CB_ONES), sq[:, :], start=(k == 0), stop=(k == 7),
                 reads=sq.r + C.cb.r, writes=ss.r)
        rsb = C.rsb
        P.act(rsb[:, :], ss[:, :], AF.Sqrt, ss.r, rsb.r, bias=C.epsb[:, 0:1], scale=1.0 / D)
        rstd = C.rstd[tg % 2]
        P.recip(rstd[:, :], rsb[:, :], rsb.r, rstd.r)
        for k in range(8):
            P.stt(C.xnT[:, k, cols], C.xT[:, k, cols], C.gn[:, gidx * 8 + k:gidx * 8 + k + 1], rstd[:, :],
                  ALU.mult, ALU.mult, [C.xT.r[4 * k + tg]] + rstd.r + C.gn.r, [C.xnT.r[4 * k + tg]])


class OutProj:
    def __init__(self, C, bufs, wsrc, nk, rhs_fn, tgs, gidx):
        self.C, self.bufs, self.wsrc, self.nk, self.rhs_fn, self.tgs, self.gidx = C, bufs, wsrc, nk, rhs_fn, tgs, gidx
        self.loaded = set()

    def load(self, dc):
        if dc in self.loaded or dc >= 8:
            return
        self.loaded.add(dc)
        C = self.C
        w = self.bufs[0][dc % 2]
        C.P.dma("pool", w[:, :], self.wsrc(dc), C.wsem[dc % 2], writes=w.r)

    def run_mm(self, fofs=0, ssb=2, bg=None):
        C, nk, rhs_fn, tgs, gidx = self.C, self.nk, self.rhs_fn, self.tgs, self.gidx
        P = C.P
        wbuf, fT, tmpb = self.bufs
        self.fofs = fofs
        self.ssb = ssb
        self.load(0)
        for dc in range(8):
            w = wbuf[dc % 2]
            for ti, tg in enumerate(tgs):
                ps = C.PS[ti]
                pairs = []
                reads = list(w.r)
                for kc in range(nk):
                    ap, rr = rhs_fn(kc, ti)
                    pairs.append((w[:, kc * 128:(kc + 1) * 128], ap))
                    reads += rr
                P.mm_group(ps[:, :], pairs, reads, ps.r)
                if ti == 0:
                    self.load(dc + 1)
                fo = fofs + ti * 512
                P.copy("dve", fT[:, dc, fo:fo + 512], ps[:, :], ps.r, [fT.r[(fofs // 512 + ti) * 8 + dc]])
                sq = C.sq[ti]
                P.act(sq[:, :], ps[:, :], AF.Square, ps.r, sq.r)
                ss = C.PS[ssb + ti]
                P.mm(ss[:, :], cbs(C, CB_ONES), sq[:, :], start=(dc == 0), stop=(dc == 7),
                     reads=sq.r + C.cb.r, writes=ss.r)
                if bg:
                    bg.popleft()()
                    if dc >= 1 and bg:
                        bg.popleft()()

    def final_units(self, rstd_bufs=None):
        C, tgs, gidx = self.C, self.tgs, self.gidx
        P = C.P
        wbuf, fT, tmpb = self.bufs
        fofs, ssb = self.fofs, self.ssb
        rstds = rstd_bufs or C.rstd
        units = []

        def rs(ti):
            def f():
                ss = C.PS[ssb + ti]
                rsb = C.rsb2[ti]
                P.act(rsb[:, :], ss[:, :], AF.Sqrt, ss.r, rsb.r, bias=C.epsb[:, 0:1], scale=1.0 / D)
                P.recip(rstds[ti][:, :], rsb[:, :], rsb.r, rstds[ti].r)
            return f

        def fin(dc, ti, tg):
            def f():
                cols = slice(512 * tg, 512 * tg + 512)
                rstd = rstds[ti]
                tmp = tmpb[ti]
                fo = fofs + ti * 512
                fr = [fT.r[(fofs // 512 + ti) * 8 + dc]]
                P.tt("pool" if ti == 0 else "dve", tmp[:, :], fT[:, dc, fo:fo + 512], rstd[:, :], ALU.mult,
                     fr + rstd.r, tmp.r)
                P.stt(C.xT[:, dc, cols], tmp[:, :], C.gn[:, gidx * 8 + dc:gidx * 8 + dc + 1], C.xT[:, dc, cols],
                      ALU.mult, ALU.add, [C.xT.r[4 * dc + tg]] + tmp.r + C.gn.r, [C.xT.r[4 * dc + tg]])
            return f

        for ti, tg in enumerate(tgs):
            units.append(rs(ti))
        for dc in range(8):
            for ti, tg in enumerate(tgs):
                units.append(fin(dc, ti, tg))
        return units

    def run(self):
        self.run_mm()
        for u in self.final_units():
            u()


def emit_attn_out_proj(C, A, wsrc, nk, oT, gidx):
    wbuf = [A.alloc("wo%d" % i, [nk * 128], BF16) for i in range(2)]
    fT = A.alloc("fT", [8, 2048], BF16, nres=32)
    tmpb = [A.alloc("tmp%d" % i, [512], F32) for i in range(2)]
    rstd2 = [A.alloc("rstdb%d" % i, [512], F32) for i in range(2)]
    ops = []
    for half in range(2):
        def rhs_fn(kc, ti, half=half):
            tg = 2 * half + ti
            return oT[:, kc, 512 * tg:512 * tg + 512], [oT.r[4 * kc + tg]]
        ops.append(OutProj(C, (wbuf, fT, tmpb), wsrc, nk, rhs_fn, [2 * half, 2 * half + 1], gidx))
    ops[0].run_mm(fofs=0, ssb=2)
    f0 = deque(ops[0].final_units())
    ops[1].run_mm(fofs=1024, ssb=6, bg=f0)
    while f0:
        f0.popleft()()
    for u in ops[1].final_units(rstd_bufs=rstd2):
        u()


def emit_out_proj(C, bufs, wsrc, nk, rhs_fn, tgs, gidx):
    OutProj(C, bufs, wsrc, nk, rhs_fn, tgs, gidx).run()


def alloc_outproj_bufs(C, nk, tmpb=None):
    A = C.A
    wbuf = [A.alloc("wo%d" % i, [nk * 128], BF16) for i in range(2)]
    fT = A.alloc("fT", [8, 1024], BF16, nres=16)
    if tmpb is None:
        tmpb = [A.alloc("tmp%d" % i, [512], F32) for i in range(2)]
    return wbuf, fT, tmpb


def emit_ffn(C, l):
    P = C.P
    A = C.A
    A.reset()
    mT = A.alloc("mT", [NPAIR, 1024], BF16, nres=NPAIR * 2)
    wup = [A.alloc("wup%d" % i, [8, 256], BF16) for i in range(2)]
    hs = [[A.alloc("hs%d%d" % (i, j), [514], F32) for j in range(2)] for i in range(2)]
    acc = [[A.alloc("acc%d%d" % (i, j), [512], F32) for j in range(2)] for i in range(2)]
    tails = A.alloc("tails", [44, 2], F32, nres=44)
    tmpx = A.alloc("tmpx", [512], F32)
    opb = alloc_outproj_bufs(C, NPAIR, tmpb=[tmpx, tmpx])
    nload = [0]

    def load_w(seq):
        if seq >= 2 * NPAIR or seq < nload[0]:
            return
        nload[0] = seq + 1
        i = seq % NPAIR
        w = wup[seq % 2]
        P.dma("pool", w[:, :, :], C.d_wup[l, i].rearrange("p (a b) -> p a b", b=256), C.usem[seq % 2], writes=w.r)

    def finish(u):
        half, i, t2, par = u
        ag, au = acc[par]
        P.act(ag[:, :], ag[:, :], AF.Silu, ag.r, ag.r)
        P.tt("pool", mT[:, i, 512 * t2:512 * t2 + 512], ag[:, :], au[:, :], ALU.mult,
             ag.r + au.r, [mT.r[2 * i + t2]])

    load_w(0)
    emit_norm(C, 4 * l + 2)
    cnt = 0
    pending = deque()
    for half in range(2):
        def rhs_fn(kc, ti):
            return mT[:, kc, 512 * ti:512 * ti + 512], [mT.r[2 * kc + ti]]
        op = OutProj(C, opb, lambda dc: C.d_wdn[l, dc], NPAIR, rhs_fn, [2 * half, 2 * half + 1], 4 * l + 3)
        prev = None
        for i in range(NPAIR):
            seq = half * NPAIR + i
            w = wup[seq % 2]
            for t2 in range(2):
                tg = 2 * half + t2
                cols = slice(512 * tg, 512 * tg + 512)
                par = cnt % 2
                pss = []
                for gu in range(2):
                    ps = C.PS[(2 * cnt + gu) % 8]
                    pairs = [(w[:, k, 128 * gu:128 * gu + 128], C.xnT[:, k, cols]) for k in range(8)]
                    P.mm_group(ps[:, :], pairs, w.r + [C.xnT.r[4 * k + tg] for k in range(8)], ps.r)
                    pss.append(ps)
                if t2 == 0:
                    load_w(seq + 1)
                    if i == NPAIR - 2:
                        op.load(0)
                cnt += 1
                for gu in range(2):
                    ci = i + NPAIR * gu
                    pbase = (l * 44 + ci) * 4
                    h = hs[par][gu]
                    a = acc[par][gu]
                    ps = pss[gu]
                    P.copy("act", h[:, 2:514], ps[:, :], ps.r, h.r)
                    P.act(a[:, :], ps[:, :], AF.Identity, ps.r + C.cvp.r, a.r,
                          bias=C.cvp[:, pbase + 3:pbase + 4], scale=C.cvp[:, pbase + 2:pbase + 3])
                for gu in range(2):
                    ci = i + NPAIR * gu
                    h = hs[par][gu]
                    if tg == 0:
                        P.memset("pool", h[:, 0:2], 0.0, h.r)
                    else:
                        P.copy("pool", h[:, 0:2], tails[:, ci, :], [tails.r[ci]], h.r)
                if tg < 3:
                    for gu in range(2):
                        ci = i + NPAIR * gu
                        h = hs[par][gu]
                        P.copy("pool", tails[:, ci, :], h[:, 512:514], h.r, [tails.r[ci]])
                if prev is not None:
                    finish(prev)
                if pending:
                    pending.popleft()()
                for tap in (1, 0):
                    for gu in range(2):
                        ci = i + NPAIR * gu
                        pbase = (l * 44 + ci) * 4
                        h = hs[par][gu]
                        a = acc[par][gu]
                        P.stt(a[:, :], h[:, tap:tap + 512], C.cvp[:, pbase + tap:pbase + tap + 1], a[:, :],
                              ALU.mult, ALU.add, h.r + a.r + C.cvp.r, a.r)
                prev = (half, i, t2, par)
        finish(prev)
        op.run_mm()
        pending = deque(op.final_units())
        pending.popleft()()
        pending.popleft()()
        if half == 1:
            while pending:
                pending.popleft()()
    P.barrier()


class Chain:
    pass


def emit_transposes(C, o_pair, oT, p, banks=(6, 7)):
    P = C.P
    for tb in range(4):
        ps = C.PS[banks[tb % 2]]
        psb = ps.t[:, :].bitcast(BF16)
        for j in range(4):
            tile = 4 * tb + j
            P.op("pe", (lambda o, i: (lambda e: e.transpose(o, i, cbs(C, CB_IDENT))))(
                psb[:, j * 128:(j + 1) * 128], o_pair[:, tile, :]),
                 [o_pair.r[tile // 4]] + C.cb.r, ps.r)
        P.copy("dve", oT[:, p, 512 * tb:512 * tb + 512], psb[:, 0:512], ps.r, [oT.r[4 * p + tb]])


def sb_steps():
    steps = []
    for QG in range(4):
        nk = 4 * QG + 4
        for jb in range(nk - 1, -1, -1):
            r = jb - 4 * QG
            ta = 4 * QG + max(0, r)
            steps.append(dict(QG=QG, jb=jb, ta=ta, tb=4 * QG + 3, diag=(jb if r >= 0 else None),
                              first=(jb == nk - 1), last=(jb == 0)))
    return steps


def emit_sb_attention(C, chains, bg):
    P = C.P
    steps = sb_steps()
    n = len(steps)

    def q0_of(k):
        st = steps[k]
        return (st["ta"] - 4 * st["QG"]) * 128

    def Zm(k):
        st = steps[k]
        q0 = q0_of(k)
        for ch in chains:
            Zb = ch.Zb[k % 2]
            kap, kr = ch.kT(st["jb"])
            qap, qr = ch.qT(st["ta"] * 128, 512 - q0)
            if st["diag"] is None:
                P.mm(Zb[:, q0:512], kap, qap, True, True, kr + qr, Zb.r)
            else:
                P.mm(Zb[:, q0:512], kap, qap, True, False, kr + qr, Zb.r, inc=False, skip_group_check=True)
                P.mm(Zb[:, q0:q0 + 128], cbs(C, CB_IDENT), cbs(C, CB_NEG_STRICT), False, True,
                     C.cb.r, Zb.r, skip_group_check=True)

    def Em(k):
        q0 = q0_of(k)
        for ch in chains:
            E = ch.E[k % 3]
            Zb = ch.Zb[k % 2]
            P.act(E[:, q0:512], Zb[:, q0:512], AF.Exp, Zb.r, E.r)

    def Lm(k):
        q0 = q0_of(k)
        for ch in chains:
            E = ch.E[k % 3]
            Lp = ch.Lp[k % 2]
            P.act(Lp[:, q0:512], E[:, q0:512], AF.Ln, E.r, Lp.r, bias=1.0, scale=1.0)

    def TRIm(k):
        st = steps[k]
        q0 = q0_of(k)
        for ch in chains:
            Lp = ch.Lp[k % 2]
            P.mm(ch.Ab[:, q0:512], cbs(C, CB_TRI_INCL), Lp[:, q0:512], st["first"], False,
                 Lp.r + C.cb.r, ch.Ab.r, skip_group_check=True)

    def Xm(k):
        q0 = q0_of(k)
        for ch in chains:
            P.act(ch.X[:, q0:512], ch.Ab[:, q0:512], AF.Exp, ch.Ab.r, ch.X.r)

    def RESTm(k):
        st = steps[k]
        q0 = q0_of(k)
        if st["last"]:
            return
        for ch in chains:
            Lp = ch.Lp[k % 2]
            P.mm(ch.Ab[:, q0:512], cbs(C, CB_TRI_REST), Lp[:, q0:512], False, False,
                 Lp.r + C.cb.r, ch.Ab.r, skip_group_check=True)

    def aTm(k):
        q0 = q0_of(k)
        for ch in chains:
            aT = ch.aT[k % 2]
            E = ch.E[k % 3]
            P.tt("dve", aT[:, q0:512], E[:, q0:512], ch.X[:, q0:512], ALU.mult, E.r + ch.X.r, aT.r)

    def Om(k):
        st = steps[k]
        for ci_, ch in enumerate(chains):
            aT = ch.aT[k % 2]
            vap, vr = ch.V(st["jb"])
            tiles = list(range(st["ta"], st["tb"] + 1))
            for ti_, t in enumerate(tiles):
                i = t - 4 * st["QG"]
                oc = ch.ocol + i * 64
                P.mm(ch.Ob[:, oc:oc + 64], aT[:, i * 128:(i + 1) * 128], vap,
                     start=(st["first"] and ti_ == 0 and ci_ == 0), stop=False, reads=aT.r + vr,
                     writes=ch.Ob.r, inc=(ti_ == len(tiles) - 1), skip_group_check=True)
        if st["last"]:
            for ch in chains:
                ap, rr = ch.o_dst4(st["QG"])
                src = ch.Ob[:, ch.ocol:ch.ocol + 256].rearrange("p (a b) -> p a b", b=64)
                P.copy("dve", ap, src, ch.Ob.r, rr)

    Zm(0)
    Em(0)
    Lm(0)
    TRIm(0)
    if n > 1:
        Zm(1)
        Em(1)
    if n > 2:
        Zm(2)
    for k in range(n):
        Xm(k)
        RESTm(k)
        if k + 1 < n:
            Lm(k + 1)
            TRIm(k + 1)
        aTm(k)
        if k + 2 < n:
            Em(k + 2)
        Om(k)
        if k + 3 < n:
            Zm(k + 3)
        nb_ = -(-len(bg) // max(1, n - 1 - k)) if bg else 0
        for _ in range(nb_):
            if bg:
                bg.popleft()()
    while bg:
        bg.popleft()()


def emit_sb_layer(C, l, j):
    P = C.P
    A = C.A
    A.reset()
    oT = A.alloc("oT", [8, 2048], BF16, nres=32)
    mark = A.off
    wq = [A.alloc("wq%d" % i, [8, 384], BF16) for i in range(2)]
    qT = [A.alloc("qT%d" % i, [2048], BF16, nres=4) for i in range(2)]
    kT = [A.alloc("kT%d" % i, [2048], BF16, nres=4) for i in range(2)]
    Vv = [A.alloc("V%d" % i, [2048], BF16, nres=4) for i in range(2)]
    o_pair = [A.alloc("op0", [16, 128], BF16, nres=4)] * 2
    chains = []
    for c in range(2):
        ch = Chain()
        ch.E = [A.alloc("E%d%d" % (c, i), [512], F32) for i in range(3)]
        ch.X = A.alloc("X%d" % c, [512], F32)
        ch.Lp = [A.alloc("Lp%d%d" % (c, i), [512], BF16) for i in range(2)]
        ch.aT = [A.alloc("aT%d%d" % (c, i), [512], BF16) for i in range(2)]
        ch.Zb = [C.PS[2 * c], C.PS[2 * c + 1]]
        ch.Ab = C.PS[4 + c]
        ch.Ob = C.PS[6]
        ch.ocol = 256 * c
        chains.append(ch)

    def proj_units(p):
        s_ = p % 2
        w = wq[s_]
        units = []
        units.append(lambda: P.dma("pool", w[:, :, :], C.d_sbqkv[j, p].rearrange("p (a b) -> p a b", b=384),
                                   C.usem[s_], writes=w.r))
        ucnt = [0]

        def qk_units(which, tg):
            ps = C.PS[7]
            ucnt[0] += 1
            cols = slice(512 * tg, 512 * tg + 512)
            pairs = [(w[:, k, 128 * which:128 * which + 128], C.xnT[:, k, cols]) for k in range(8)]
            us = split_mm_group(P, ps[:, :], pairs, w.r + [C.xnT.r[4 * k + tg] for k in range(8)], ps.r, 2)
            if which == 0:
                us.append(lambda: P.ts("dve", qT[s_][:, cols], ps[:, :], 0.125, None, ALU.mult, None, ps.r,
                                       [qT[s_].r[tg]]))
            else:
                us.append(lambda: P.copy("dve", kT[s_][:, cols], ps[:, :], ps.r, [kT[s_].r[tg]]))
            return us

        def v_units(vb):
            ps = C.PS[7]
            ucnt[0] += 1
            us = []
            for jj in range(4):
                tile = 4 * vb + jj
                tcols = slice(128 * tile, 128 * tile + 128)
                pairs = [(C.xnT[:, k, tcols], w[:, k, 256:384]) for k in range(8)]
                us += split_mm_group(P, ps[:, jj * 128:(jj + 1) * 128], pairs,
                                     w.r + [C.xnT.r[4 * k + vb] for k in range(8)], ps.r, 4)
            us.append(lambda: P.copy("dve", Vv[s_][:, 512 * vb:512 * vb + 512], ps[:, :], ps.r, [Vv[s_].r[vb]]))
            return us

        for tg in range(4):
            units += qk_units(1, tg)
            units += v_units(tg)
            units += qk_units(0, tg)
        return units

    u0 = proj_units(0)
    u0[0]()
    emit_norm(C, 4 * l + 0)
    for u in u0[1:]:
        u()
    for p in range(8):
        s_ = p % 2
        for c, ch in enumerate(chains):
            rows = slice(64 * c, 64 * c + 64)
            ch.kT = (lambda rows, s_: (lambda jb: (kT[s_][rows, 128 * jb:128 * jb + 128], [kT[s_].r[jb // 4]])))(rows, s_)
            ch.qT = (lambda rows, s_: (lambda t0, n: (qT[s_][rows, t0:t0 + n], [qT[s_].r[t0 // 512]])))(rows, s_)
            ch.V = (lambda c, s_: (lambda jb: (Vv[s_][:, 128 * jb + 64 * c:128 * jb + 64 * c + 64],
                                               [Vv[s_].r[jb // 4]])))(c, s_)
            ch.o_dst4 = (lambda c, s_: (lambda QG: (o_pair[s_][:, 4 * QG:4 * QG + 4, 64 * c:64 * c + 64],
                                                    [o_pair[s_].r[QG]])))(c, s_)
        bg = deque(proj_units(p + 1)) if p < 7 else deque()
        emit_sb_attention(C, chains, bg)
        emit_transposes(C, o_pair[s_], oT, p, banks=(7, 7))
    P.barrier()
    A.reset(mark)
    emit_attn_out_proj(C, A, lambda dc: C.d_sbwo[j, dc], 8, oT, 4 * l + 1)
    P.barrier()


def emit_softmax_attention(C, chains, bg, bg_every=3, look=1):
    P = C.P
    n = len(chains[0].steps)

    nbuf = look + 1

    def front(k):
        b = k % nbuf
        for ch in chains:
            st = ch.steps[k]
            q0 = (st["ta"] - 4 * st["QG"]) * 128
            N = (st["tb"] - st["ta"] + 1) * 128
            kap, kr = ch.kT(st["g"], st["jb"])
            qap, qr = ch.qT(st["g"], st["ta"] * 128, N)
            if st.get("negmask") is None:
                P.mm(ch.Zb[b][:, q0:q0 + N], kap, qap, True, True, kr + qr, ch.Zb[b].r)
            else:
                P.mm(ch.Zb[b][:, q0:q0 + N], kap, qap, True, False, kr + qr, ch.Zb[b].r, inc=False,
                     skip_group_check=True)
                P.mm(ch.Zb[b][:, q0:q0 + 128], cbs(C, CB_IDENT), st["negmask"], False, True,
                     C.cb.r, ch.Zb[b].r, skip_group_check=True)
        for ch in chains:
            st = ch.steps[k]
            q0 = (st["ta"] - 4 * st["QG"]) * 128
            N = (st["tb"] - st["ta"] + 1) * 128
            PT = ch.PT[b]
            P.act(PT[:, q0:q0 + N], ch.Zb[b][:, q0:q0 + N], AF.Exp, ch.Zb[b].r, PT.r)
        for ci_, ch in enumerate(chains):
            st = ch.steps[k]
            if st["mask"] is None:
                continue
            q0 = (st["ta"] - 4 * st["QG"]) * 128
            PT = ch.PT[b]
            map_, mr, mn = st["mask"]
            eng = "pool" if (ci_ + k) % 2 == 0 else "dve"
            P.tt(eng, PT[:, q0:q0 + mn], PT[:, q0:q0 + mn], map_, ALU.mult, PT.r + mr, PT.r)

    def back(k):
        b = k % nbuf
        for ch in chains:
            st = ch.steps[k]
            PT = ch.PT[b]
            vap, vr = ch.V(st["g"], st["jb"])
            tiles = list(range(st["ta"], st["tb"] + 1))
            for ti_, t in enumerate(tiles):
                i = t - 4 * st["QG"]
                P.mm(ch.Ob[:, i * 65:(i + 1) * 65], PT[:, i * 128:(i + 1) * 128], vap,
                     start=(st["first"] and ti_ == 0), stop=False, reads=PT.r + vr, writes=ch.Ob.r,
                     inc=(ti_ == len(tiles) - 1), skip_group_check=True)
        for ch in chains:
            st = ch.steps[k]
            if st["last"]:
                rd = ch.rden
                ov = ch.Ob[:, 0:260].rearrange("p (a b) -> p a b", b=65)
                rdv = rd[:, 0:4].rearrange("p (a b) -> p a b", b=1)
                P.recip(rdv, ov[:, :, 64:65], ch.Ob.r, rd.r)
                ap, rr = ch.o_dst4(st["QG"])
                P.tt("dve", ap, ov[:, :, 0:64], rdv.broadcast_to([128, 4, 64]), ALU.mult, ch.Ob.r + rd.r, rr)

    for j_ in range(min(look, n)):
        front(j_)
    for k in range(n):
        if k + look < n:
            front(k + look)
        back(k)
        if bg and k % bg_every == bg_every - 1:
            bg.popleft()()
    while bg:
        bg.popleft()()


def emit_sin_table(C, out_ap, out_res, ang, tmp, tmpi, rows, turns_off):
    P = C.P
    TWO_PI = 2.0 * math.pi
    a = ang[rows, :]
    t = tmp[rows, :]
    ti = tmpi[rows, :]
    P.ts("dve", t, a, 1.0 / TWO_PI, turns_off + 0.5, ALU.mult, ALU.add, ang.r, tmp.r)
    P.copy("dve", ti, t, tmp.r, tmpi.r)
    P.copy("dve", t, ti, tmpi.r, tmp.r)
    P.stt(t, t, -TWO_PI, a, ALU.mult, ALU.add, tmp.r + ang.r, tmp.r)
    if turns_off != 0.0:
        P.ts("dve", t, t, TWO_PI * turns_off, None, ALU.add, None, tmp.r, tmp.r)
    P.ts("dve", ti.bitcast(F32), t, -math.pi, TWO_PI, ALU.is_lt, ALU.mult, tmp.r, tmpi.r)
    P.tt("dve", t, t, ti.bitcast(F32), ALU.add, tmp.r + tmpi.r, tmp.r)
    P.ts("dve", ti.bitcast(F32), t, math.pi, -TWO_PI, ALU.is_gt, ALU.mult, tmp.r, tmpi.r)
    P.tt("dve", t, t, ti.bitcast(F32), ALU.add, tmp.r + tmpi.r, tmp.r)
    P.ts("dve", t, t, math.pi, -math.pi, ALU.min, ALU.max, tmp.r, tmp.r)
    P.act(out_ap, t, AF.Sin, tmp.r, out_res)


def emit_mla_layer(C, l):
    P = C.P
    A = C.A
    SC = 96.0 ** -0.5
    A.reset()
    R64 = slice(64, 96)
    cqn = A.alloc("cqn", [3, 2048], BF16, nres=12)
    ckvn = A.alloc("ckvn", [2, 2048], BF16, nres=8)
    Ct = A.alloc("Ct", [2048], F32)
    St = A.alloc("St", [2048], F32)
    kr = A.alloc("kr", [2048], BF16, nres=4)
    mark0 = A.off
    win = A.alloc("win", [8, 768], BF16)
    cq_t = [A.alloc("cqt%d" % i, [512], F32) for i in range(3)]
    ang = A.alloc("ang", [2048], F32)
    tmp = A.alloc("rtmp", [2048], F32)
    tmpi = A.alloc("rtmpi", [2048], I32)
    t1 = A.alloc("t1", [512], F32)
    t2 = A.alloc("t2", [512], F32)
    qng = A.alloc("qng", [8], F32)
    P.dma("pool", win[:, :, :], C.d_mlawin.rearrange("p (a b) -> p a b", b=768), C.usem[0], writes=win.r)
    P.dma("sp", qng[:, :], C.d_mlang, C.usem[1], writes=qng.r)
    P.dma("sp", tmpi[0:96, :], C.d_pos.broadcast_to([96, 2048]), C.usem[1], writes=tmpi.r)
    emit_norm(C, 4 * l + 0)
    P.copy("dve", ang[R64, :], tmpi[R64, :], tmpi.r, ang.r)
    P.ts("dve", ang[R64, :], ang[R64, :], C.cf[R64, 0:1], None, ALU.mult, None, ang.r + C.cf.r, ang.r)
    emit_sin_table(C, Ct[R64, :], Ct.r, ang, tmp, tmpi, R64, 0.25)
    emit_sin_table(C, St[R64, :], St.r, ang, tmp, tmpi, R64, 0.0)
    P.ts("dve", St[R64, :], St[R64, :], C.cf[R64, 1:2], None, ALU.mult, None, St.r + C.cf.r, St.r)
    for tg in range(4):
        cols = slice(512 * tg, 512 * tg + 512)
        xr = [C.xnT.r[4 * k + tg] for k in range(8)]
        for (nch, c0, dst, gcol, ssb) in ((3, 0, cqn, 0, 6), (2, 384, ckvn, 3, 7)):
            ss = C.PS[ssb]
            for ch_ in range(nch):
                ps = C.PS[ch_ % 2]
                pairs = [(win[:, k, c0 + 128 * ch_:c0 + 128 * ch_ + 128], C.xnT[:, k, cols]) for k in range(8)]
                P.mm_group(ps[:, :], pairs, win.r + xr, ps.r)
                P.copy("dve", cq_t[ch_][:, :], ps[:, :], ps.r, cq_t[ch_].r)
                sq = C.sq[ch_ % 2]
                P.act(sq[:, :], ps[:, :], AF.Square, ps.r, sq.r)
                P.mm(ss[:, :], cbs(C, CB_ONES), sq[:, :], start=(ch_ == 0), stop=(ch_ == nch - 1),
                     reads=sq.r + C.cb.r, writes=ss.r)
            P.act(C.rsb[:, :], ss[:, :], AF.Sqrt, ss.r, C.rsb.r, bias=C.epsb[:, 0:1], scale=1.0 / (128 * nch))
            rstd = C.rstd[0]
            P.recip(rstd[:, :], C.rsb[:, :], C.rsb.r, rstd.r)
            for ch_ in range(nch):
                P.stt(dst[:, ch_, cols], cq_t[ch_][:, :], qng[:, gcol + ch_:gcol + ch_ + 1], rstd[:, :],
                      ALU.mult, ALU.mult, cq_t[ch_].r + rstd.r + qng.r, [dst.r[4 * ch_ + tg]])
        psA = C.PS[2]
        psB = C.PS[3]
        P.mm_group(psA[0:96, :], [(win[:, k, 576:672], C.xnT[:, k, cols]) for k in range(8)], win.r + xr, psA.r)
        P.mm_group(psB[0:96, :], [(win[:, k, 672:768], C.xnT[:, k, cols]) for k in range(8)], win.r + xr, psB.r)
        P.tt("dve", t1[R64, :], psA[R64, :], Ct[R64, cols], ALU.mult, psA.r + Ct.r, t1.r)
        P.tt("dve", t2[R64, :], psB[R64, :], St[R64, cols], ALU.mult, psB.r + St.r, t2.r)
        P.tt("pool", kr[R64, cols], t1[R64, :], t2[R64, :], ALU.add, t1.r + t2.r, [kr.r[tg]])
    P.barrier()
    A.reset(mark0)
    oT = C.xnT
    wh = [A.alloc("wh%d" % i, [3 * 192 + 2 * 128], BF16) for i in range(2)]
    qf = [A.alloc("qf%d" % i, [2048], BF16, nres=4) for i in range(2)]
    kf = [A.alloc("kf%d" % i, [2048], BF16, nres=4) for i in range(2)]
    Vh = [A.alloc("Vh%d" % i, [16, 65], BF16, nres=4) for i in range(2)]
    o_pair = [A.alloc("op%d" % i, [16, 128], BF16, nres=4) for i in range(2)]
    t1 = A.alloc("t1b", [512], F32)
    t2 = A.alloc("t2b", [512], F32)
    chains = []
    for c in range(2):
        ch = Chain()
        ch.PT = [A.alloc("PT%d%d" % (c, i), [512], BF16) for i in range(2)]
        ch.rden = A.alloc("rden%d" % c, [4], F32)
        ch.Zb = [C.PS[c], C.PS[2 + c]]
        ch.Ob = C.PS[4 + c]
        steps = []
        for QG in ([0, 3] if c == 0 else [1, 2]):
            nk = 4 * QG + 4
            for jb in range(nk):
                r = jb - 4 * QG
                ta = 4 * QG + max(0, r)
                steps.append(dict(QG=QG, jb=jb, ta=ta, tb=4 * QG + 3, first=(jb == 0), last=(jb == nk - 1),
                                  g=0, mask=None, negmask=(cbs(C, CB_NEG_INCL) if r >= 0 else None)))
        ch.steps = steps
        chains.append(ch)
    for s_ in range(2):
        P.memset("pool", Vh[s_][:, :, 64:65], 1.0, Vh[s_].r)

    def head_units(h):
        s_ = h % 2
        w = wh[s_]
        wq_ = w[:, 0:576].rearrange("p (a b) -> p a b", b=192)
        wkv = w[:, 576:832].rearrange("p (a b) -> p a b", b=128)
        units = []
        units.append(lambda: P.dma("pool", w[:, :], C.d_mlawh[h], C.usem[s_], writes=w.r))
        ucnt = [0]

        def q_unit(tg):
            def f():
                cols = slice(512 * tg, 512 * tg + 512)
                psA = C.PS[6]
                psB = C.PS[7]
                cr = [cqn.r[4 * kc + tg] for kc in range(3)]
                P.mm_group(psA[0:96, :], [(wq_[:, kc, 0:96], cqn[:, kc, cols]) for kc in range(3)], w.r + cr, psA.r)
                P.mm_group(psB[0:96, :], [(wq_[:, kc, 96:192], cqn[:, kc, cols]) for kc in range(3)], w.r + cr, psB.r)
                P.ts("dve", qf[s_][0:64, cols], psA[0:64, :], SC, None, ALU.mult, None, psA.r, [qf[s_].r[tg]])
                P.stt(t1[R64, :], psA[R64, :], SC, Ct[R64, cols], ALU.mult, ALU.mult, psA.r + Ct.r, t1.r)
                P.stt(t2[R64, :], psB[R64, :], SC, St[R64, cols], ALU.mult, ALU.mult, psB.r + St.r, t2.r)
                P.tt("pool", qf[s_][R64, cols], t1[R64, :], t2[R64, :], ALU.add, t1.r + t2.r, [qf[s_].r[tg]])
            return f

        def k_unit(tg):
            def f():
                cols = slice(512 * tg, 512 * tg + 512)
                ps = C.PS[6 + tg % 2]
                cr = [ckvn.r[4 * kc + tg] for kc in range(2)]
                P.mm_group(ps[0:64, :], [(wkv[:, kc, 0:64], ckvn[:, kc, cols]) for kc in range(2)], w.r + cr, ps.r)
                P.copy("dve", kf[s_][0:64, cols], ps[0:64, :], ps.r, [kf[s_].r[tg]])
                P.copy("pool", kf[s_][R64, cols], kr[R64, cols], [kr.r[tg]], [kf[s_].r[tg]])
            return f

        def v_unit(vb):
            def f():
                ps = C.PS[6 + vb % 2]
                for jj in range(8):
                    tile = 8 * vb + jj
                    tcols = slice(128 * tile, 128 * tile + 128)
                    pairs = [(ckvn[:, kc, tcols], wkv[:, kc, 64:128]) for kc in range(2)]
                    P.mm_group(ps[:, jj * 64:(jj + 1) * 64], pairs,
                               w.r + [ckvn.r[4 * kc + tile // 4] for kc in range(2)], ps.r)
                P.copy("dve", Vh[s_][:, 8 * vb:8 * vb + 8, 0:64],
                       ps[:, :].rearrange("p (a b) -> p a b", b=64), ps.r, [Vh[s_].r[2 * vb], Vh[s_].r[2 * vb + 1]])
            return f

        for tg in range(4):
            units.append(k_unit(tg))
            units.append(q_unit(tg))
        units.append(v_unit(0))
        units.append(v_unit(1))
        return units

    for u in head_units(0):
        u()
    for h in range(16):
        s_ = h % 2
        ps_ = (h // 2) % 2
        for ch in chains:
            ch.kT = (lambda s_: (lambda g, jb: (kf[s_][0:96, 128 * jb:128 * jb + 128], [kf[s_].r[jb // 4]])))(s_)
            ch.qT = (lambda s_: (lambda g, t0, n: (qf[s_][0:96, t0:t0 + n], [qf[s_].r[t0 // 512]])))(s_)
            ch.V = (lambda s_: (lambda g, jb: (Vh[s_][:, jb, :], [Vh[s_].r[jb // 4]])))(s_)
            ch.o_dst4 = (lambda hh, ps_: (lambda QG: (o_pair[ps_][:, 4 * QG:4 * QG + 4, 64 * hh:64 * hh + 64],
                                                      [o_pair[ps_].r[QG]])))(h % 2, ps_)
        bg = deque(head_units(h + 1)) if h < 15 else deque()
        emit_softmax_attention(C, chains, bg, bg_every=1)
        if h % 2 == 1:
            emit_transposes(C, o_pair[ps_], oT, h // 2)
    P.barrier()
    A.reset(mark0)
    emit_attn_out_proj(C, A, lambda dc: C.d_mlawo[dc], 8, oT, 4 * l + 1)
    P.barrier()


DIL_W = (256, 640, 2048)
DIL_OFF = (0, 256, 896)
DIL_BACK = (1, 4, 15)


def emit_dil_layer(C, l):
    P = C.P
    A = C.A
    A.reset()
    oT = A.alloc("oT", [4, 2048], BF16, nres=16)
    mark = A.off
    valid = A.alloc("valid", [2944], BF16)
    wq = [A.alloc("wq%d" % i, [8, 384], BF16) for i in range(2)]
    qT = [A.alloc("qT%d" % g, [2048], BF16, nres=4) for g in range(3)]
    kT = [A.alloc("kT%d" % g, [2048], BF16, nres=4) for g in range(3)]
    Vv = [A.alloc("V%d" % g, [16, 130], BF16, nres=4) for g in range(3)]
    M = [A.alloc("M%d" % hh, [2944], BF16, nres=3) for hh in range(2)]
    stage = [A.alloc("stage0", [1024], F32)] * 2
    scnt = [0]
    o_pair = A.alloc("op", [16, 128], BF16, nres=4)
    for g in range(3):
        for hh in range(2):
            P.memset("pool", Vv[g][:, :, 65 * hh + 64:65 * hh + 65], 1.0, Vv[g].r)
    chains = []
    for c in range(2):
        ch = Chain()
        ch.PT = [A.alloc("PT%d%d" % (c, i), [512], BF16) for i in range(3)]
        ch.rden = A.alloc("rden%d" % c, [4], F32)
        ch.Zb = [C.PS[c], C.PS[2 + c], C.PS[6 + c]]
        ch.Ob = C.PS[4 + c]
        steps = []
        for QG in range(4):
            lst = []
            for g in range(3):
                for jb in range(max(0, 4 * QG - DIL_BACK[g]), 4 * QG + 4):
                    ta = max(jb, 4 * QG)
                    tb = min(jb + DIL_BACK[g], 4 * QG + 3)
                    if tb < ta:
                        continue
                    x0 = DIL_OFF[g] + (ta - jb) * 128
                    N = (tb - ta + 1) * 128
                    lst.append(dict(QG=QG, jb=jb, ta=ta, tb=tb, first=False, last=False, g=g,
                                    mask=(M[c][:, x0:x0 + N], [M[c].r[g]], N)))
            lst[0]["first"] = True
            lst[-1]["last"] = True
            steps += lst
        ch.steps = steps
        chains.append(ch)
    wissued = [0]

    def issue_w(idx):
        if idx >= 12 or idx < wissued[0]:
            return
        wissued[0] = idx + 1
        w_ = wq[idx % 2]
        P.dma("pool", w_[:, :, :], C.d_dilqkv[idx].rearrange("p (a b) -> p a b", b=384),
              C.usem[idx % 2], writes=w_.r)

    for p in range(4):
        issue_w(3 * p)
        issue_w(3 * p + 1)
        if p == 0:
            P.dma("pool", valid[:, :], C.d_dilvalid, C.wsem[1], writes=valid.r)
            emit_norm(C, 4 * l + 0)
        for hh in range(2):
            for g in range(3):
                for c0 in range(0, DIL_W[g], 1024):
                    W = min(1024, DIL_W[g] - c0)
                    o = DIL_OFF[g] + c0
                    stg = stage[scnt[0] % 2]
                    scnt[0] += 1
                    P.dma("sp", stg[:, 0:W], C.d_dilslab[g * 8 + 2 * p + hh][:, c0:c0 + W], C.wsem[scnt[0] % 2],
                          writes=stg.r)
                    P.act(stg[:, 0:W], stg[:, 0:W], AF.Exp, stg.r, stg.r)
                    P.tt("pool", M[hh][:, o:o + W], stg[:, 0:W], valid[:, o:o + W], ALU.mult,
                         stg.r + valid.r, [M[hh].r[g]])
        for g in range(3):
            issue_w(3 * p + g)
            w = wq[(3 * p + g) % 2]
            u = 0
            for tg in range(4):
                cols = slice(512 * tg, 512 * tg + 512)
                xr = [C.xnT.r[4 * k + tg] for k in range(8)]
                for which in range(2):
                    ps = C.PS[6 + u % 2]
                    u += 1
                    pairs = [(w[:, k, 128 * which:128 * which + 128], C.xnT[:, k, cols]) for k in range(8)]
                    P.mm_group(ps[:, :], pairs, w.r + xr, ps.r)
                    if which == 0:
                        P.ts("dve", qT[g][:, cols], ps[:, :], 0.125, None, ALU.mult, None, ps.r, [qT[g].r[tg]])
                    else:
                        P.copy("dve", kT[g][:, cols], ps[:, :], ps.r, [kT[g].r[tg]])
                ps = C.PS[6 + u % 2]
                u += 1
                for jj in range(4):
                    tile = 4 * tg + jj
                    tcols = slice(128 * tile, 128 * tile + 128)
                    pairs = [(C.xnT[:, k, tcols], w[:, k, 256:384]) for k in range(8)]
                    P.mm_group(ps[:, jj * 128:(jj + 1) * 128], pairs, w.r + xr, ps.r)
                for hh in range(2):
                    src = ps[:, :].rearrange("p (a b) -> p a b", b=128)[:, :, 64 * hh:64 * hh + 64]
                    P.copy("dve", Vv[g][:, 4 * tg:4 * tg + 4, 65 * hh:65 * hh + 64], src,
                           ps.r, [Vv[g].r[tg]])
            issue_w(3 * p + g + 2)
            if g == 0 and p > 0:
                emit_transposes(C, o_pair, oT, p - 1)
        for c, ch in enumerate(chains):
            rows = slice(64 * c, 64 * c + 64)
            ch.kT = (lambda rows: (lambda g, jb: (kT[g][rows, 128 * jb:128 * jb + 128], [kT[g].r[jb // 4]])))(rows)
            ch.qT = (lambda rows: (lambda g, t0, n: (qT[g][rows, t0:t0 + n],
                                                     [qT[g].r[i] for i in range(t0 // 512, (t0 + n - 1) // 512 + 1)])))(rows)
            ch.V = (lambda c: (lambda g, jb: (Vv[g][:, jb, 65 * c:65 * c + 65], [Vv[g].r[jb // 4]])))(c)
            ch.o_dst4 = (lambda c: (lambda QG: (o_pair[:, 4 * QG:4 * QG + 4, 64 * c:64 * c + 64], [o_pair.r[QG]])))(c)
        emit_softmax_attention(C, chains, deque(), look=2)
    emit_transposes(C, o_pair, oT, 3)
    P.barrier()
    A.reset(mark)
    emit_attn_out_proj(C, A, lambda dc: C.d_dilwo[dc], 4, oT, 4 * l + 1)
    P.barrier()


def build_nc(stages):
    nc = bass.Bass("TRN2", target_bir_lowering=False)
    C = Ctx()
    C.nc = nc

    def din(name, shape, dt=F32):
        return nc.dram_tensor(name, list(shape), dt, kind="ExternalInput").ap()

    d_xT = din("xT", [D, S])
    d_cst = din("cst", [128, 1024])
    d_gn = din("gains", [128, 128])
    d_cvp = din("convp", [128, 704])
    C.d_wup = din("wup", [4, NPAIR, 128, 2048])
    C.d_wdn = din("wdn", [4, 8, 128, DFF])
    C.d_sbqkv = din("sbqkv", [2, 8, 128, 3072])
    C.d_sbwo = din("sbwo", [2, 8, 128, 1024])
    C.d_pos = din("pos", [1, S], I32)
    d_cf = din("cf", [128, 8])
    C.d_mlawin = din("mlawin", [128, 8 * 768])
    C.d_mlang = din("mlang", [128, 8])
    C.d_mlawh = din("mlawh", [16, 128, 832])
    C.d_mlawo = din("mlawo", [8, 128, 1024])
    C.d_dilqkv = din("dilqkv", [12, 128, 3072])
    C.d_dilwo = din("dilwo", [8, 128, 512])
    C.d_dilslab = din("dilslab", [24, 128, 2048])
    C.d_dilvalid = din("dilvalid", [128, 2944])
    d_out = nc.dram_tensor("outT", [D, S], F32, kind="ExternalOutput").ap()

    with ExitStack() as es:
        P = Prog(nc, es)
        C.P = P
        C.xT = P.sbuf("xT_sb", [128, 8, S], F32, nres=32)
        C.xnT = P.sbuf("xnT_sb", [128, 8, S], BF16, nres=32)
        C.cb = P.sbuf("cb", [128, 1024], BF16)
        C.gn = P.sbuf("gn", [128, 128], F32)
        C.cvp = P.sbuf("cvp", [128, 704], F32)
        C.sq = [P.sbuf("sq%d" % i, [128, 512], BF16) for i in range(2)]
        C.rsb = P.sbuf("rsb", [128, 512], F32)
        C.rsb2 = [C.rsb, C.rsb]
        C.rstd = [P.sbuf("rstd%d" % i, [128, 512], F32) for i in range(2)]
        C.cf = P.sbuf("cf_sb", [128, 8], F32)
        C.epsb = P.sbuf("epsb", [128, 1], F32)
        C.oneb = P.sbuf("oneb", [128, 1], F32)
        arena = P.sbuf("arena", [128, ARENA_F32], F32)
        C.A = Arena(arena, ARENA_F32)
        C.PS = [P.psum("ps%d" % i, [128, 512], F32) for i in range(8)]
        C.wsem = [P.dsem("wsem%d" % i) for i in range(2)]
        C.usem = [P.dsem("usem%d" % i) for i in range(2)]
        s_in = P.dsem("s_in")
        s_x = [P.dsem("s_x%d" % i) for i in range(2)]
        s_out = P.dsem("s_out")

        P.dma("pool", C.cb[:, :], d_cst, s_in, writes=C.cb.r)
        P.dma("sp", C.gn[:, :], d_gn, s_in, writes=C.gn.r)
        P.dma("sp", C.cvp[:, :], d_cvp, s_in, writes=C.cvp.r)
        P.dma("sp", C.cf[:, :], d_cf, s_in, writes=C.cf.r)
        P.memset("pool", C.epsb[:, :], EPS, C.epsb.r)
        P.memset("pool", C.oneb[:, :], 1.0, C.oneb.r)
        for k in range(8):
            P.dma("sp", C.xT[:, k, :], d_xT[128 * k:128 * k + 128, :], s_x[k % 2],
                  writes=[C.xT.r[4 * k + tg] for tg in range(4)])

        for st in stages:
            kind, l = st
            if kind == "norm":
                emit_norm(C, l)
            elif kind == "ffn":
                emit_ffn(C, l)
            elif kind == "mix":
                if l % 3 == 0:
                    emit_sb_layer(C, l, l // 3)
                elif l % 3 == 1:
                    emit_dil_layer(C, l)
                else:
                    emit_mla_layer(C, l)
        P.barrier()
        fin = Res("fin")
        for k in range(8):
            P.dma("sp", d_out[128 * k:128 * k + 128, :], C.xT[:, k, :], s_out,
                  reads=[C.xT.r[4 * k + tg] for tg in range(4)], writes=[fin])
        P.op("sp", None, reads=[fin])
        P.emit()
        C.stats = dict(ecnt=dict(P.ecnt), n_wait=P.n_wait)
    return nc, C


def lhsT_stream_layout(W):
    K = W.shape[0]
    nk = K // 128
    return np.ascontiguousarray(W.reshape(nk, 128, 8, 128).transpose(2, 1, 0, 3).reshape(8, 128, K))


def prep_shared(inp):
    f = lambda a: np.asarray(a, dtype=np.float32)
    out = {}
    i = np.arange(128)
    cst = np.zeros((128, 1024), np.float32)
    cst[:, 0:128] = np.eye(128)
    cst[:, 128:256] = 1.0
    cst[:, 256:384] = -1.0 * (i[:, None] >= i[None, :])
    cst[:, 384:512] = -1.0 * (i[:, None] < i[None, :])
    cst[:, 512:640] = (i[:, None] < i[None, :])
    cst[:, 640:768] = (i[:, None] <= i[None, :])
    cst[:, 768:896] = NEG_BIG * (i[:, None] >= i[None, :])
    cst[:, 896:1024] = NEG_BIG * (i[:, None] > i[None, :])
    out["cst"] = cst
    ng = f(inp["norm_gains"])
    out["gains"] = np.ascontiguousarray(ng.reshape(16, 8, 128).transpose(2, 0, 1).reshape(128, 128))
    cw = f(inp["ffn_conv_w"])
    cbias = f(inp["ffn_conv_b"])
    cv = np.concatenate([cw, cbias[:, None, :]], axis=1)
    cv = cv.reshape(4, 4, 44, 128).transpose(3, 0, 2, 1)
    out["convp"] = np.ascontiguousarray(cv.reshape(128, 704))
    wu = f(inp["ffn_w_up"])
    g = wu[:, :, :DFF].reshape(4, 8, 128, NPAIR, 128)
    u = wu[:, :, DFF:].reshape(4, 8, 128, NPAIR, 128)
    gu = np.stack([g, u], axis=4)
    out["wup"] = np.ascontiguousarray(gu.transpose(0, 3, 2, 1, 4, 5).reshape(4, NPAIR, 128, 2048))
    wd = f(inp["ffn_w_down"])
    out["wdn"] = np.stack([lhsT_stream_layout(wd[l]) for l in range(4)])
    sq = f(inp["sb_w_qkv"])
    t = sq.reshape(2, 8, 128, 3, 8, 128)
    out["sbqkv"] = np.ascontiguousarray(t.transpose(0, 4, 2, 1, 3, 5).reshape(2, 8, 128, 3072))
    so = f(inp["sb_w_o"])
    out["sbwo"] = np.stack([lhsT_stream_layout(so[jj]) for jj in range(2)])
    cfm = np.zeros((128, 8), np.float32)
    freqs = (np.float32(10000.0) ** (-np.arange(16, dtype=np.float32) / np.float32(16))).astype(np.float32)
    cfm[64:80, 0] = freqs
    cfm[80:96, 0] = freqs
    cfm[64:80, 1] = -1.0
    cfm[80:96, 1] = 1.0
    out["cf"] = cfm
    wi = f(inp["mla_w_in"])[0]
    wi2 = np.concatenate([wi, wi[:, 576:640], wi[:, 656:672], wi[:, 640:656]], axis=1)
    out["mlawin"] = np.ascontiguousarray(wi2.reshape(8, 128, 768).transpose(1, 0, 2).reshape(128, 8 * 768))
    ng_ = np.zeros((128, 8), np.float32)
    ng_[:, 0:3] = f(inp["mla_q_norm"])[0].reshape(3, 128).T
    ng_[:, 3:5] = f(inp["mla_kv_norm"])[0].reshape(2, 128).T
    out["mlang"] = ng_
    wqb = f(inp["mla_w_qb"])[0]
    wkvb = f(inp["mla_w_kvb"])[0]
    whs = []
    for h in range(16):
        a = wqb[:, 96 * h:96 * h + 96]
        b = np.concatenate([wqb[:, 96 * h:96 * h + 64], wqb[:, 96 * h + 80:96 * h + 96],
                            wqb[:, 96 * h + 64:96 * h + 80]], axis=1)
        q2 = np.concatenate([a, b], axis=1).reshape(3, 128, 192).transpose(1, 0, 2).reshape(128, 576)
        kv = wkvb[:, 128 * h:128 * h + 128].reshape(2, 128, 128).transpose(1, 0, 2).reshape(128, 256)
        whs.append(np.concatenate([q2, kv], axis=1))
    out["mlawh"] = np.ascontiguousarray(np.stack(whs))
    out["mlawo"] = lhsT_stream_layout(f(inp["mla_w_o"])[0])
    dq = f(inp["dil_w_qkv"])[0]
    t = dq.reshape(8, 128, 3, 3, 4, 128)
    out["dilqkv"] = np.ascontiguousarray(t.transpose(4, 3, 1, 0, 2, 5).reshape(12, 128, 3072))
    out["dilwo"] = lhsT_stream_layout(f(inp["dil_w_o"])[0])
    rb = f(inp["rel_bias"])
    jk = np.arange(128)[:, None]
    xx = np.arange(2048)[None, :]
    delta = xx - jk
    dpos = np.maximum(delta, 0)
    dflt = np.maximum(dpos.astype(np.float32), np.float32(1.0))
    large = 16 + (np.log(dflt / np.float32(16)) / np.float32(math.log(2048 / 16)) * np.float32(16)).astype(np.int32)
    large = np.minimum(large, 31)
    bucket = np.where(dpos < 16, dpos, large)
    slab = np.zeros((24, 128, 2048), np.float32)
    for gh in range(24):
        slab[gh] = rb[bucket, gh]
    out["dilslab"] = slab
    valid = np.zeros((128, 2944), np.float32)
    for g, (W, r) in enumerate(((256, 1), (640, 4), (2048, 16))):
        dl = delta[:, :W]
        valid[:, DIL_OFF[g]:DIL_OFF[g] + W] = (dl >= 0) & (dl % r == 0) & (dl <= 128 * r)
    out["dilvalid"] = valid
    return out


ALL_STAGES = [("mix", 0), ("ffn", 0), ("mix", 1), ("ffn", 1), ("mix", 2), ("ffn", 2), ("mix", 3), ("ffn", 3)]
_CACHE = {}


def run(inputs, stages, n_cores=8, trace=False):
    key = tuple(stages)
    if key not in _CACHE:
        _CACHE[key] = build_nc(stages)
    nc, C = _CACHE[key]
    shared = prep_shared(inputs)
    x = np.asarray(inputs["x"], dtype=np.float32)
    in_maps = []
    for b in range(n_cores):
        m = dict(shared)
        m["xT"] = np.ascontiguousarray(x[b].T)
        m["pos"] = np.ascontiguousarray(np.asarray(inputs["positions"])[b][None, :].astype(np.int32))
        in_maps.append(m)
    res = run_bass_kernel_spmd(nc, in_maps, core_ids=list(range(n_cores)), trace=trace)
    out = np.stack([np.ascontiguousarray(r["outT"].T) for r in res.results])
    return out, res


def kernel(**inputs):
    out, _ = run(inputs, ALL_STAGES, 8)
    return out.astype(np.float32)
```

```python
import math
from collections import deque
from contextlib import ExitStack

import numpy as np
import concourse.bass as bass
import concourse.mybir as mybir
from concourse.bass_utils import run_bass_kernel_spmd

F32 = mybir.dt.float32
BF16 = mybir.dt.bfloat16
I32 = mybir.dt.int32
AF = mybir.ActivationFunctionType
ALU = mybir.AluOpType

S = 2048
D = 1024
DFF = 2816
NPAIR = 22
EPS = 1e-6
ENGS = ("pe", "act", "dve", "pool", "sp")
import os as _os
SAME_ENGINE_SYNC = _os.environ.get("NOSES", "") == ""


class Res:
    __slots__ = ("name", "w", "rs", "excl")

    def __init__(self, name=""):
        self.name = name
        self.w = None
        self.rs = {}
        self.excl = False


class Buf:
    def __init__(self, t, nres=1, name=""):
        self.t = t
        self.r = [Res("%s.%d" % (name, i)) for i in range(nres)]

    def __getitem__(self, idx):
        return self.t[idx]


class Prog:
    def __init__(self, nc, es, same_engine_sync=SAME_ENGINE_SYNC):
        self.nc = nc
        self.es = es
        self.ops = {e: [] for e in ENGS}
        self.ecnt = {e: 0 for e in ENGS}
        self.esem = {}
        for e in ("pe", "act", "dve", "pool"):
            self.esem[e] = es.enter_context(nc.semaphore("es_" + e))
        self.known = {e: {} for e in ENGS}
        self.dcnt = {}
        self.same_engine_sync = same_engine_sync
        self.n_wait = 0

    def sbuf(self, name, shape, dtype, nres=1):
        t = self.es.enter_context(self.nc.sbuf_tensor(name, list(shape), dtype))
        return Buf(t, nres, name)

    def psum(self, name, shape, dtype, nres=1):
        t = self.es.enter_context(self.nc.psum_tensor(name, list(shape), dtype))
        b = Buf(t, nres, name)
        for r in b.r:
            r.excl = True
        return b

    def dsem(self, name):
        s = self.es.enter_context(self.nc.semaphore(name))
        self.dcnt[s] = 0
        return s

    def op(self, eng, fn, reads=(), writes=(), inc=True, dsem=None):
        waits = {}
        kn = self.known[eng]
        own = self.esem.get(eng)
        if any(r.excl for r in reads):
            writes = list(writes) + [r for r in reads if r.excl and r not in writes]
            reads = [r for r in reads if not r.excl]

        def need(m):
            if m is None:
                return
            sem, val = m
            if sem in self.dcnt:
                val = self.dcnt[sem]
            elif sem is own:
                if eng == "pe" or not self.same_engine_sync:
                    return
            if kn.get(sem, 0) >= val:
                return
            if waits.get(sem, 0) < val:
                waits[sem] = val

        for r in reads:
            need(r.w)
        for w in writes:
            need(w.w)
            for m in w.rs.items():
                need(m)
        for sem, val in waits.items():
            kn[sem] = val
        if dsem is not None:
            self.dcnt[dsem] += 16
            marker = (dsem, self.dcnt[dsem])
            incspec = (dsem, 16)
        elif eng == "sp":
            marker = None
            incspec = None
        elif inc:
            self.ecnt[eng] += 1
            marker = (own, self.ecnt[eng])
            incspec = (own, 1)
        else:
            marker = (own, self.ecnt[eng] + 1)
            incspec = None
        if marker is not None:
            for r in reads:
                if r.rs.get(marker[0], 0) < marker[1]:
                    r.rs[marker[0]] = marker[1]
            for w in writes:
                w.w = marker
                w.rs = {}
        self.n_wait += len(waits)
        self.ops[eng].append((list(waits.items()), fn, incspec))

    def barrier(self):
        targets = [(self.esem[e], self.ecnt[e]) for e in self.esem if self.ecnt[e] > 0]
        targets += [(s, c) for s, c in self.dcnt.items() if c > 0]
        for eng in ENGS:
            waits = []
            for sem, val in targets:
                if sem is self.esem.get(eng):
                    continue
                if self.known[eng].get(sem, 0) >= val:
                    continue
                self.known[eng][sem] = val
                waits.append((sem, val))
            if waits:
                self.ops[eng].append((waits, None, None))

    def emit(self):
        nc = self.nc
        with nc.Block() as block:
            def run(engname):
                def body(e):
                    for waits, fn, incspec in self.ops[engname]:
                        for sem, val in waits:
                            e.wait_ge(sem, val)
                        if fn is None:
                            continue
                        ins = fn(e)
                        if incspec is not None:
                            ins.then_inc(incspec[0], incspec[1])
                return body

            block.tensor(run("pe"))
            block.scalar(run("act"))
            block.vector(run("dve"))
            block.gpsimd(run("pool"))
            block.sync(run("sp"))

    def dma(self, eng, out, in_, dsem, reads=(), writes=()):
        if eng == "pool":
            self.op(eng, lambda e: e.dma_start(out=out, in_=in_, max_dma_last_dim=8192), reads, writes, dsem=dsem)
        else:
            self.op(eng, lambda e: e.dma_start(out=out, in_=in_), reads, writes, dsem=dsem)

    def mm(self, out, lhsT, rhs, start, stop, reads, writes, inc=True, **kw):
        self.op("pe", lambda e: e.matmul(out, lhsT, rhs, start=start, stop=stop, **kw),
                reads, writes, inc=inc)

    def mm_group(self, out, pairs, reads, writes, **kw):
        n = len(pairs)
        for i, (l, r) in enumerate(pairs):
            self.mm(out, l, r, start=(i == 0), stop=(i == n - 1),
                    reads=reads, writes=writes, inc=(i == n - 1), **kw)

    def act(self, out, in_, func, reads, writes, **kw):
        self.op("act", lambda e: e.activation(out=out, in_=in_, func=func, **kw), reads, writes)

    def tt(self, eng, out, in0, in1, op, reads, writes):
        self.op(eng, lambda e: e.tensor_tensor(out=out, in0=in0, in1=in1, op=op), reads, writes)

    def ts(self, eng, out, in0, s1, s2, op0, op1, reads, writes):
        if s2 is None:
            self.op(eng, lambda e: e.tensor_scalar(out=out, in0=in0, scalar1=s1, scalar2=None, op0=op0),
                    reads, writes)
        else:
            self.op(eng, lambda e: e.tensor_scalar(out=out, in0=in0, scalar1=s1, scalar2=s2,
                                                   op0=op0, op1=op1), reads, writes)

    def stt(self, out, in0, scalar, in1, op0, op1, reads, writes):
        self.op("dve", lambda e: e.scalar_tensor_tensor(out=out, in0=in0, scalar=scalar, in1=in1,
                                                        op0=op0, op1=op1), reads, writes)

    def copy(self, eng, out, in_, reads, writes):
        if eng == "act":
            self.act(out, in_, AF.Copy, reads, writes)
        else:
            self.op(eng, lambda e: e.tensor_copy(out=out, in_=in_), reads, writes)

    def recip(self, out, in_, reads, writes):
        self.op("dve", lambda e: e.reciprocal(out=out, in_=in_), reads, writes)

    def memset(self, eng, out, val, writes):
        self.op(eng, lambda e: e.memset(out, val), (), writes)


def split_mm_group(P, out, pairs, reads, writes, chunk=2):
    units = []
    n = len(pairs)
    for c0 in range(0, n, chunk):
        def f(c0=c0):
            for i in range(c0, min(n, c0 + chunk)):
                P.mm(out, pairs[i][0], pairs[i][1], start=(i == 0), stop=(i == n - 1),
                     reads=reads, writes=writes, inc=(i == n - 1))
        units.append(f)
    return units


class Arena:
    def __init__(self, buf, nf32):
        self.buf = buf
        self.n = nf32
        self.off = 0

    def reset(self, to=0):
        self.off = to

    def alloc(self, name, shape, dtype, nres=1):
        n = 1
        for s_ in shape:
            n *= s_
        esz = 4 if dtype in (F32, I32) else 2
        nf = (n * esz + 3) // 4
        nf = (nf + 3) // 4 * 4
        assert self.off + nf <= self.n, "arena overflow %s: %d + %d > %d" % (name, self.off, nf, self.n)
        ap = self.buf.t[:, self.off:self.off + nf]
        self.off += nf
        if dtype != F32:
            ap = ap.bitcast(dtype)
        ap = ap[:, 0:n]
        if len(shape) == 2:
            ap = ap.rearrange("p (a b) -> p a b", b=shape[1])
        elif len(shape) == 3:
            ap = ap.rearrange("p (a b c) -> p a b c", b=shape[1], c=shape[2])
        return Buf(ap, nres, name)


CB_IDENT, CB_ONES, CB_TRI_INCL, CB_TRI_REST, CB_M_STRICT, CB_M_INCL, CB_NEG_STRICT, CB_NEG_INCL = range(8)
NEG_BIG = -30000.0
ARENA_F32 = 25216


class Ctx:
    pass


def cbs(C, i):
    return C.cb[:, 128 * i:128 * (i + 1)]


def emit_norm(C, gidx):
    P = C.P
    for tg in range(4):
        cols = slice(512 * tg, 512 * tg + 512)
        ss = C.PS[6 + tg % 2]
        for k in range(8):
            sq = C.sq[k % 2]
            P.act(sq[:, :], C.xT[:, k, cols], AF.Square, [C.xT.r[4 * k + tg]], sq.r)
            P.mm(ss[:, :], cbs(C, CB_ONES), sq[:, :], start=(k == 0), stop=(k == 7),
                 reads=sq.r + C.cb.r, writes=ss.r)
        rsb = C.rsb
        P.act(rsb[:, :], ss[:, :], AF.Sqrt, ss.r, rsb.r, bias=C.epsb[:, 0:1], scale=1.0 / D)
        rstd = C.rstd[tg % 2]
        P.recip(rstd[:, :], rsb[:, :], rsb.r, rstd.r)
        for k in range(8):
            P.stt(C.xnT[:, k, cols], C.xT[:, k, cols], C.gn[:, gidx * 8 + k:gidx * 8 + k + 1], rstd[:, :],
                  ALU.mult, ALU.mult, [C.xT.r[4 * k + tg]] + rstd.r + C.gn.r, [C.xnT.r[4 * k + tg]])


class OutProj:
    def __init__(self, C, bufs, wsrc, nk, rhs_fn, tgs, gidx):
        self.C, self.bufs, self.wsrc, self.nk, self.rhs_fn, self.tgs, self.gidx = C, bufs, wsrc, nk, rhs_fn, tgs, gidx
        self.loaded = set()

    def load(self, dc):
        if dc in self.loaded or dc >= 8:
            return
        self.loaded.add(dc)
        C = self.C
        w = self.bufs[0][dc % 2]
        C.P.dma("pool", w[:, :], self.wsrc(dc), C.wsem[dc % 2], writes=w.r)

    def run_mm(self, fofs=0, ssb=2, bg=None):
        C, nk, rhs_fn, tgs, gidx = self.C, self.nk, self.rhs_fn, self.tgs, self.gidx
        P = C.P
        wbuf, fT, tmpb = self.bufs
        self.fofs = fofs
        self.ssb = ssb
        self.load(0)
        for dc in range(8):
            w = wbuf[dc % 2]
            for ti, tg in enumerate(tgs):
                ps = C.PS[ti]
                pairs = []
                reads = list(w.r)
                for kc in range(nk):
                    ap, rr = rhs_fn(kc, ti)
                    pairs.append((w[:, kc * 128:(kc + 1) * 128], ap))
                    reads += rr
                P.mm_group(ps[:, :], pairs, reads, ps.r)
                if ti == 0:
                    self.load(dc + 1)
                fo = fofs + ti * 512
                P.copy("dve", fT[:, dc, fo:fo + 512], ps[:, :], ps.r, [fT.r[(fofs // 512 + ti) * 8 + dc]])
                sq = C.sq[ti]
                P.act(sq[:, :], ps[:, :], AF.Square, ps.r, sq.r)
                ss = C.PS[ssb + ti]
                P.mm(ss[:, :], cbs(C, CB_ONES), sq[:, :], start=(dc == 0), stop=(dc == 7),
                     reads=sq.r + C.cb.r, writes=ss.r)
                if bg:
                    bg.popleft()()
                    if dc >= 1 and bg:
                        bg.popleft()()

    def final_units(self, rstd_bufs=None):
        C, tgs, gidx = self.C, self.tgs, self.gidx
        P = C.P
        wbuf, fT, tmpb = self.bufs
        fofs, ssb = self.fofs, self.ssb
        rstds = rstd_bufs or C.rstd
        units = []

        def rs(ti):
            def f():
                ss = C.PS[ssb + ti]
                rsb = C.rsb2[ti]
                P.act(rsb[:, :], ss[:, :], AF.Sqrt, ss.r, rsb.r, bias=C.epsb[:, 0:1], scale=1.0 / D)
                P.recip(rstds[ti][:, :], rsb[:, :], rsb.r, rstds[ti].r)
            return f

        def fin(dc, ti, tg):
            def f():
                cols = slice(512 * tg, 512 * tg + 512)
                rstd = rstds[ti]
                tmp = tmpb[ti]
                fo = fofs + ti * 512
                fr = [fT.r[(fofs // 512 + ti) * 8 + dc]]
                P.tt("pool" if ti == 0 else "dve", tmp[:, :], fT[:, dc, fo:fo + 512], rstd[:, :], ALU.mult,
                     fr + rstd.r, tmp.r)
                P.stt(C.xT[:, dc, cols], tmp[:, :], C.gn[:, gidx * 8 + dc:gidx * 8 + dc + 1], C.xT[:, dc, cols],
                      ALU.mult, ALU.add, [C.xT.r[4 * dc + tg]] + tmp.r + C.gn.r, [C.xT.r[4 * dc + tg]])
            return f

        for ti, tg in enumerate(tgs):
            units.append(rs(ti))
        for dc in range(8):
            for ti, tg in enumerate(tgs):
                units.append(fin(dc, ti, tg))
        return units

    def run(self):
        self.run_mm()
        for u in self.final_units():
            u()


def emit_attn_out_proj(C, A, wsrc, nk, oT, gidx):
    wbuf = [A.alloc("wo%d" % i, [nk * 128], BF16) for i in range(2)]
    fT = A.alloc("fT", [8, 2048], BF16, nres=32)
    tmpb = [A.alloc("tmp%d" % i, [512], F32) for i in range(2)]
    rstd2 = [A.alloc("rstdb%d" % i, [512], F32) for i in range(2)]
    ops = []
    for half in range(2):
        def rhs_fn(kc, ti, half=half):
            tg = 2 * half + ti
            return oT[:, kc, 512 * tg:512 * tg + 512], [oT.r[4 * kc + tg]]
        ops.append(OutProj(C, (wbuf, fT, tmpb), wsrc, nk, rhs_fn, [2 * half, 2 * half + 1], gidx))
    ops[0].run_mm(fofs=0, ssb=2)
    f0 = deque(ops[0].final_units())
    ops[1].run_mm(fofs=1024, ssb=6, bg=f0)
    while f0:
        f0.popleft()()
    for u in ops[1].final_units(rstd_bufs=rstd2):
        u()


def emit_out_proj(C, bufs, wsrc, nk, rhs_fn, tgs, gidx):
    OutProj(C, bufs, wsrc, nk, rhs_fn, tgs, gidx).run()


def alloc_outproj_bufs(C, nk, tmpb=None):
    A = C.A
    wbuf = [A.alloc("wo%d" % i, [nk * 128], BF16) for i in range(2)]
    fT = A.alloc("fT", [8, 1024], BF16, nres=16)
    if tmpb is None:
        tmpb = [A.alloc("tmp%d" % i, [512], F32) for i in range(2)]
    return wbuf, fT, tmpb


def emit_ffn(C, l):
    P = C.P
    A = C.A
    A.reset()
    mT = A.alloc("mT", [NPAIR, 1024], BF16, nres=NPAIR * 2)
    wup = [A.alloc("wup%d" % i, [8, 256], BF16) for i in range(2)]
    hs = [[A.alloc("hs%d%d" % (i, j), [514], F32) for j in range(2)] for i in range(2)]
    acc = [[A.alloc("acc%d%d" % (i, j), [512], F32) for j in range(2)] for i in range(2)]
    tails = A.alloc("tails", [44, 2], F32, nres=44)
    tmpx = A.alloc("tmpx", [512], F32)
    opb = alloc_outproj_bufs(C, NPAIR, tmpb=[tmpx, tmpx])
    nload = [0]

    def load_w(seq):
        if seq >= 2 * NPAIR or seq < nload[0]:
            return
        nload[0] = seq + 1
        i = seq % NPAIR
        w = wup[seq % 2]
        P.dma("pool", w[:, :, :], C.d_wup[l, i].rearrange("p (a b) -> p a b", b=256), C.usem[seq % 2], writes=w.r)

    def finish(u):
        half, i, t2, par = u
        ag, au = acc[par]
        P.act(ag[:, :], ag[:, :], AF.Silu, ag.r, ag.r)
        P.tt("pool", mT[:, i, 512 * t2:512 * t2 + 512], ag[:, :], au[:, :], ALU.mult,
             ag.r + au.r, [mT.r[2 * i + t2]])

    load_w(0)
    emit_norm(C, 4 * l + 2)
    cnt = 0
    pending = deque()
    for half in range(2):
        def rhs_fn(kc, ti):
            return mT[:, kc, 512 * ti:512 * ti + 512], [mT.r[2 * kc + ti]]
        op = OutProj(C, opb, lambda dc: C.d_wdn[l, dc], NPAIR, rhs_fn, [2 * half, 2 * half + 1], 4 * l + 3)
        prev = None
        for i in range(NPAIR):
            seq = half * NPAIR + i
            w = wup[seq % 2]
            for t2 in range(2):
                tg = 2 * half + t2
                cols = slice(512 * tg, 512 * tg + 512)
                par = cnt % 2
                pss = []
                for gu in range(2):
                    ps = C.PS[(2 * cnt + gu) % 8]
                    pairs = [(w[:, k, 128 * gu:128 * gu + 128], C.xnT[:, k, cols]) for k in range(8)]
                    P.mm_group(ps[:, :], pairs, w.r + [C.xnT.r[4 * k + tg] for k in range(8)], ps.r)
                    pss.append(ps)
                if t2 == 0:
                    load_w(seq + 1)
                    if i == NPAIR - 2:
                        op.load(0)
                cnt += 1
                for gu in range(2):
                    ci = i + NPAIR * gu
                    pbase = (l * 44 + ci) * 4
                    h = hs[par][gu]
                    a = acc[par][gu]
                    ps = pss[gu]
                    P.copy("act", h[:, 2:514], ps[:, :], ps.r, h.r)
                    P.act(a[:, :], ps[:, :], AF.Identity, ps.r + C.cvp.r, a.r,
                          bias=C.cvp[:, pbase + 3:pbase + 4], scale=C.cvp[:, pbase + 2:pbase + 3])
                for gu in range(2):
                    ci = i + NPAIR * gu
                    h = hs[par][gu]
                    if tg == 0:
                        P.memset("pool", h[:, 0:2], 0.0, h.r)
                    else:
                        P.copy("pool", h[:, 0:2], tails[:, ci, :], [tails.r[ci]], h.r)
                if tg < 3:
                    for gu in range(2):
                        ci = i + NPAIR * gu
                        h = hs[par][gu]
                        P.copy("pool", tails[:, ci, :], h[:, 512:514], h.r, [tails.r[ci]])
                if prev is not None:
                    finish(prev)
                if pending:
                    pending.popleft()()
                for tap in (1, 0):
                    for gu in range(2):
                        ci = i + NPAIR * gu
                        pbase = (l * 44 + ci) * 4
                        h = hs[par][gu]
                        a = acc[par][gu]
                        P.stt(a[:, :], h[:, tap:tap + 512], C.cvp[:, pbase + tap:pbase + tap + 1], a[:, :],
                              ALU.mult, ALU.add, h.r + a.r + C.cvp.r, a.r)
                prev = (half, i, t2, par)
        finish(prev)
        op.run_mm()
        pending = deque(op.final_units())
        pending.popleft()()
        pending.popleft()()
        if half == 1:
            while pending:
                pending.popleft()()
    P.barrier()


class Chain:
    pass


def emit_transposes(C, o_pair, oT, p, banks=(6, 7)):
    P = C.P
    for tb in range(4):
        ps = C.PS[banks[tb % 2]]
        psb = ps.t[:, :].bitcast(BF16)
        for j in range(4):
            tile = 4 * tb + j
            P.op("pe", (lambda o, i: (lambda e: e.transpose(o, i, cbs(C, CB_IDENT))))(
                psb[:, j * 128:(j + 1) * 128], o_pair[:, tile, :]),
                 [o_pair.r[tile // 4]] + C.cb.r, ps.r)
        P.copy("dve", oT[:, p, 512 * tb:512 * tb + 512], psb[:, 0:512], ps.r, [oT.r[4 * p + tb]])


def sb_steps():
    steps = []
    for QG in range(4):
        nk = 4 * QG + 4
        for jb in range(nk - 1, -1, -1):
            r = jb - 4 * QG
            ta = 4 * QG + max(0, r)
            steps.append(dict(QG=QG, jb=jb, ta=ta, tb=4 * QG + 3, diag=(jb if r >= 0 else None),
                              first=(jb == nk - 1), last=(jb == 0)))
    return steps


def emit_sb_attention(C, chains, bg):
    P = C.P
    steps = sb_steps()
    n = len(steps)

    def q0_of(k):
        st = steps[k]
        return (st["ta"] - 4 * st["QG"]) * 128

    def Zm(k):
        st = steps[k]
        q0 = q0_of(k)
        for ch in chains:
            Zb = ch.Zb[k % 2]
            kap, kr = ch.kT(st["jb"])
            qap, qr = ch.qT(st["ta"] * 128, 512 - q0)
            if st["diag"] is None:
                P.mm(Zb[:, q0:512], kap, qap, True, True, kr + qr, Zb.r)
            else:
                P.mm(Zb[:, q0:512], kap, qap, True, False, kr + qr, Zb.r, inc=False, skip_group_check=True)
                P.mm(Zb[:, q0:q0 + 128], cbs(C, CB_IDENT), cbs(C, CB_NEG_STRICT), False, True,
                     C.cb.r, Zb.r, skip_group_check=True)

    def Em(k):
        q0 = q0_of(k)
        for ch in chains:
            E = ch.E[k % 3]
            Zb = ch.Zb[k % 2]
            P.act(E[:, q0:512], Zb[:, q0:512], AF.Exp, Zb.r, E.r)

    def Lm(k):
        q0 = q0_of(k)
        for ch in chains:
            E = ch.E[k % 3]
            Lp = ch.Lp[k % 2]
            P.act(Lp[:, q0:512], E[:, q0:512], AF.Ln, E.r, Lp.r, bias=1.0, scale=1.0)

    def TRIm(k):
        st = steps[k]
        q0 = q0_of(k)
        for ch in chains:
            Lp = ch.Lp[k % 2]
            P.mm(ch.Ab[:, q0:512], cbs(C, CB_TRI_INCL), Lp[:, q0:512], st["first"], False,
                 Lp.r + C.cb.r, ch.Ab.r, skip_group_check=True)

    def Xm(k):
        q0 = q0_of(k)
        for ch in chains:
            P.act(ch.X[:, q0:512], ch.Ab[:, q0:512], AF.Exp, ch.Ab.r, ch.X.r)

    def RESTm(k):
        st = steps[k]
        q0 = q0_of(k)
        if st["last"]:
            return
        for ch in chains:
            Lp = ch.Lp[k % 2]
            P.mm(ch.Ab[:, q0:512], cbs(C, CB_TRI_REST), Lp[:, q0:512], False, False,
                 Lp.r + C.cb.r, ch.Ab.r, skip_group_check=True)

    def aTm(k):
        q0 = q0_of(k)
        for ch in chains:
            aT = ch.aT[k % 2]
            E = ch.E[k % 3]
            P.tt("dve", aT[:, q0:512], E[:, q0:512], ch.X[:, q0:512], ALU.mult, E.r + ch.X.r, aT.r)

    def Om(k):
        st = steps[k]
        for ci_, ch in enumerate(chains):
            aT = ch.aT[k % 2]
            vap, vr = ch.V(st["jb"])
            tiles = list(range(st["ta"], st["tb"] + 1))
            for ti_, t in enumerate(tiles):
                i = t - 4 * st["QG"]
                oc = ch.ocol + i * 64
                P.mm(ch.Ob[:, oc:oc + 64], aT[:, i * 128:(i + 1) * 128], vap,
                     start=(st["first"] and ti_ == 0 and ci_ == 0), stop=False, reads=aT.r + vr,
                     writes=ch.Ob.r, inc=(ti_ == len(tiles) - 1), skip_group_check=True)
        if st["last"]:
            for ch in chains:
                ap, rr = ch.o_dst4(st["QG"])
                src = ch.Ob[:, ch.ocol:ch.ocol + 256].rearrange("p (a b) -> p a b", b=64)
                P.copy("dve", ap, src, ch.Ob.r, rr)

    Zm(0)
    Em(0)
    Lm(0)
    TRIm(0)
    if n > 1:
        Zm(1)
        Em(1)
    if n > 2:
        Zm(2)
    for k in range(n):
        Xm(k)
        RESTm(k)
        if k + 1 < n:
            Lm(k + 1)
            TRIm(k + 1)
        aTm(k)
        if k + 2 < n:
            Em(k + 2)
        Om(k)
        if k + 3 < n:
            Zm(k + 3)
        nb_ = -(-len(bg) // max(1, n - 1 - k)) if bg else 0
        for _ in range(nb_):
            if bg:
                bg.popleft()()
    while bg:
        bg.popleft()()


def emit_sb_layer(C, l, j):
    P = C.P
    A = C.A
    A.reset()
    oT = A.alloc("oT", [8, 2048], BF16, nres=32)
    mark = A.off
    wq = [A.alloc("wq%d" % i, [8, 384], BF16) for i in range(2)]
    qT = [A.alloc("qT%d" % i, [2048], BF16, nres=4) for i in range(2)]
    kT = [A.alloc("kT%d" % i, [2048], BF16, nres=4) for i in range(2)]
    Vv = [A.alloc("V%d" % i, [2048], BF16, nres=4) for i in range(2)]
    o_pair = [A.alloc("op0", [16, 128], BF16, nres=4)] * 2
    chains = []
    for c in range(2):
        ch = Chain()
        ch.E = [A.alloc("E%d%d" % (c, i), [512], F32) for i in range(3)]
        ch.X = A.alloc("X%d" % c, [512], F32)
        ch.Lp = [A.alloc("Lp%d%d" % (c, i), [512], BF16) for i in range(2)]
        ch.aT = [A.alloc("aT%d%d" % (c, i), [512], BF16) for i in range(2)]
        ch.Zb = [C.PS[2 * c], C.PS[2 * c + 1]]
        ch.Ab = C.PS[4 + c]
        ch.Ob = C.PS[6]
        ch.ocol = 256 * c
        chains.append(ch)

    def proj_units(p):
        s_ = p % 2
        w = wq[s_]
        units = []
        units.append(lambda: P.dma("pool", w[:, :, :], C.d_sbqkv[j, p].rearrange("p (a b) -> p a b", b=384),
                                   C.usem[s_], writes=w.r))
        ucnt = [0]

        def qk_units(which, tg):
            ps = C.PS[7]
            ucnt[0] += 1
            cols = slice(512 * tg, 512 * tg + 512)
            pairs = [(w[:, k, 128 * which:128 * which + 128], C.xnT[:, k, cols]) for k in range(8)]
            us = split_mm_group(P, ps[:, :], pairs, w.r + [C.xnT.r[4 * k + tg] for k in range(8)], ps.r, 2)
            if which == 0:
                us.append(lambda: P.ts("dve", qT[s_][:, cols], ps[:, :], 0.125, None, ALU.mult, None, ps.r,
                                       [qT[s_].r[tg]]))
            else:
                us.append(lambda: P.copy("dve", kT[s_][:, cols], ps[:, :], ps.r, [kT[s_].r[tg]]))
            return us

        def v_units(vb):
            ps = C.PS[7]
            ucnt[0] += 1
            us = []
            for jj in range(4):
                tile = 4 * vb + jj
                tcols = slice(128 * tile, 128 * tile + 128)
                pairs = [(C.xnT[:, k, tcols], w[:, k, 256:384]) for k in range(8)]
                us += split_mm_group(P, ps[:, jj * 128:(jj + 1) * 128], pairs,
                                     w.r + [C.xnT.r[4 * k + vb] for k in range(8)], ps.r, 4)
            us.append(lambda: P.copy("dve", Vv[s_][:, 512 * vb:512 * vb + 512], ps[:, :], ps.r, [Vv[s_].r[vb]]))
            return us

        for tg in range(4):
            units += qk_units(1, tg)
            units += v_units(tg)
            units += qk_units(0, tg)
        return units

    u0 = proj_units(0)
    u0[0]()
    emit_norm(C, 4 * l + 0)
    for u in u0[1:]:
        u()
    for p in range(8):
        s_ = p % 2
        for c, ch in enumerate(chains):
            rows = slice(64 * c, 64 * c + 64)
            ch.kT = (lambda rows, s_: (lambda jb: (kT[s_][rows, 128 * jb:128 * jb + 128], [kT[s_].r[jb // 4]])))(rows, s_)
            ch.qT = (lambda rows, s_: (lambda t0, n: (qT[s_][rows, t0:t0 + n], [qT[s_].r[t0 // 512]])))(rows, s_)
            ch.V = (lambda c, s_: (lambda jb: (Vv[s_][:, 128 * jb + 64 * c:128 * jb + 64 * c + 64],
                                               [Vv[s_].r[jb // 4]])))(c, s_)
            ch.o_dst4 = (lambda c, s_: (lambda QG: (o_pair[s_][:, 4 * QG:4 * QG + 4, 64 * c:64 * c + 64],
                                                    [o_pair[s_].r[QG]])))(c, s_)
        bg = deque(proj_units(p + 1)) if p < 7 else deque()
        emit_sb_attention(C, chains, bg)
        emit_transposes(C, o_pair[s_], oT, p, banks=(7, 7))
    P.barrier()
    A.reset(mark)
    emit_attn_out_proj(C, A, lambda dc: C.d_sbwo[j, dc], 8, oT, 4 * l + 1)
    P.barrier()


def emit_softmax_attention(C, chains, bg, bg_every=3, look=1):
    P = C.P
    n = len(chains[0].steps)

    nbuf = look + 1

    def front(k):
        b = k % nbuf
        for ch in chains:
            st = ch.steps[k]
            q0 = (st["ta"] - 4 * st["QG"]) * 128
            N = (st["tb"] - st["ta"] + 1) * 128
            kap, kr = ch.kT(st["g"], st["jb"])
            qap, qr = ch.qT(st["g"], st["ta"] * 128, N)
            if st.get("negmask") is None:
                P.mm(ch.Zb[b][:, q0:q0 + N], kap, qap, True, True, kr + qr, ch.Zb[b].r)
            else:
                P.mm(ch.Zb[b][:, q0:q0 + N], kap, qap, True, False, kr + qr, ch.Zb[b].r, inc=False,
                     skip_group_check=True)
                P.mm(ch.Zb[b][:, q0:q0 + 128], cbs(C, CB_IDENT), st["negmask"], False, True,
                     C.cb.r, ch.Zb[b].r, skip_group_check=True)
        for ch in chains:
            st = ch.steps[k]
            q0 = (st["ta"] - 4 * st["QG"]) * 128
            N = (st["tb"] - st["ta"] + 1) * 128
            PT = ch.PT[b]
            P.act(PT[:, q0:q0 + N], ch.Zb[b][:, q0:q0 + N], AF.Exp, ch.Zb[b].r, PT.r)
        for ci_, ch in enumerate(chains):
            st = ch.steps[k]
            if st["mask"] is None:
                continue
            q0 = (st["ta"] - 4 * st["QG"]) * 128
            PT = ch.PT[b]
            map_, mr, mn = st["mask"]
            eng = "pool" if (ci_ + k) % 2 == 0 else "dve"
            P.tt(eng, PT[:, q0:q0 + mn], PT[:, q0:q0 + mn], map_, ALU.mult, PT.r + mr, PT.r)

    def back(k):
        b = k % nbuf
        for ch in chains:
            st = ch.steps[k]
            PT = ch.PT[b]
            vap, vr = ch.V(st["g"], st["jb"])
            tiles = list(range(st["ta"], st["tb"] + 1))
            for ti_, t in enumerate(tiles):
                i = t - 4 * st["QG"]
                P.mm(ch.Ob[:, i * 65:(i + 1) * 65], PT[:, i * 128:(i + 1) * 128], vap,
                     start=(st["first"] and ti_ == 0), stop=False, reads=PT.r + vr, writes=ch.Ob.r,
                     inc=(ti_ == len(tiles) - 1), skip_group_check=True)
        for ch in chains:
            st = ch.steps[k]
            if st["last"]:
                rd = ch.rden
                ov = ch.Ob[:, 0:260].rearrange("p (a b) -> p a b", b=65)
                rdv = rd[:, 0:4].rearrange("p (a b) -> p a b", b=1)
                P.recip(rdv, ov[:, :, 64:65], ch.Ob.r, rd.r)
                ap, rr = ch.o_dst4(st["QG"])
                P.tt("dve", ap, ov[:, :, 0:64], rdv.broadcast_to([128, 4, 64]), ALU.mult, ch.Ob.r + rd.r, rr)

    for j_ in range(min(look, n)):
        front(j_)
    for k in range(n):
        if k + look < n:
            front(k + look)
        back(k)
        if bg and k % bg_every == bg_every - 1:
            bg.popleft()()
    while bg:
        bg.popleft()()


def emit_sin_table(C, out_ap, out_res, ang, tmp, tmpi, rows, turns_off):
    P = C.P
    TWO_PI = 2.0 * math.pi
    a = ang[rows, :]
    t = tmp[rows, :]
    ti = tmpi[rows, :]
    P.ts("dve", t, a, 1.0 / TWO_PI, turns_off + 0.5, ALU.mult, ALU.add, ang.r, tmp.r)
    P.copy("dve", ti, t, tmp.r, tmpi.r)
    P.copy("dve", t, ti, tmpi.r, tmp.r)
    P.stt(t, t, -TWO_PI, a, ALU.mult, ALU.add, tmp.r + ang.r, tmp.r)
    if turns_off != 0.0:
        P.ts("dve", t, t, TWO_PI * turns_off, None, ALU.add, None, tmp.r, tmp.r)
    P.ts("dve", ti.bitcast(F32), t, -math.pi, TWO_PI, ALU.is_lt, ALU.mult, tmp.r, tmpi.r)
    P.tt("dve", t, t, ti.bitcast(F32), ALU.add, tmp.r + tmpi.r, tmp.r)
    P.ts("dve", ti.bitcast(F32), t, math.pi, -TWO_PI, ALU.is_gt, ALU.mult, tmp.r, tmpi.r)
    P.tt("dve", t, t, ti.bitcast(F32), ALU.add, tmp.r + tmpi.r, tmp.r)
    P.ts("dve", t, t, math.pi, -math.pi, ALU.min, ALU.max, tmp.r, tmp.r)
    P.act(out_ap, t, AF.Sin, tmp.r, out_res)


def emit_mla_layer(C, l):
    P = C.P
    A = C.A
    SC = 96.0 ** -0.5
    A.reset()
    R64 = slice(64, 96)
    cqn = A.alloc("cqn", [3, 2048], BF16, nres=12)
    ckvn = A.alloc("ckvn", [2, 2048], BF16, nres=8)
    Ct = A.alloc("Ct", [2048], F32)
    St = A.alloc("St", [2048], F32)
    kr = A.alloc("kr", [2048], BF16, nres=4)
    mark0 = A.off
    win = A.alloc("win", [8, 768], BF16)
    cq_t = [A.alloc("cqt%d" % i, [512], F32) for i in range(3)]
    ang = A.alloc("ang", [2048], F32)
    tmp = A.alloc("rtmp", [2048], F32)
    tmpi = A.alloc("rtmpi", [2048], I32)
    t1 = A.alloc("t1", [512], F32)
    t2 = A.alloc("t2", [512], F32)
    qng = A.alloc("qng", [8], F32)
    P.dma("pool", win[:, :, :], C.d_mlawin.rearrange("p (a b) -> p a b", b=768), C.usem[0], writes=win.r)
    P.dma("sp", qng[:, :], C.d_mlang, C.usem[1], writes=qng.r)
    P.dma("sp", tmpi[0:96, :], C.d_pos.broadcast_to([96, 2048]), C.usem[1], writes=tmpi.r)
    emit_norm(C, 4 * l + 0)
    P.copy("dve", ang[R64, :], tmpi[R64, :], tmpi.r, ang.r)
    P.ts("dve", ang[R64, :], ang[R64, :], C.cf[R64, 0:1], None, ALU.mult, None, ang.r + C.cf.r, ang.r)
    emit_sin_table(C, Ct[R64, :], Ct.r, ang, tmp, tmpi, R64, 0.25)
    emit_sin_table(C, St[R64, :], St.r, ang, tmp, tmpi, R64, 0.0)
    P.ts("dve", St[R64, :], St[R64, :], C.cf[R64, 1:2], None, ALU.mult, None, St.r + C.cf.r, St.r)
    for tg in range(4):
        cols = slice(512 * tg, 512 * tg + 512)
        xr = [C.xnT.r[4 * k + tg] for k in range(8)]
        for (nch, c0, dst, gcol, ssb) in ((3, 0, cqn, 0, 6), (2, 384, ckvn, 3, 7)):
            ss = C.PS[ssb]
            for ch_ in range(nch):
                ps = C.PS[ch_ % 2]
                pairs = [(win[:, k, c0 + 128 * ch_:c0 + 128 * ch_ + 128], C.xnT[:, k, cols]) for k in range(8)]
                P.mm_group(ps[:, :], pairs, win.r + xr, ps.r)
                P.copy("dve", cq_t[ch_][:, :], ps[:, :], ps.r, cq_t[ch_].r)
                sq = C.sq[ch_ % 2]
                P.act(sq[:, :], ps[:, :], AF.Square, ps.r, sq.r)
                P.mm(ss[:, :], cbs(C, CB_ONES), sq[:, :], start=(ch_ == 0), stop=(ch_ == nch - 1),
                     reads=sq.r + C.cb.r, writes=ss.r)
            P.act(C.rsb[:, :], ss[:, :], AF.Sqrt, ss.r, C.rsb.r, bias=C.epsb[:, 0:1], scale=1.0 / (128 * nch))
            rstd = C.rstd[0]
            P.recip(rstd[:, :], C.rsb[:, :], C.rsb.r, rstd.r)
            for ch_ in range(nch):
                P.stt(dst[:, ch_, cols], cq_t[ch_][:, :], qng[:, gcol + ch_:gcol + ch_ + 1], rstd[:, :],
                      ALU.mult, ALU.mult, cq_t[ch_].r + rstd.r + qng.r, [dst.r[4 * ch_ + tg]])
        psA = C.PS[2]
        psB = C.PS[3]
        P.mm_group(psA[0:96, :], [(win[:, k, 576:672], C.xnT[:, k, cols]) for k in range(8)], win.r + xr, psA.r)
        P.mm_group(psB[0:96, :], [(win[:, k, 672:768], C.xnT[:, k, cols]) for k in range(8)], win.r + xr, psB.r)
        P.tt("dve", t1[R64, :], psA[R64, :], Ct[R64, cols], ALU.mult, psA.r + Ct.r, t1.r)
        P.tt("dve", t2[R64, :], psB[R64, :], St[R64, cols], ALU.mult, psB.r + St.r, t2.r)
        P.tt("pool", kr[R64, cols], t1[R64, :], t2[R64, :], ALU.add, t1.r + t2.r, [kr.r[tg]])
    P.barrier()
    A.reset(mark0)
    oT = C.xnT
    wh = [A.alloc("wh%d" % i, [3 * 192 + 2 * 128], BF16) for i in range(2)]
    qf = [A.alloc("qf%d" % i, [2048], BF16, nres=4) for i in range(2)]
    kf = [A.alloc("kf%d" % i, [2048], BF16, nres=4) for i in range(2)]
    Vh = [A.alloc("Vh%d" % i, [16, 65], BF16, nres=4) for i in range(2)]
    o_pair = [A.alloc("op%d" % i, [16, 128], BF16, nres=4) for i in range(2)]
    t1 = A.alloc("t1b", [512], F32)
    t2 = A.alloc("t2b", [512], F32)
    chains = []
    for c in range(2):
        ch = Chain()
        ch.PT = [A.alloc("PT%d%d" % (c, i), [512], BF16) for i in range(2)]
        ch.rden = A.alloc("rden%d" % c, [4], F32)
        ch.Zb = [C.PS[c], C.PS[2 + c]]
        ch.Ob = C.PS[4 + c]
        steps = []
        for QG in ([0, 3] if c == 0 else [1, 2]):
            nk = 4 * QG + 4
            for jb in range(nk):
                r = jb - 4 * QG
                ta = 4 * QG + max(0, r)
                steps.append(dict(QG=QG, jb=jb, ta=ta, tb=4 * QG + 3, first=(jb == 0), last=(jb == nk - 1),
                                  g=0, mask=None, negmask=(cbs(C, CB_NEG_INCL) if r >= 0 else None)))
        ch.steps = steps
        chains.append(ch)
    for s_ in range(2):
        P.memset("pool", Vh[s_][:, :, 64:65], 1.0, Vh[s_].r)

    def head_units(h):
        s_ = h % 2
        w = wh[s_]
        wq_ = w[:, 0:576].rearrange("p (a b) -> p a b", b=192)
        wkv = w[:, 576:832].rearrange("p (a b) -> p a b", b=128)
        units = []
        units.append(lambda: P.dma("pool", w[:, :], C.d_mlawh[h], C.usem[s_], writes=w.r))
        ucnt = [0]

        def q_unit(tg):
            def f():
                cols = slice(512 * tg, 512 * tg + 512)
                psA = C.PS[6]
                psB = C.PS[7]
                cr = [cqn.r[4 * kc + tg] for kc in range(3)]
                P.mm_group(psA[0:96, :], [(wq_[:, kc, 0:96], cqn[:, kc, cols]) for kc in range(3)], w.r + cr, psA.r)
                P.mm_group(psB[0:96, :], [(wq_[:, kc, 96:192], cqn[:, kc, cols]) for kc in range(3)], w.r + cr, psB.r)
                P.ts("dve", qf[s_][0:64, cols], psA[0:64, :], SC, None, ALU.mult, None, psA.r, [qf[s_].r[tg]])
                P.stt(t1[R64, :], psA[R64, :], SC, Ct[R64, cols], ALU.mult, ALU.mult, psA.r + Ct.r, t1.r)
                P.stt(t2[R64, :], psB[R64, :], SC, St[R64, cols], ALU.mult, ALU.mult, psB.r + St.r, t2.r)
                P.tt("pool", qf[s_][R64, cols], t1[R64, :], t2[R64, :], ALU.add, t1.r + t2.r, [qf[s_].r[tg]])
            return f

        def k_unit(tg):
            def f():
                cols = slice(512 * tg, 512 * tg + 512)
                ps = C.PS[6 + tg % 2]
                cr = [ckvn.r[4 * kc + tg] for kc in range(2)]
                P.mm_group(ps[0:64, :], [(wkv[:, kc, 0:64], ckvn[:, kc, cols]) for kc in range(2)], w.r + cr, ps.r)
                P.copy("dve", kf[s_][0:64, cols], ps[0:64, :], ps.r, [kf[s_].r[tg]])
                P.copy("pool", kf[s_][R64, cols], kr[R64, cols], [kr.r[tg]], [kf[s_].r[tg]])
            return f

        def v_unit(vb):
            def f():
                ps = C.PS[6 + vb % 2]
                for jj in range(8):
                    tile = 8 * vb + jj
                    tcols = slice(128 * tile, 128 * tile + 128)
                    pairs = [(ckvn[:, kc, tcols], wkv[:, kc, 64:128]) for kc in range(2)]
                    P.mm_group(ps[:, jj * 64:(jj + 1) * 64], pairs,
                               w.r + [ckvn.r[4 * kc + tile // 4] for kc in range(2)], ps.r)
                P.copy("dve", Vh[s_][:, 8 * vb:8 * vb + 8, 0:64],
                       ps[:, :].rearrange("p (a b) -> p a b", b=64), ps.r, [Vh[s_].r[2 * vb], Vh[s_].r[2 * vb + 1]])
            return f

        for tg in range(4):
            units.append(k_unit(tg))
            units.append(q_unit(tg))
        units.append(v_unit(0))
        units.append(v_unit(1))
        return units

    for u in head_units(0):
        u()
    for h in range(16):
        s_ = h % 2
        ps_ = (h // 2) % 2
        for ch in chains:
            ch.kT = (lambda s_: (lambda g, jb: (kf[s_][0:96, 128 * jb:128 * jb + 128], [kf[s_].r[jb // 4]])))(s_)
            ch.qT = (lambda s_: (lambda g, t0, n: (qf[s_][0:96, t0:t0 + n], [qf[s_].r[t0 // 512]])))(s_)
            ch.V = (lambda s_: (lambda g, jb: (Vh[s_][:, jb, :], [Vh[s_].r[jb // 4]])))(s_)
            ch.o_dst4 = (lambda hh, ps_: (lambda QG: (o_pair[ps_][:, 4 * QG:4 * QG + 4, 64 * hh:64 * hh + 64],
                                                      [o_pair[ps_].r[QG]])))(h % 2, ps_)
        bg = deque(head_units(h + 1)) if h < 15 else deque()
        emit_softmax_attention(C, chains, bg, bg_every=1)
        if h % 2 == 1:
            emit_transposes(C, o_pair[ps_], oT, h // 2)
    P.barrier()
    A.reset(mark0)
    emit_attn_out_proj(C, A, lambda dc: C.d_mlawo[dc], 8, oT, 4 * l + 1)
    P.barrier()


DIL_W = (256, 640, 2048)
DIL_OFF = (0, 256, 896)
DIL_BACK = (1, 4, 15)


def emit_dil_layer(C, l):
    P = C.P
    A = C.A
    A.reset()
    oT = A.alloc("oT", [4, 2048], BF16, nres=16)
    mark = A.off
    valid = A.alloc("valid", [2944], BF16)
    wq = [A.alloc("wq%d" % i, [8, 384], BF16) for i in range(2)]
    qT = [A.alloc("qT%d" % g, [2048], BF16, nres=4) for g in range(3)]
    kT = [A.alloc("kT%d" % g, [2048], BF16, nres=4) for g in range(3)]
    Vv = [A.alloc("V%d" % g, [16, 130], BF16, nres=4) for g in range(3)]
    M = [A.alloc("M%d" % hh, [2944], BF16, nres=3) for hh in range(2)]
    stage = [A.alloc("stage0", [1024], F32)] * 2
    scnt = [0]
    o_pair = A.alloc("op", [16, 128], BF16, nres=4)
    for g in range(3):
        for hh in range(2):
            P.memset("pool", Vv[g][:, :, 65 * hh + 64:65 * hh + 65], 1.0, Vv[g].r)
    chains = []
    for c in range(2):
        ch = Chain()
        ch.PT = [A.alloc("PT%d%d" % (c, i), [512], BF16) for i in range(3)]
        ch.rden = A.alloc("rden%d" % c, [4], F32)
        ch.Zb = [C.PS[c], C.PS[2 + c], C.PS[6 + c]]
        ch.Ob = C.PS[4 + c]
        steps = []
        for QG in range(4):
            lst = []
            for g in range(3):
                for jb in range(max(0, 4 * QG - DIL_BACK[g]), 4 * QG + 4):
                    ta = max(jb, 4 * QG)
                    tb = min(jb + DIL_BACK[g], 4 * QG + 3)
                    if tb < ta:
                        continue
                    x0 = DIL_OFF[g] + (ta - jb) * 128
                    N = (tb - ta + 1) * 128
                    lst.append(dict(QG=QG, jb=jb, ta=ta, tb=tb, first=False, last=False, g=g,
                                    mask=(M[c][:, x0:x0 + N], [M[c].r[g]], N)))
            lst[0]["first"] = True
            lst[-1]["last"] = True
            steps += lst
        ch.steps = steps
        chains.append(ch)
    wissued = [0]

    def issue_w(idx):
        if idx >= 12 or idx < wissued[0]:
            return
        wissued[0] = idx + 1
        w_ = wq[idx % 2]
        P.dma("pool", w_[:, :, :], C.d_dilqkv[idx].rearrange("p (a b) -> p a b", b=384),
              C.usem[idx % 2], writes=w_.r)

    for p in range(4):
        issue_w(3 * p)
        issue_w(3 * p + 1)
        if p == 0:
            P.dma("pool", valid[:, :], C.d_dilvalid, C.wsem[1], writes=valid.r)
            emit_norm(C, 4 * l + 0)
        for hh in range(2):
            for g in range(3):
                for c0 in range(0, DIL_W[g], 1024):
                    W = min(1024, DIL_W[g] - c0)
                    o = DIL_OFF[g] + c0
                    stg = stage[scnt[0] % 2]
                    scnt[0] += 1
                    P.dma("sp", stg[:, 0:W], C.d_dilslab[g * 8 + 2 * p + hh][:, c0:c0 + W], C.wsem[scnt[0] % 2],
                          writes=stg.r)
                    P.act(stg[:, 0:W], stg[:, 0:W], AF.Exp, stg.r, stg.r)
                    P.tt("pool", M[hh][:, o:o + W], stg[:, 0:W], valid[:, o:o + W], ALU.mult,
                         stg.r + valid.r, [M[hh].r[g]])
        for g in range(3):
            issue_w(3 * p + g)
            w = wq[(3 * p + g) % 2]
            u = 0
            for tg in range(4):
                cols = slice(512 * tg, 512 * tg + 512)
                xr = [C.xnT.r[4 * k + tg] for k in range(8)]
                for which in range(2):
                    ps = C.PS[6 + u % 2]
                    u += 1
                    pairs = [(w[:, k, 128 * which:128 * which + 128], C.xnT[:, k, cols]) for k in range(8)]
                    P.mm_group(ps[:, :], pairs, w.r + xr, ps.r)
                    if which == 0:
                        P.ts("dve", qT[g][:, cols], ps[:, :], 0.125, None, ALU.mult, None, ps.r, [qT[g].r[tg]])
                    else:
                        P.copy("dve", kT[g][:, cols], ps[:, :], ps.r, [kT[g].r[tg]])
                ps = C.PS[6 + u % 2]
                u += 1
                for jj in range(4):
                    tile = 4 * tg + jj
                    tcols = slice(128 * tile, 128 * tile + 128)
                    pairs = [(C.xnT[:, k, tcols], w[:, k, 256:384]) for k in range(8)]
                    P.mm_group(ps[:, jj * 128:(jj + 1) * 128], pairs, w.r + xr, ps.r)
                for hh in range(2):
                    src = ps[:, :].rearrange("p (a b) -> p a b", b=128)[:, :, 64 * hh:64 * hh + 64]
                    P.copy("dve", Vv[g][:, 4 * tg:4 * tg + 4, 65 * hh:65 * hh + 64], src,
                           ps.r, [Vv[g].r[tg]])
            issue_w(3 * p + g + 2)
            if g == 0 and p > 0:
                emit_transposes(C, o_pair, oT, p - 1)
        for c, ch in enumerate(chains):
            rows = slice(64 * c, 64 * c + 64)
            ch.kT = (lambda rows: (lambda g, jb: (kT[g][rows, 128 * jb:128 * jb + 128], [kT[g].r[jb // 4]])))(rows)
            ch.qT = (lambda rows: (lambda g, t0, n: (qT[g][rows, t0:t0 + n],
                                                     [qT[g].r[i] for i in range(t0 // 512, (t0 + n - 1) // 512 + 1)])))(rows)
            ch.V = (lambda c: (lambda g, jb: (Vv[g][:, jb, 65 * c:65 * c + 65], [Vv[g].r[jb // 4]])))(c)
            ch.o_dst4 = (lambda c: (lambda QG: (o_pair[:, 4 * QG:4 * QG + 4, 64 * c:64 * c + 64], [o_pair.r[QG]])))(c)
        emit_softmax_attention(C, chains, deque(), look=2)
    emit_transposes(C, o_pair, oT, 3)
    P.barrier()
    A.reset(mark)
    emit_attn_out_proj(C, A, lambda dc: C.d_dilwo[dc], 4, oT, 4 * l + 1)
    P.barrier()


def build_nc(stages):
    nc = bass.Bass("TRN2", target_bir_lowering=False)
    C = Ctx()
    C.nc = nc

    def din(name, shape, dt=F32):
        return nc.dram_tensor(name, list(shape), dt, kind="ExternalInput").ap()

    d_xT = din("xT", [D, S])
    d_cst = din("cst", [128, 1024])
    d_gn = din("gains", [128, 128])
    d_cvp = din("convp", [128, 704])
    C.d_wup = din("wup", [4, NPAIR, 128, 2048])
    C.d_wdn = din("wdn", [4, 8, 128, DFF])
    C.d_sbqkv = din("sbqkv", [2, 8, 128, 3072])
    C.d_sbwo = din("sbwo", [2, 8, 128, 1024])
    C.d_pos = din("pos", [1, S], I32)
    d_cf = din("cf", [128, 8])
    C.d_mlawin = din("mlawin", [128, 8 * 768])
    C.d_mlang = din("mlang", [128, 8])
    C.d_mlawh = din("mlawh", [16, 128, 832])
    C.d_mlawo = din("mlawo", [8, 128, 1024])
    C.d_dilqkv = din("dilqkv", [12, 128, 3072])
    C.d_dilwo = din("dilwo", [8, 128, 512])
    C.d_dilslab = din("dilslab", [24, 128, 2048])
    C.d_dilvalid = din("dilvalid", [128, 2944])
    d_out = nc.dram_tensor("outT", [D, S], F32, kind="ExternalOutput").ap()

    with ExitStack() as es:
        P = Prog(nc, es)
        C.P = P
        C.xT = P.sbuf("xT_sb", [128, 8, S], F32, nres=32)
        C.xnT = P.sbuf("xnT_sb", [128, 8, S], BF16, nres=32)
        C.cb = P.sbuf("cb", [128, 1024], BF16)
        C.gn = P.sbuf("gn", [128, 128], F32)
        C.cvp = P.sbuf("cvp", [128, 704], F32)
        C.sq = [P.sbuf("sq%d" % i, [128, 512], BF16) for i in range(2)]
        C.rsb = P.sbuf("rsb", [128, 512], F32)
        C.rsb2 = [C.rsb, C.rsb]
        C.rstd = [P.sbuf("rstd%d" % i, [128, 512], F32) for i in range(2)]
        C.cf = P.sbuf("cf_sb", [128, 8], F32)
        C.epsb = P.sbuf("epsb", [128, 1], F32)
        C.oneb = P.sbuf("oneb", [128, 1], F32)
        arena = P.sbuf("arena", [128, ARENA_F32], F32)
        C.A = Arena(arena, ARENA_F32)
        C.PS = [P.psum("ps%d" % i, [128, 512], F32) for i in range(8)]
        C.wsem = [P.dsem("wsem%d" % i) for i in range(2)]
        C.usem = [P.dsem("usem%d" % i) for i in range(2)]
        s_in = P.dsem("s_in")
        s_x = [P.dsem("s_x%d" % i) for i in range(2)]
        s_out = P.dsem("s_out")

        P.dma("pool", C.cb[:, :], d_cst, s_in, writes=C.cb.r)
        P.dma("sp", C.gn[:, :], d_gn, s_in, writes=C.gn.r)
        P.dma("sp", C.cvp[:, :], d_cvp, s_in, writes=C.cvp.r)
        P.dma("sp", C.cf[:, :], d_cf, s_in, writes=C.cf.r)
        P.memset("pool", C.epsb[:, :], EPS, C.epsb.r)
        P.memset("pool", C.oneb[:, :], 1.0, C.oneb.r)
        for k in range(8):
            P.dma("sp", C.xT[:, k, :], d_xT[128 * k:128 * k + 128, :], s_x[k % 2],
                  writes=[C.xT.r[4 * k + tg] for tg in range(4)])

        for st in stages:
            kind, l = st
            if kind == "norm":
                emit_norm(C, l)
            elif kind == "ffn":
                emit_ffn(C, l)
            elif kind == "mix":
                if l % 3 == 0:
                    emit_sb_layer(C, l, l // 3)
                elif l % 3 == 1:
                    emit_dil_layer(C, l)
                else:
                    emit_mla_layer(C, l)
        P.barrier()
        fin = Res("fin")
        for k in range(8):
            P.dma("sp", d_out[128 * k:128 * k + 128, :], C.xT[:, k, :], s_out,
                  reads=[C.xT.r[4 * k + tg] for tg in range(4)], writes=[fin])
        P.op("sp", None, reads=[fin])
        P.emit()
        C.stats = dict(ecnt=dict(P.ecnt), n_wait=P.n_wait)
    return nc, C


def lhsT_stream_layout(W):
    K = W.shape[0]
    nk = K // 128
    return np.ascontiguousarray(W.reshape(nk, 128, 8, 128).transpose(2, 1, 0, 3).reshape(8, 128, K))


def prep_shared(inp):
    f = lambda a: np.asarray(a, dtype=np.float32)
    out = {}
    i = np.arange(128)
    cst = np.zeros((128, 1024), np.float32)
    cst[:, 0:128] = np.eye(128)
    cst[:, 128:256] = 1.0
    cst[:, 256:384] = -1.0 * (i[:, None] >= i[None, :])
    cst[:, 384:512] = -1.0 * (i[:, None] < i[None, :])
    cst[:, 512:640] = (i[:, None] < i[None, :])
    cst[:, 640:768] = (i[:, None] <= i[None, :])
    cst[:, 768:896] = NEG_BIG * (i[:, None] >= i[None, :])
    cst[:, 896:1024] = NEG_BIG * (i[:, None] > i[None, :])
    out["cst"] = cst
    ng = f(inp["norm_gains"])
    out["gains"] = np.ascontiguousarray(ng.reshape(16, 8, 128).transpose(2, 0, 1).reshape(128, 128))
    cw = f(inp["ffn_conv_w"])
    cbias = f(inp["ffn_conv_b"])
    cv = np.concatenate([cw, cbias[:, None, :]], axis=1)
    cv = cv.reshape(4, 4, 44, 128).transpose(3, 0, 2, 1)
    out["convp"] = np.ascontiguousarray(cv.reshape(128, 704))
    wu = f(inp["ffn_w_up"])
    g = wu[:, :, :DFF].reshape(4, 8, 128, NPAIR, 128)
    u = wu[:, :, DFF:].reshape(4, 8, 128, NPAIR, 128)
    gu = np.stack([g, u], axis=4)
    out["wup"] = np.ascontiguousarray(gu.transpose(0, 3, 2, 1, 4, 5).reshape(4, NPAIR, 128, 2048))
    wd = f(inp["ffn_w_down"])
    out["wdn"] = np.stack([lhsT_stream_layout(wd[l]) for l in range(4)])
    sq = f(inp["sb_w_qkv"])
    t = sq.reshape(2, 8, 128, 3, 8, 128)
    out["sbqkv"] = np.ascontiguousarray(t.transpose(0, 4, 2, 1, 3, 5).reshape(2, 8, 128, 3072))
    so = f(inp["sb_w_o"])
    out["sbwo"] = np.stack([lhsT_stream_layout(so[jj]) for jj in range(2)])
    cfm = np.zeros((128, 8), np.float32)
    freqs = (np.float32(10000.0) ** (-np.arange(16, dtype=np.float32) / np.float32(16))).astype(np.float32)
    cfm[64:80, 0] = freqs
    cfm[80:96, 0] = freqs
    cfm[64:80, 1] = -1.0
    cfm[80:96, 1] = 1.0
    out["cf"] = cfm
    wi = f(inp["mla_w_in"])[0]
    wi2 = np.concatenate([wi, wi[:, 576:640], wi[:, 656:672], wi[:, 640:656]], axis=1)
    out["mlawin"] = np.ascontiguousarray(wi2.reshape(8, 128, 768).transpose(1, 0, 2).reshape(128, 8 * 768))
    ng_ = np.zeros((128, 8), np.float32)
    ng_[:, 0:3] = f(inp["mla_q_norm"])[0].reshape(3, 128).T
    ng_[:, 3:5] = f(inp["mla_kv_norm"])[0].reshape(2, 128).T
    out["mlang"] = ng_
    wqb = f(inp["mla_w_qb"])[0]
    wkvb = f(inp["mla_w_kvb"])[0]
    whs = []
    for h in range(16):
        a = wqb[:, 96 * h:96 * h + 96]
        b = np.concatenate([wqb[:, 96 * h:96 * h + 64], wqb[:, 96 * h + 80:96 * h + 96],
                            wqb[:, 96 * h + 64:96 * h + 80]], axis=1)
        q2 = np.concatenate([a, b], axis=1).reshape(3, 128, 192).transpose(1, 0, 2).reshape(128, 576)
        kv = wkvb[:, 128 * h:128 * h + 128].reshape(2, 128, 128).transpose(1, 0, 2).reshape(128, 256)
        whs.append(np.concatenate([q2, kv], axis=1))
    out["mlawh"] = np.ascontiguousarray(np.stack(whs))
    out["mlawo"] = lhsT_stream_layout(f(inp["mla_w_o"])[0])
    dq = f(inp["dil_w_qkv"])[0]
    t = dq.reshape(8, 128, 3, 3, 4, 128)
    out["dilqkv"] = np.ascontiguousarray(t.transpose(4, 3, 1, 0, 2, 5).reshape(12, 128, 3072))
    out["dilwo"] = lhsT_stream_layout(f(inp["dil_w_o"])[0])
    rb = f(inp["rel_bias"])
    jk = np.arange(128)[:, None]
    xx = np.arange(2048)[None, :]
    delta = xx - jk
    dpos = np.maximum(delta, 0)
    dflt = np.maximum(dpos.astype(np.float32), np.float32(1.0))
    large = 16 + (np.log(dflt / np.float32(16)) / np.float32(math.log(2048 / 16)) * np.float32(16)).astype(np.int32)
    large = np.minimum(large, 31)
    bucket = np.where(dpos < 16, dpos, large)
    slab = np.zeros((24, 128, 2048), np.float32)
    for gh in range(24):
        slab[gh] = rb[bucket, gh]
    out["dilslab"] = slab
    valid = np.zeros((128, 2944), np.float32)
    for g, (W, r) in enumerate(((256, 1), (640, 4), (2048, 16))):
        dl = delta[:, :W]
        valid[:, DIL_OFF[g]:DIL_OFF[g] + W] = (dl >= 0) & (dl % r == 0) & (dl <= 128 * r)
    out["dilvalid"] = valid
    return out


ALL_STAGES = [("mix", 0), ("ffn", 0), ("mix", 1), ("ffn", 1), ("mix", 2), ("ffn", 2), ("mix", 3), ("ffn", 3)]
_CACHE = {}


def run(inputs, stages, n_cores=8, trace=False):
    key = tuple(stages)
    if key not in _CACHE:
        _CACHE[key] = build_nc(stages)
    nc, C = _CACHE[key]
    shared = prep_shared(inputs)
    x = np.asarray(inputs["x"], dtype=np.float32)
    in_maps = []
    for b in range(n_cores):
        m = dict(shared)
        m["xT"] = np.ascontiguousarray(x[b].T)
        m["pos"] = np.ascontiguousarray(np.asarray(inputs["positions"])[b][None, :].astype(np.int32))
        in_maps.append(m)
    res = run_bass_kernel_spmd(nc, in_maps, core_ids=list(range(n_cores)), trace=trace)
    out = np.stack([np.ascontiguousarray(r["outT"].T) for r in res.results])
    return out, res


def kernel(**inputs):
    out, _ = run(inputs, ALL_STAGES, 8)
    return out.astype(np.float32)
```
